# Optimizing a Trainium2 kernel written in Bass

```python
import math
import jax, jax.numpy as jnp
from jax import lax
import numpy as np

D_MODEL = 1024
BATCH = 2
SEQ = 8192
DEPTH = 2

CONV_DIM = 512
CONV_WIDTH = 3
FOX_HEADS = 8
FOX_HEAD_DIM = 64
FOX_DIM = FOX_HEADS * FOX_HEAD_DIM
Q_BLOCK = 128
EVEN_IN = 3 * CONV_DIM + 3 * FOX_DIM + FOX_HEADS
EVEN_MIX = CONV_DIM + FOX_DIM
SSM_INNER = 2 * D_MODEL
SSM_HEAD_DIM = 64
SSM_HEADS = SSM_INNER // SSM_HEAD_DIM
SSM_GROUPS = 4
SSM_STATE = 128
SSM_CONV_WIDTH = 4
SSM_CHUNK = 128
SSM_CONV_CH = SSM_INNER + 2 * SSM_GROUPS * SSM_STATE
ODD_IN = SSM_INNER + SSM_CONV_CH + SSM_HEADS
D_FF = 2816
FFN_CONV_WIDTH = 3
PLE_DIM = 256
N_EVEN = (DEPTH + 1) // 2
N_ODD = DEPTH // 2
LN_EPS = 1e-5
RMS_EPS = 1e-5

kernel_name = "hybrid_shortconv_fox_mamba2_deepnorm_convffn_ple"


def layer_norm(x, g, b):
    xf = x.astype(jnp.float32)
    mu = jnp.mean(xf, axis=-1, keepdims=True)
    var = jnp.mean(jnp.square(xf - mu), axis=-1, keepdims=True)
    return ((xf - mu) * lax.rsqrt(var + LN_EPS) * g + b).astype(x.dtype)


def causal_dwconv(x, w, b=None):
    K, C = w.shape
    out = lax.conv_general_dilated(
        x, w[:, None, :].astype(x.dtype), window_strides=(1,), padding=[(K - 1, 0)],
        dimension_numbers=('NWC', 'WIO', 'NWC'), feature_group_count=C)
    if b is not None:
        out = out + b
    return out


def forgetting_attention(q, k, v, f_logit):
    Bsz, L, _ = q.shape
    H, Dh = FOX_HEADS, FOX_HEAD_DIM
    def heads(t):
        return t.reshape(Bsz, L, H, Dh).transpose(0, 2, 1, 3)
    q, k, v = heads(q), heads(k), heads(v)
    log_f = jax.nn.log_sigmoid(f_logit.astype(jnp.float32))
    F = jnp.cumsum(log_f, axis=1).transpose(0, 2, 1)
    n_blk = L // Q_BLOCK
    qb = q.reshape(Bsz, H, n_blk, Q_BLOCK, Dh).transpose(2, 0, 1, 3, 4)
    Fb = F.reshape(Bsz, H, n_blk, Q_BLOCK).transpose(2, 0, 1, 3)
    k_pos = jnp.arange(L)
    scale = FOX_HEAD_DIM ** -0.5

    def one_block(args):
        i, q_i, F_i = args
        s = jnp.einsum('bhqd,bhkd->bhqk', q_i, k, preferred_element_type=jnp.float32) * scale
        s = s + F_i[..., None] - F[:, :, None, :]
        q_pos = i * Q_BLOCK + jnp.arange(Q_BLOCK)
        s = jnp.where(k_pos[None, :] <= q_pos[:, None], s, -jnp.inf)
        pr = jax.nn.softmax(s, axis=-1)
        return jnp.einsum('bhqk,bhkd->bhqd', pr.astype(v.dtype), v)

    out = lax.map(one_block, (jnp.arange(n_blk), qb, Fb))
    return out.transpose(1, 0, 3, 2, 4).reshape(Bsz, L, H * Dh)


def shortconv_fox_mixer(x, w_in, b_f, w_conv, w_out):
    sizes = [CONV_DIM] * 3 + [FOX_DIM] * 3 + [FOX_HEADS]
    splits = [int(s) for s in np.cumsum(sizes)[:-1]]
    gB, gC, h, q, k, v, f_logit = jnp.split(x @ w_in, splits, axis=-1)
    y_a = gB * causal_dwconv(gC * h, w_conv)
    y_b = forgetting_attention(q, k, v, f_logit + b_f)
    return jnp.concatenate([y_a, y_b], axis=-1) @ w_out


def ssd_chunked(x, dt, A, Bm, Cm):
    Bsz, L, H, P = x.shape
    G, N, Q = SSM_GROUPS, SSM_STATE, SSM_CHUNK
    R = H // G
    nc = L // Q
    xc = (x * dt[..., None]).reshape(Bsz, nc, Q, G, R, P)
    acs = jnp.cumsum((dt * A).reshape(Bsz, nc, Q, G, R), axis=2)
    Bc = Bm.reshape(Bsz, nc, Q, G, N)
    Cc = Cm.reshape(Bsz, nc, Q, G, N)
    seg = acs[:, :, :, None] - acs[:, :, None, :]
    causal = jnp.tril(jnp.ones((Q, Q), dtype=bool))[:, :, None, None]
    Lmat = jnp.exp(jnp.where(causal, seg, -jnp.inf))
    CB = jnp.einsum('bclgn,bcsgn->bclsg', Cc, Bc)
    y_diag = jnp.einsum('bclsg,bclsgr,bcsgrp->bclgrp', CB, Lmat, xc)
    decay_to_end = jnp.exp(acs[:, :, -1:] - acs)
    states = jnp.einsum('bcsgn,bcsgr,bcsgrp->bcgrpn', Bc, decay_to_end, xc)
    chunk_decay = jnp.exp(acs[:, :, -1])

    def step(hs, inp):
        s_c, d_c = inp
        return hs * d_c[..., None, None] + s_c, hs

    h0 = jnp.zeros((Bsz, G, R, P, N), jnp.float32)
    _, prev = lax.scan(step, h0, (states.astype(jnp.float32).transpose(1, 0, 2, 3, 4, 5),
                                  chunk_decay.transpose(1, 0, 2, 3)))
    prev = prev.transpose(1, 0, 2, 3, 4, 5)
    y_off = jnp.einsum('bclgn,bcgrpn,bclgr->bclgrp', Cc, prev, jnp.exp(acs))
    return (y_diag + y_off).reshape(Bsz, L, H, P)


def gated_group_rmsnorm(y, z, g):
    Bsz, L, Dn = y.shape
    u = (y * jax.nn.silu(z)).astype(jnp.float32).reshape(Bsz, L, SSM_GROUPS, Dn // SSM_GROUPS)
    u = u * lax.rsqrt(jnp.mean(jnp.square(u), axis=-1, keepdims=True) + RMS_EPS)
    return (u.reshape(Bsz, L, Dn) * g).astype(y.dtype)


def mamba2_mixer(x, w_in, conv_w, conv_b, dt_bias, a_log, d_skip, norm_g, w_out):
    Bsz, L, _ = x.shape
    z, xBC, dt = jnp.split(x @ w_in, [SSM_INNER, SSM_INNER + SSM_CONV_CH], axis=-1)
    xBC = jax.nn.silu(causal_dwconv(xBC, conv_w, conv_b))
    xs, Bm, Cm = jnp.split(xBC, [SSM_INNER, SSM_INNER + SSM_GROUPS * SSM_STATE], axis=-1)
    dt = jax.nn.softplus(dt.astype(jnp.float32) + dt_bias)
    A = -jnp.exp(a_log.astype(jnp.float32))
    xh = xs.reshape(Bsz, L, SSM_HEADS, SSM_HEAD_DIM)
    y = ssd_chunked(xh, dt, A,
                    Bm.reshape(Bsz, L, SSM_GROUPS, SSM_STATE),
                    Cm.reshape(Bsz, L, SSM_GROUPS, SSM_STATE))
    y = (y + d_skip[:, None] * xh).astype(x.dtype).reshape(Bsz, L, SSM_INNER)
    return gated_group_rmsnorm(y, z, norm_g) @ w_out


def conv_ffn(x, w_up, conv_w, conv_b, w_down):
    u = causal_dwconv(x @ w_up, conv_w, conv_b)
    g, v = jnp.split(u, 2, axis=-1)
    return (jax.nn.silu(g) * v) @ w_down


def per_layer_embed(x, p_i, w_proj, w_gate, b_gate):
    gate = jax.nn.sigmoid(x @ w_gate + b_gate)
    return x + gate * (p_i @ w_proj)


def setup_inputs(seed: int = 0) -> dict:
    key = jax.random.key(seed)
    ks = iter(jax.random.split(key, 32))
    beta = (8.0 * DEPTH) ** -0.25

    def nrm(shape, scale):
        return jax.random.normal(next(ks), shape, jnp.float32) * scale

    x = nrm((BATCH, SEQ, D_MODEL), 1.0)
    p = nrm((DEPTH, BATCH, SEQ, PLE_DIM), 1.0)
    even_w_in = nrm((N_EVEN, D_MODEL, EVEN_IN), D_MODEL ** -0.5)
    even_b_f = 2.0 + nrm((N_EVEN, FOX_HEADS), 0.5)
    even_conv_w = nrm((N_EVEN, CONV_WIDTH, CONV_DIM), CONV_WIDTH ** -0.5)
    even_w_out = nrm((N_EVEN, EVEN_MIX, D_MODEL), beta * EVEN_MIX ** -0.5)
    odd_w_in = nrm((N_ODD, D_MODEL, ODD_IN), D_MODEL ** -0.5)
    odd_conv_w = nrm((N_ODD, SSM_CONV_WIDTH, SSM_CONV_CH), SSM_CONV_WIDTH ** -0.5)
    odd_conv_b = nrm((N_ODD, SSM_CONV_CH), 0.02)
    u = jax.random.uniform(next(ks), (N_ODD, SSM_HEADS), jnp.float32)
    dt0 = jnp.exp(u * (math.log(0.1) - math.log(0.001)) + math.log(0.001))
    odd_dt_bias = dt0 + jnp.log(-jnp.expm1(-dt0))
    odd_a_log = jnp.log(jax.random.uniform(next(ks), (N_ODD, SSM_HEADS), jnp.float32, 1.0, 16.0))
    odd_d_skip = 1.0 + nrm((N_ODD, SSM_HEADS), 0.1)
    odd_norm_g = 1.0 + nrm((N_ODD, SSM_INNER), 0.02)
    odd_w_out = nrm((N_ODD, SSM_INNER, D_MODEL), beta * SSM_INNER ** -0.5)
    ln_mix_g = 1.0 + nrm((DEPTH, D_MODEL), 0.02)
    ln_mix_b = nrm((DEPTH, D_MODEL), 0.02)
    ffn_w_up = nrm((DEPTH, D_MODEL, 2 * D_FF), D_MODEL ** -0.5)
    ffn_conv_w = nrm((DEPTH, FFN_CONV_WIDTH, 2 * D_FF), FFN_CONV_WIDTH ** -0.5)
    ffn_conv_b = nrm((DEPTH, 2 * D_FF), 0.02)
    ffn_w_down = nrm((DEPTH, D_FF, D_MODEL), beta * D_FF ** -0.5)
    ln_ffn_g = 1.0 + nrm((DEPTH, D_MODEL), 0.02)
    ln_ffn_b = nrm((DEPTH, D_MODEL), 0.02)
    ple_w_proj = nrm((DEPTH, PLE_DIM, D_MODEL), PLE_DIM ** -0.5)
    ple_w_gate = nrm((DEPTH, D_MODEL, D_MODEL), D_MODEL ** -0.5)
    ple_b_gate = nrm((DEPTH, D_MODEL), 0.02)
    return {"x": x, "p": p,
            "even_w_in": even_w_in, "even_b_f": even_b_f, "even_conv_w": even_conv_w,
            "even_w_out": even_w_out,
            "odd_w_in": odd_w_in, "odd_conv_w": odd_conv_w, "odd_conv_b": odd_conv_b,
            "odd_dt_bias": odd_dt_bias, "odd_a_log": odd_a_log, "odd_d_skip": odd_d_skip,
            "odd_norm_g": odd_norm_g, "odd_w_out": odd_w_out,
            "ln_mix_g": ln_mix_g, "ln_mix_b": ln_mix_b,
            "ffn_w_up": ffn_w_up, "ffn_conv_w": ffn_conv_w, "ffn_conv_b": ffn_conv_b,
            "ffn_w_down": ffn_w_down, "ln_ffn_g": ln_ffn_g, "ln_ffn_b": ln_ffn_b,
            "ple_w_proj": ple_w_proj, "ple_w_gate": ple_w_gate, "ple_b_gate": ple_b_gate}


def reference(x, p, even_w_in, even_b_f, even_conv_w, even_w_out,
              odd_w_in, odd_conv_w, odd_conv_b, odd_dt_bias, odd_a_log, odd_d_skip,
              odd_norm_g, odd_w_out, ln_mix_g, ln_mix_b, ffn_w_up, ffn_conv_w, ffn_conv_b,
              ffn_w_down, ln_ffn_g, ln_ffn_b, ple_w_proj, ple_w_gate, ple_b_gate):
    alpha = (2.0 * DEPTH) ** 0.25
    h = x
    for i in range(DEPTH):
        j = i // 2
        if i % 2 == 0:
            mix = shortconv_fox_mixer(h, even_w_in[j], even_b_f[j], even_conv_w[j], even_w_out[j])
        else:
            mix = mamba2_mixer(h, odd_w_in[j], odd_conv_w[j], odd_conv_b[j], odd_dt_bias[j],
                               odd_a_log[j], odd_d_skip[j], odd_norm_g[j], odd_w_out[j])
        h = layer_norm(alpha * h + mix, ln_mix_g[i], ln_mix_b[i])
        ffn = conv_ffn(h, ffn_w_up[i], ffn_conv_w[i], ffn_conv_b[i], ffn_w_down[i])
        h = layer_norm(alpha * h + ffn, ln_ffn_g[i], ln_ffn_b[i])
        h = per_layer_embed(h, p[i], ple_w_proj[i], ple_w_gate[i], ple_b_gate[i])
    return h
```

```python
import numpy as np
import concourse.bass as bass
import concourse.mybir as mybir
from concourse.bass_utils import run_bass_kernel_spmd

F32 = mybir.dt.float32
BF16 = mybir.dt.bfloat16
ALU = mybir.AluOpType
AF = mybir.ActivationFunctionType
AX = mybir.AxisListType

NCORES = 8
D = 1024
SEQ = 8192
TOK = 2048
DFF = 2816
NJ = DFF // 128
ALPHA = 4.0 ** 0.25
LN_EPS = 1e-5

COMPUTE = ("pe", "act", "dve", "pool")
DMAQ = ("sp", "pq")
NDSEM = 12


class Op:
    __slots__ = ("eng", "fn", "reads", "writes", "deps", "inc", "cnt", "dsem", "duse", "barrier")

    def __init__(self, eng, fn, reads, writes):
        self.eng = eng; self.fn = fn; self.reads = reads; self.writes = writes
        self.deps = set(); self.inc = False; self.cnt = 0; self.dsem = None; self.duse = 0
        self.barrier = False


class Prog:
    def __init__(self, nc):
        self.nc = nc
        self.ops = []
        self._uid = 0

    def stream(self, e):
        return "pool" if e == "pq" else e

    def op(self, eng, fn, reads=(), writes=()):
        o = Op(eng, fn, tuple(reads), tuple(writes))
        self.ops.append(o)
        return o

    def dma(self, q, out, in_, reads=(), writes=(), **kw):
        return self.op(q, lambda e: e.dma_start(out=out, in_=in_, **kw), reads, writes)

    def barrier(self):
        for e in COMPUTE + ("sp",):
            o = self.op(e, None)
            o.barrier = True

    def sbuf(self, shape, dt, name=None):
        self._uid += 1
        return self.nc.alloc_sbuf_tensor(name or f"sb{self._uid}", list(shape), dt)

    def psum(self, shape, dt=F32, name=None):
        self._uid += 1
        return self.nc.alloc_psum_tensor(name or f"ps{self._uid}", list(shape), dt)

    def finalize(self):
        nc = self.nc
        ops = self.ops
        last_w = {}
        readers = {}
        last_on = {}
        all_dma = []
        for i, o in enumerate(ops):
            st = self.stream(o.eng)
            isdma = o.eng in DMAQ and not o.barrier
            if o.barrier:
                for s2, j in last_on.items():
                    if s2 != st:
                        o.deps.add(j)
                for j in all_dma:
                    o.deps.add(j)
            else:
                raw = set()
                for k in o.reads:
                    j = last_w.get(k)
                    if j is not None:
                        raw.add(j)
                war = set()
                for k in o.writes:
                    j = last_w.get(k)
                    if j is not None:
                        raw.add(j)
                    for r in readers.get(k, ()):
                        war.add(r)
                for j in raw:
                    p = ops[j]
                    pst = self.stream(p.eng)
                    pdma = p.eng in DMAQ
                    if pst == st and not pdma and not isdma and st == "pe":
                        continue
                    o.deps.add(j)
                for j in war:
                    p = ops[j]
                    pst = self.stream(p.eng)
                    pdma = p.eng in DMAQ
                    if pst == st and not pdma and not isdma:
                        continue
                    o.deps.add(j)
                o.deps.discard(i)
                for k in o.reads:
                    readers.setdefault(k, []).append(i)
                for k in o.writes:
                    last_w[k] = i
                    readers[k] = []
            if isdma:
                all_dma.append(i)
            else:
                last_on[st] = i
        for o in ops:
            for j in o.deps:
                ops[j].inc = True
        cnt = {s: 0 for s in COMPUTE}
        dcount = {q: 0 for q in DMAQ}
        for o in ops:
            if o.barrier:
                continue
            if o.eng in DMAQ:
                n = dcount[o.eng]; dcount[o.eng] += 1
                o.dsem = (o.eng, n % NDSEM); o.duse = n // NDSEM
            elif o.inc:
                cnt[o.eng] += 1
                o.cnt = cnt[o.eng]
        sems = {s: nc.alloc_semaphore(f"s_{s}") for s in COMPUTE}
        dsems = {(q, k): nc.alloc_semaphore(f"d_{q}{k}") for q in DMAQ for k in range(NDSEM)}
        streams = {s: [] for s in COMPUTE + ("sp",)}
        for i, o in enumerate(ops):
            streams[self.stream(o.eng)].append(i)
        self.stats = {s: len(v) for s, v in streams.items()}

        def emit(stream_name, eng):
            waited = {}
            for i in streams[stream_name]:
                o = ops[i]
                need = {}
                for j in o.deps:
                    p = ops[j]
                    if p.eng in DMAQ:
                        key = p.dsem; val = 16 * (p.duse + 1)
                    else:
                        key = p.eng; val = p.cnt
                    if need.get(key, 0) < val:
                        need[key] = val
                if o.eng in DMAQ and not o.barrier and o.duse > 0:
                    key = o.dsem; val = 16 * o.duse
                    if need.get(key, 0) < val:
                        need[key] = val
                for key, val in need.items():
                    if waited.get(key, 0) >= val:
                        continue
                    waited[key] = val
                    sem = dsems[key] if isinstance(key, tuple) else sems[key]
                    eng.wait_ge(sem, val)
                if o.barrier:
                    continue
                ins = o.fn(eng)
                if o.eng in DMAQ:
                    ins.then_inc(dsems[o.dsem], 16)
                elif o.inc:
                    ins.then_inc(sems[o.eng], 1)

        with nc.Block() as block:
            @block.tensor
            def _(e):
                emit("pe", e)

            @block.scalar
            def _(e):
                emit("act", e)

            @block.vector
            def _(e):
                emit("dve", e)

            @block.gpsimd
            def _(e):
                emit("pool", e)

            @block.sync
            def _(e):
                emit("sp", e)
                fin = {}
                for o in ops:
                    if o.eng in DMAQ and not o.barrier:
                        fin[o.dsem] = max(fin.get(o.dsem, 0), 16 * (o.duse + 1))
                for key, val in fin.items():
                    e.wait_ge(dsems[key], val)


def new_nc():
    return bass.Bass("TRN2", target_bir_lowering=False)


def din(nc, name, shape, dt=F32):
    return nc.dram_tensor(name, list(shape), dt, kind="ExternalInput").ap()


def dout(nc, name, shape, dt=F32):
    return nc.dram_tensor(name, list(shape), dt, kind="ExternalOutput").ap()


def make_ident(P, dt=F32):
    ident = P.sbuf([128, 128], dt, "ident_" + str(dt).split(".")[-1])
    P.op("pool", lambda e: e.memset(ident[:], 0.0), writes=["ident"])
    P.op("pool", lambda e: e.affine_select(out=ident[:], in_=ident[:], pattern=[[-1, 128]],
                                           compare_op=ALU.not_equal, fill=1.0, base=0,
                                           channel_multiplier=1), reads=["ident"], writes=["ident"])
    return ident


class Banks:
    def __init__(self, P):
        self.t = [P.psum([128, 512], F32, f"bank{i}") for i in range(8)]
        self.key = [f"bank{i}" for i in range(8)]


def layer_norm_block(P, tag, r, g_bc, b_bc, out, stat, eng2="pool"):
    rk = tag + "r"
    st6 = stat["st6"]; mv = stat["mv"]; rstd = stat["rstd"]
    for c in range(2):
        P.op("dve", lambda e, c=c: e.bn_stats(out=st6[:, c * 6:(c + 1) * 6], in_=r[:, c * 512:(c + 1) * 512]),
             reads=[rk], writes=[tag + "st6"])
    P.op("dve", lambda e: e.bn_aggr(out=mv[:, 0:2], in_=st6[:, 0:12]), reads=[tag + "st6"], writes=[tag + "mv"])
    P.op("dve", lambda e: e.tensor_scalar_add(out=rstd[:, 0:1], in0=mv[:, 1:2], scalar1=LN_EPS),
         reads=[tag + "mv"], writes=[tag + "rstd"])
    P.op("act", lambda e: e.activation(out=rstd[:, 0:1], in_=rstd[:, 0:1], func=AF.Sqrt),
         reads=[tag + "rstd"], writes=[tag + "rstd"])
    P.op("dve", lambda e: e.reciprocal(out=rstd[:, 0:1], in_=rstd[:, 0:1]), reads=[tag + "rstd"], writes=[tag + "rstd"])
    P.op("dve", lambda e: e.tensor_scalar(out=r[:, :], in0=r[:, :], scalar1=mv[:, 0:1], scalar2=rstd[:, 0:1],
                                          op0=ALU.subtract, op1=ALU.mult),
         reads=[rk, tag + "mv", tag + "rstd"], writes=[rk])
    P.op(eng2, lambda e: e.tensor_tensor(out=r[:, :], in0=r[:, :], in1=g_bc[:, :], op=ALU.mult),
         reads=[rk, "lnconst"], writes=[rk])
    P.op(eng2, lambda e: e.tensor_tensor(out=out[:, :], in0=r[:, :], in1=b_bc[:, :], op=ALU.add),
         reads=[rk, "lnconst"], writes=[tag + "lnout"])


def build_ffn(TT=512, dbg_nt=None, dbg_nj=None, dbg_block=True):
    nc = new_nc(); P = Prog(nc)
    NT = TOK // TT
    NB = TT // 128
    hT_d = din(nc, "hT", [128, 8, TOK + 2])
    htm_d = din(nc, "htm", [TOK, D])
    wup_d = din(nc, "wup", [NJ, 128, 8, 256])
    cw_d = din(nc, "cw", [128, NJ * 2 * 3])
    cb_d = din(nc, "cb", [128, NJ * 2])
    wdn_d = din(nc, "wdn", [128, NJ, D])
    lng_d = din(nc, "lng", [128, D]); lnb_d = din(nc, "lnb", [128, D]); bg_d = din(nc, "bg", [128, D])
    wg_d = din(nc, "wg", [128, 8, D]); wp_d = din(nc, "wp", [128, 2, D])
    pT_d = din(nc, "pT", [128, 2, TOK])
    ho_d = dout(nc, "ho", [TOK, D])

    B = Banks(P)
    ident = make_ident(P, F32)
    cw = P.sbuf([128, NJ * 6], F32); cb = P.sbuf([128, NJ * 2], F32)
    lng = P.sbuf([128, D], F32); lnb = P.sbuf([128, D], F32); bg = P.sbuf([128, D], F32)
    P.dma("sp", cw[:], cw_d[:, :], writes=["cw"]); P.dma("sp", cb[:], cb_d[:, :], writes=["cw"])
    P.dma("sp", lng[:], lng_d[:, :], writes=["lnconst"]); P.dma("sp", lnb[:], lnb_d[:, :], writes=["lnconst"])
    P.dma("sp", bg[:], bg_d[:, :], writes=["lnconst"])
    wdn = P.sbuf([128, NJ, D], BF16); wg = P.sbuf([128, 8, D], BF16); wp = P.sbuf([128, 2, D], BF16)
    hTs = [P.sbuf([128, 8, TT + 2], BF16) for _ in range(2)]
    wbuf = [P.sbuf([128, 8, 256], BF16) for _ in range(2)]
    ubuf = [[P.sbuf([128, TT + 2], F32) for _ in range(2)] for _ in range(2)]
    cbuf = [[P.sbuf([128, TT], F32) for _ in range(2)] for _ in range(2)]
    sg = [P.sbuf([128, TT], F32) for _ in range(2)]
    hdn = P.sbuf([128, NJ, TT], BF16)
    pTs = [P.sbuf([128, 2, 128], BF16) for _ in range(2)]
    hres = [P.sbuf([128, D], F32) for _ in range(2)]
    rb = [P.sbuf([128, D], F32) for _ in range(2)]
    yb = [P.sbuf([128, D], F32) for _ in range(2)]
    yT = [P.sbuf([128, 8, 128], BF16) for _ in range(2)]
    gsb = [P.sbuf([128, D], F32) for _ in range(2)]
    ob = [P.sbuf([128, D], F32) for _ in range(2)]
    stat = {"st6": P.sbuf([128, 12], F32), "mv": P.sbuf([128, 2], F32), "rstd": P.sbuf([128, 1], F32)}

    def load_resident():
        for j in range(NJ):
            P.dma("pq", wdn[:, j, :], wdn_d[:, j, :], writes=[f"wdn{j}"])
        for k in range(8):
            P.dma("pq", wg[:, k, :], wg_d[:, k, :], writes=["wg"])
        P.dma("pq", wp[:, :, :], wp_d[:, :, :], writes=["wp"])

    resident_loaded = False
    bankrr = 0
    for tt in range(dbg_nt or NT):
        hp = tt % 2
        P.dma("pq", hTs[hp][:, :, :], hT_d[:, :, tt * TT: tt * TT + TT + 2], writes=[f"hTs{hp}"])
        for j in range(dbg_nj or NJ):
            wp_ = j % 2
            P.dma("pq", wbuf[wp_][:, :, :], wup_d[j, :, :, :], writes=[f"wbuf{wp_}"])
            up = j % 2
            for gv in range(2):
                bh = bankrr % 4; bm = (bankrr + 1) % 4; bankrr += 2
                for (bk, c0, n) in ((bh, 0, 2), (bm, 2, TT)):
                    for k in range(8):
                        P.op("pe", lambda e, bk=bk, k=k, c0=c0, n=n, gv=gv, wp_=wp_, hp=hp: e.matmul(
                            B.t[bk][:, 0:n], lhsT=wbuf[wp_][:, k, gv * 128:(gv + 1) * 128],
                            rhs=hTs[hp][:, k, c0:c0 + n], start=(k == 0), stop=(k == 7)),
                            reads=[f"wbuf{wp_}", f"hTs{hp}"], writes=[B.key[bk]])
                    P.op("act", lambda e, bk=bk, c0=c0, n=n, gv=gv, up=up: e.copy(
                        out=ubuf[gv][up][:, c0:c0 + n], in_=B.t[bk][:, 0:n]),
                        reads=[B.key[bk]], writes=[f"ubuf{gv}{up}"])
            for gv in range(2):
                ci = (j * 2 + gv) * 3
                u = ubuf[gv][up]; c = cbuf[gv][up]
                P.op("act", lambda e, u=u, c=c, ci=ci, j=j, gv=gv: e.activation(
                    out=c[:, :], in_=u[:, 2:TT + 2], func=AF.Identity,
                    scale=cw[:, ci + 2:ci + 3], bias=cb[:, j * 2 + gv:j * 2 + gv + 1]),
                    reads=[f"ubuf{gv}{up}", "cw"], writes=[f"cbuf{gv}{up}"])
                for tap in (1, 0):
                    P.op("dve", lambda e, u=u, c=c, ci=ci, tap=tap: e.scalar_tensor_tensor(
                        out=c[:, :], in0=u[:, tap:tap + TT], scalar=cw[:, ci + tap:ci + tap + 1], in1=c[:, :],
                        op0=ALU.mult, op1=ALU.add),
                        reads=[f"ubuf{gv}{up}", "cw", f"cbuf{gv}{up}"], writes=[f"cbuf{gv}{up}"])
            P.op("act", lambda e, up=up: e.activation(out=sg[up][:, :], in_=cbuf[0][up][:, :], func=AF.Silu),
                 reads=[f"cbuf0{up}"], writes=[f"sg{up}"])
            P.op("pool", lambda e, up=up, j=j: e.tensor_tensor(out=hdn[:, j, :], in0=sg[up][:, :],
                                                              in1=cbuf[1][up][:, :], op=ALU.mult),
                 reads=[f"sg{up}", f"cbuf1{up}"], writes=[f"hdn{j}"])
            if not resident_loaded and j == 1:
                load_resident(); resident_loaded = True
        for bi in range(NB if dbg_block else 0):
            gb = tt * NB + bi
            par = gb % 2
            t0 = gb * 128
            P.dma("sp", hres[par][:, :], htm_d[t0:t0 + 128, :], writes=[f"hres{par}"])
            P.dma("pq", pTs[par][:, :, :], pT_d[:, :, t0:t0 + 128], writes=[f"pTs{par}"])
            for n in range(2):
                bk = 4 + n
                for j in range(NJ):
                    P.op("pe", lambda e, bk=bk, j=j, n=n, bi=bi: e.matmul(
                        B.t[bk][:, :], lhsT=hdn[:, j, bi * 128:(bi + 1) * 128], rhs=wdn[:, j, n * 512:(n + 1) * 512],
                        start=(j == 0), stop=(j == NJ - 1)),
                        reads=[f"hdn{j}", f"wdn{j}"], writes=[B.key[bk]])
                P.op("dve", lambda e, bk=bk, n=n, par=par: e.scalar_tensor_tensor(
                    out=rb[par][:, n * 512:(n + 1) * 512], in0=hres[par][:, n * 512:(n + 1) * 512], scalar=ALPHA,
                    in1=B.t[bk][:, :], op0=ALU.mult, op1=ALU.add),
                    reads=[f"hres{par}", B.key[bk]], writes=[f"b{par}r"])
            layer_norm_block(P, f"b{par}", rb[par], lng, lnb, yb[par], stat)
            yk = f"b{par}lnout"
            for k in range(8):
                bk = 6 + k // 4
                P.op("pe", lambda e, bk=bk, k=k, par=par: e.transpose(
                    out=B.t[bk][:, (k % 4) * 128:(k % 4 + 1) * 128], in_=yb[par][:, k * 128:(k + 1) * 128],
                    identity=ident[:]), reads=[yk, "ident"], writes=[B.key[bk]])
            for hh in range(2):
                P.op("act", lambda e, hh=hh, par=par: e.copy(
                    out=yT[par][:, hh * 4:(hh + 1) * 4, :], in_=B.t[6 + hh][:, :].rearrange("p (k t) -> p k t", k=4)),
                    reads=[B.key[6 + hh]], writes=[f"yT{par}"])
            for n in range(2):
                for k in range(8):
                    P.op("pe", lambda e, n=n, k=k, par=par: e.matmul(
                        B.t[n][:, :], lhsT=yT[par][:, k, :], rhs=wg[:, k, n * 512:(n + 1) * 512],
                        start=(k == 0), stop=(k == 7)), reads=[f"yT{par}", "wg"], writes=[B.key[n]])
                for k in range(2):
                    P.op("pe", lambda e, n=n, k=k, par=par: e.matmul(
                        B.t[2 + n][:, :], lhsT=pTs[par][:, k, :], rhs=wp[:, k, n * 512:(n + 1) * 512],
                        start=(k == 0), stop=(k == 1)), reads=[f"pTs{par}", "wp"], writes=[B.key[2 + n]])
                P.op("dve", lambda e, n=n, par=par: e.tensor_tensor(
                    out=gsb[par][:, n * 512:(n + 1) * 512], in0=B.t[n][:, :], in1=bg[:, n * 512:(n + 1) * 512],
                    op=ALU.add), reads=[B.key[n], "lnconst"], writes=[f"gsb{par}"])
            P.op("act", lambda e, par=par: e.activation(out=gsb[par][:, :], in_=gsb[par][:, :], func=AF.Sigmoid),
                 reads=[f"gsb{par}"], writes=[f"gsb{par}"])
            for n in range(2):
                P.op("dve", lambda e, n=n, par=par: e.tensor_tensor(
                    out=gsb[par][:, n * 512:(n + 1) * 512], in0=gsb[par][:, n * 512:(n + 1) * 512],
                    in1=B.t[2 + n][:, :], op=ALU.mult), reads=[f"gsb{par}", B.key[2 + n]], writes=[f"gsb{par}"])
            P.op("pool", lambda e, par=par: e.tensor_tensor(out=ob[par][:, :], in0=gsb[par][:, :], in1=yb[par][:, :],
                                                            op=ALU.add), reads=[f"gsb{par}", yk], writes=[f"ob{par}"])
            P.dma("sp", ho_d[t0:t0 + 128, :], ob[par][:, :], reads=[f"ob{par}"])
    P.finalize()
    return nc, P


def _fm(a, halo=0):
    t, f = a.shape
    return np.ascontiguousarray(a.reshape(t, f // 128, 128).transpose(2, 1, 0))


def _wr(w):
    k, n = w.shape
    return np.ascontiguousarray(w.reshape(k // 128, 128, n).transpose(1, 0, 2))


def _bc(v):
    return np.ascontiguousarray(np.broadcast_to(v[None, :], (128, v.shape[0])))


def ffn_inputs(h, p_i, w_up, conv_w, conv_b, w_down, g, b, w_proj, w_gate, b_gate):
    wup = w_up.reshape(8, 128, 2, NJ, 128).transpose(3, 1, 0, 2, 4).reshape(NJ, 128, 8, 256)
    cw = conv_w.reshape(3, 2, NJ, 128).transpose(3, 2, 1, 0).reshape(128, NJ * 6)
    cb = conv_b.reshape(2, NJ, 128).transpose(2, 1, 0).reshape(128, NJ * 2)
    common = {"wup": np.ascontiguousarray(wup), "cw": np.ascontiguousarray(cw), "cb": np.ascontiguousarray(cb),
              "wdn": _wr(w_down), "lng": _bc(g), "lnb": _bc(b), "bg": _bc(b_gate),
              "wg": _wr(w_gate), "wp": _wr(w_proj)}
    maps = []
    for c in range(NCORES):
        bi, s = divmod(c, 4)
        t0 = s * TOK
        hpad = np.zeros((TOK + 2, D), np.float32)
        lo = max(t0 - 2, 0)
        hpad[2 - (t0 - lo):] = h[bi, lo:t0 + TOK]
        m = dict(common)
        m["hT"] = _fm(hpad)
        m["htm"] = np.ascontiguousarray(h[bi, t0:t0 + TOK])
        m["pT"] = _fm(np.ascontiguousarray(p_i[bi, t0:t0 + TOK]))
        maps.append(m)
    return maps


NEG = -30000.0


def build_attn(dbg_nqg=None):
    nc = new_nc(); P = Prog(nc)
    NTT = SEQ // 512
    xT_d = din(nc, "xT", [128, 8, SEQ])
    wq_d = din(nc, "wq", [128, 8, 128]); wk_d = din(nc, "wk", [128, 8, 128]); wv_d = din(nc, "wv", [128, 8, 128])
    wf_d = din(nc, "wf", [128, 8, 2]); bf_d = din(nc, "bf", [2, 1])
    yb_d = dout(nc, "ybT", [128, SEQ])
    wc_d = din(nc, "wc", [128, 8, 384]); cwa_d = din(nc, "cwa", [128, 3])
    ya_d = dout(nc, "yaT", [128, SEQ])

    B = Banks(P)
    wc = P.sbuf([128, 8, 384], BF16, "wc_s"); cwa = P.sbuf([128, 3], F32, "cwa_s")
    P.dma("pq", wc[:, :, :], wc_d[:, :, :], writes=["w"]); P.dma("sp", cwa[:, :], cwa_d[:, :], writes=["w"])
    mbuf = [P.sbuf([128, 514], F32) for _ in range(2)]
    gcs = P.sbuf([128, 512], F32, "gcs"); cacc = P.sbuf([128, 512], F32, "cacc")
    yas = [P.sbuf([128, 512], F32) for _ in range(2)]
    P.op("pool", lambda e: e.memset(mbuf[1][:, :], 0.0), writes=["mbuf1"])
    identb = P.sbuf([128, 128], BF16, "identb")
    maskT = P.sbuf([128, 128], BF16, "maskT")
    tmpi = P.sbuf([128, 128], F32, "tmpi")
    P.op("pool", lambda e: e.memset(tmpi[:], 0.0), writes=["tmpi"])
    P.op("pool", lambda e: e.affine_select(out=tmpi[:], in_=tmpi[:], pattern=[[-1, 128]], compare_op=ALU.not_equal,
                                           fill=1.0, base=0, channel_multiplier=1), reads=["tmpi"], writes=["tmpi"])
    P.op("pool", lambda e: e.tensor_copy(out=identb[:], in_=tmpi[:]), reads=["tmpi"], writes=["identb"])
    tmpm = P.sbuf([128, 128], F32, "tmpm")
    P.op("pool", lambda e: e.memset(tmpm[:], 0.0), writes=["tmpm"])
    P.op("pool", lambda e: e.affine_select(out=tmpm[:], in_=tmpm[:], pattern=[[1, 128]], compare_op=ALU.is_ge,
                                           fill=NEG, base=0, channel_multiplier=-1), reads=["tmpm"], writes=["tmpm"])
    P.op("pool", lambda e: e.tensor_copy(out=maskT[:], in_=tmpm[:]), reads=["tmpm"], writes=["maskT"])

    wq = P.sbuf([128, 8, 128], BF16); wk = P.sbuf([128, 8, 128], BF16); wv = P.sbuf([128, 8, 128], BF16)
    wf = P.sbuf([128, 8, 2], BF16); bfs = P.sbuf([2, 1], F32)
    P.dma("pq", wq[:, :, :], wq_d[:, :, :], writes=["w"]); P.dma("pq", wk[:, :, :], wk_d[:, :, :], writes=["w"])
    P.dma("pq", wv[:, :, :], wv_d[:, :, :], writes=["w"]); P.dma("pq", wf[:, :, :], wf_d[:, :, :], writes=["w"])
    P.dma("sp", bfs[:, :], bf_d[:, :], writes=["w"])
    Q = [P.sbuf([70, SEQ], BF16, f"Q{h}") for h in range(2)]
    Kt = [P.sbuf([70, SEQ], BF16, f"K{h}") for h in range(2)]
    Va = P.sbuf([128, SEQ // 128, 2, 128], BF16, "Va")
    P.op("pool", lambda e: e.memset(Va[:, :, :, 64:128], 1.0), writes=["Vones"])
    xs = [P.sbuf([128, 8, 512], BF16) for _ in range(2)]
    Fl = P.sbuf([2, SEQ], F32, "Fl")

    for tt in range(NTT):
        xp = tt % 2
        c0 = tt * 512
        P.dma("pq", xs[xp][:, :, :], xT_d[:, :, c0:c0 + 512], writes=[f"xs{xp}"])
        for (w, dst, nm, bk, scale) in ((wq, Q, "Q", 0, 0.125), (wk, Kt, "K", 1, 1.0)):
            for k in range(8):
                P.op("pe", lambda e, w=w, k=k, bk=bk, xp=xp: e.matmul(
                    B.t[bk][:, :], lhsT=w[:, k, :], rhs=xs[xp][:, k, :], start=(k == 0), stop=(k == 7)),
                    reads=["w", f"xs{xp}"], writes=[B.key[bk]])
            for h in range(2):
                eng = "act" if h == 0 else "dve"
                if eng == "act":
                    P.op("act", lambda e, dst=dst, h=h, bk=bk, c0=c0, scale=scale: e.activation(
                        out=dst[h][0:64, c0:c0 + 512], in_=B.t[bk][h * 64:(h + 1) * 64, :], func=AF.Copy, scale=scale),
                        reads=[B.key[bk]], writes=[f"{nm}{h}_{tt}"])
                else:
                    P.op("dve", lambda e, dst=dst, h=h, bk=bk, c0=c0, scale=scale: e.tensor_scalar_mul(
                        out=dst[h][0:64, c0:c0 + 512], in0=B.t[bk][h * 64:(h + 1) * 64, :], scalar1=scale),
                        reads=[B.key[bk]], writes=[f"{nm}{h}_{tt}"])
        for bi in range(4):
            for k in range(8):
                P.op("pe", lambda e, k=k, bi=bi, xp=xp: e.matmul(
                    B.t[2][:, bi * 128:(bi + 1) * 128], lhsT=xs[xp][:, k, bi * 128:(bi + 1) * 128], rhs=wv[:, k, :],
                    start=(k == 0), stop=(k == 7)), reads=["w", f"xs{xp}"], writes=[B.key[2]])
        P.op("dve", lambda e, tt=tt: e.tensor_copy(
            out=Va[:, tt * 4:(tt + 1) * 4, :, 0:64],
            in_=B.t[2][:, :].rearrange("p (b h d) -> p b h d", b=4, h=2)),
            reads=[B.key[2]], writes=[f"V{tt}"])
        for k in range(8):
            P.op("pe", lambda e, k=k, xp=xp: e.matmul(
                B.t[3][0:2, :], lhsT=wf[:, k, :], rhs=xs[xp][:, k, :], start=(k == 0), stop=(k == 7)),
                reads=["w", f"xs{xp}"], writes=[B.key[3]])
        P.op("act", lambda e, c0=c0: e.activation(out=Fl[:, c0:c0 + 512], in_=B.t[3][0:2, :], func=AF.Sigmoid,
                                                  bias=bfs[:, 0:1]), reads=[B.key[3], "w"], writes=["Fl"])
        for ci in range(3):
            for k in range(8):
                P.op("pe", lambda e, ci=ci, k=k, xp=xp: e.matmul(
                    B.t[4 + ci][:, :], lhsT=wc[:, k, ci * 128:(ci + 1) * 128], rhs=xs[xp][:, k, :],
                    start=(k == 0), stop=(k == 7)), reads=["w", f"xs{xp}"], writes=[B.key[4 + ci]])
        mp = tt % 2
        P.op("act", lambda e: e.copy(out=gcs[:, :], in_=B.t[5][:, :]), reads=[B.key[5]], writes=["gcs"])
        P.op("pool", lambda e, mp=mp: e.tensor_copy(out=mbuf[mp][:, 0:2], in_=mbuf[1 - mp][:, 512:514]),
             reads=[f"mbuf{1 - mp}"], writes=[f"mbuf{mp}h"])
        P.op("dve", lambda e, mp=mp: e.tensor_tensor(out=mbuf[mp][:, 2:514], in0=B.t[6][:, :], in1=gcs[:, :],
                                                     op=ALU.mult), reads=[B.key[6], "gcs"], writes=[f"mbuf{mp}"])
        P.op("act", lambda e, mp=mp: e.activation(out=cacc[:, :], in_=mbuf[mp][:, 2:514], func=AF.Copy,
                                                  scale=cwa[:, 2:3]), reads=[f"mbuf{mp}", "w"], writes=["cacc"])
        for tap in (1, 0):
            P.op("dve", lambda e, mp=mp, tap=tap: e.scalar_tensor_tensor(
                out=cacc[:, :], in0=mbuf[mp][:, tap:tap + 512], scalar=cwa[:, tap:tap + 1], in1=cacc[:, :],
                op0=ALU.mult, op1=ALU.add), reads=[f"mbuf{mp}", f"mbuf{mp}h", "w", "cacc"], writes=["cacc"])
        P.op("dve", lambda e, mp=mp: e.tensor_tensor(out=yas[mp][:, :], in0=B.t[4][:, :], in1=cacc[:, :], op=ALU.mult),
             reads=[B.key[4], "cacc"], writes=[f"yas{mp}"])
        P.dma("sp", ya_d[:, c0:c0 + 512], yas[mp][:, :], reads=[f"yas{mp}"])
    P.op("act", lambda e: e.activation(out=Fl[:, :], in_=Fl[:, :], func=AF.Ln), reads=["Fl"], writes=["Fl"])
    ones2 = P.sbuf([2, 512], F32, "ones2")
    P.op("pool", lambda e: e.memset(ones2[:], 1.0), writes=["ones2"])
    carry = P.sbuf([2, 1], F32, "carry")
    P.op("pool", lambda e: e.memset(carry[:], 0.0), writes=["carry"])
    PC = 512
    augq = P.sbuf([2, 6, PC], BF16, "augq"); augk = P.sbuf([2, 6, PC], BF16, "augk")
    t1 = P.sbuf([2, PC], F32, "t1"); t2 = P.sbuf([2, PC], F32, "t2")
    P.op("pool", lambda e: e.memset(augq[:, 3:6, :], 1.0), writes=["augq"])
    P.op("pool", lambda e: e.memset(augk[:, 0:3, :], 1.0), writes=["augk"])
    for pc in range(SEQ // PC):
        c0 = pc * PC
        P.op("dve", lambda e, c0=c0: e.tensor_tensor_scan(
            out=Fl[:, c0:c0 + PC], data0=ones2[:, :], data1=Fl[:, c0:c0 + PC], initial=carry[:, 0:1],
            op0=ALU.mult, op1=ALU.add), reads=["Fl", "ones2", "carry"], writes=["Fl"])
        P.op("dve", lambda e, c0=c0: e.tensor_copy(out=carry[:, 0:1], in_=Fl[:, c0 + PC - 1:c0 + PC]),
             reads=["Fl"], writes=["carry"])
        src = Fl[:, c0:c0 + PC]
        for term in range(3):
            P.op("dve", lambda e, term=term, src=src: e.tensor_copy(out=augq[:, term, :], in_=src),
                 reads=["Fl", "t1", "t2"], writes=["augq"])
            P.op("dve", lambda e, term=term: e.tensor_scalar_mul(out=augk[:, 3 + term, :], in0=augq[:, term, :],
                                                                 scalar1=-1.0), reads=["augq"], writes=["augk"])
            if term < 2:
                dst = t1 if term == 0 else t2
                P.op("dve", lambda e, term=term, src=src, dst=dst: e.tensor_tensor(
                    out=dst[:, :], in0=src, in1=augq[:, term, :], op=ALU.subtract),
                    reads=["Fl", "t1", "t2", "augq"], writes=["t1", "t2"])
                src = dst[:, :]
        for h in range(2):
            P.dma("sp", Q[h][64:70, c0:c0 + PC], augq[h:h + 1, :, :], reads=["augq"], writes=[f"Qaug{h}"])
            P.dma("sp", Kt[h][64:70, c0:c0 + PC], augk[h:h + 1, :, :], reads=["augk"], writes=[f"Kaug{h}"])
    pT = [P.sbuf([128, 512], BF16) for _ in range(4)]
    rd = P.sbuf([64, 512], F32, "rd")
    yst = [P.sbuf([128, 512], F32) for _ in range(2)]
    NQG = dbg_nqg or (SEQ // 512)
    tile_i = 0
    for qg in range(NQG):
        q0 = qg * 512
        for h in range(2):
            ob = 4 + h
            nkb = 4 * qg + 4
            for kb in range(nkb):
                j = kb - 4 * qg
                qoff = 0 if j <= 0 else j * 128
                n = 512 - qoff
                sb = tile_i % 4; tile_i += 1
                diag = j >= 0
                P.op("pe", lambda e, sb=sb, h=h, kb=kb, q0=q0, qoff=qoff, n=n, diag=diag: e.matmul(
                    B.t[sb][:, 0:n], lhsT=Kt[h][0:70, kb * 128:(kb + 1) * 128], rhs=Q[h][0:70, q0 + qoff:q0 + 512],
                    start=True, stop=(not diag)),
                    reads=[f"K{h}_{kb // 4}", f"Kaug{h}", f"Q{h}_{qg}", f"Qaug{h}"], writes=[B.key[sb]])
                if diag:
                    P.op("pe", lambda e, sb=sb: e.matmul(B.t[sb][:, 0:128], lhsT=identb[:, :], rhs=maskT[:, :],
                                                         start=False, stop=True),
                         reads=["identb", "maskT"], writes=[B.key[sb]])
                P.op("act", lambda e, sb=sb, n=n: e.activation(out=pT[sb][:, 0:n], in_=B.t[sb][:, 0:n], func=AF.Exp),
                     reads=[B.key[sb]], writes=[f"pT{sb}"])
                P.op("pe", lambda e, sb=sb, h=h, kb=kb, qoff=qoff, n=n, ob=ob, nkb=nkb: e.matmul(
                    B.t[ob][:, qoff:512], lhsT=Va[:, kb, h, :], rhs=pT[sb][:, 0:n],
                    start=(kb == 0), stop=(kb == nkb - 1)),
                    reads=[f"V{kb // 4}", "Vones", f"pT{sb}"], writes=[B.key[ob]])
            yp = qg % 2
            P.op("dve", lambda e, ob=ob: e.reciprocal(out=rd[0:64, :], in_=B.t[ob][64:128, :]),
                 reads=[B.key[ob]], writes=["rd"])
            P.op("dve", lambda e, ob=ob, h=h, yp=yp: e.tensor_tensor(
                out=yst[yp][h * 64:(h + 1) * 64, :], in0=B.t[ob][0:64, :], in1=rd[0:64, :], op=ALU.mult),
                reads=[B.key[ob], "rd"], writes=[f"yst{yp}"])
        P.dma("sp", yb_d[:, q0:q0 + 512], yst[qg % 2][:, :], reads=[f"yst{qg % 2}"])
    P.finalize()
    return nc, P


def attn_inputs(x, w_in, b_f, conv_w):
    maps = []
    for c in range(NCORES):
        bi, hp = divmod(c, 4)
        m = {"xT": _fm(np.ascontiguousarray(x[bi])),
             "wq": _wr(np.ascontiguousarray(w_in[:, 1536 + hp * 128:1536 + (hp + 1) * 128])),
             "wk": _wr(np.ascontiguousarray(w_in[:, 2048 + hp * 128:2048 + (hp + 1) * 128])),
             "wv": _wr(np.ascontiguousarray(w_in[:, 2560 + hp * 128:2560 + (hp + 1) * 128])),
             "wf": _wr(np.ascontiguousarray(w_in[:, 3072 + hp * 2:3072 + hp * 2 + 2])),
             "bf": np.ascontiguousarray(b_f[hp * 2:hp * 2 + 2].reshape(2, 1)),
             "wc": _wr(np.ascontiguousarray(np.concatenate(
                 [w_in[:, i * 512 + hp * 128:i * 512 + (hp + 1) * 128] for i in range(3)], axis=1))),
             "cwa": np.ascontiguousarray(conv_w[:, hp * 128:(hp + 1) * 128].T)}
        maps.append(m)
    return maps


def build_projln(KC):
    nc = new_nc(); P = Prog(nc)
    yT_d = din(nc, "yT", [128, KC, TOK])
    htm_d = din(nc, "htm", [TOK, D])
    w_d = din(nc, "w", [128, KC, D])
    lng_d = din(nc, "lng", [128, D]); lnb_d = din(nc, "lnb", [128, D])
    ho_d = dout(nc, "ho", [TOK, D])
    B = Banks(P)
    lng = P.sbuf([128, D], F32); lnb = P.sbuf([128, D], F32)
    P.dma("sp", lng[:], lng_d[:, :], writes=["lnconst"]); P.dma("sp", lnb[:], lnb_d[:, :], writes=["lnconst"])
    w = P.sbuf([128, KC, D], BF16)
    yTs = P.sbuf([128, KC, TOK], BF16)
    for k in range(KC):
        P.dma("pq", w[:, k, :], w_d[:, k, :], writes=[f"w{k}"])
    for q in range(4):
        P.dma("pq", yTs[:, :, q * 512:(q + 1) * 512], yT_d[:, :, q * 512:(q + 1) * 512], writes=[f"yT{q}"])
    hres = [P.sbuf([128, D], F32) for _ in range(2)]
    rb = [P.sbuf([128, D], F32) for _ in range(2)]
    ob = [P.sbuf([128, D], F32) for _ in range(2)]
    stat = {"st6": P.sbuf([128, 12], F32), "mv": P.sbuf([128, 2], F32), "rstd": P.sbuf([128, 1], F32)}
    for gb in range(TOK // 128):
        par = gb % 2
        t0 = gb * 128
        P.dma("sp", hres[par][:, :], htm_d[t0:t0 + 128, :], writes=[f"hres{par}"])
        for n in range(2):
            bk = (gb % 2) * 2 + n
            for k in range(KC):
                P.op("pe", lambda e, bk=bk, k=k, n=n, t0=t0: e.matmul(
                    B.t[bk][:, :], lhsT=yTs[:, k, t0:t0 + 128], rhs=w[:, k, n * 512:(n + 1) * 512],
                    start=(k == 0), stop=(k == KC - 1)), reads=[f"yT{gb // 4}", f"w{k}"], writes=[B.key[bk]])
            P.op("dve", lambda e, bk=bk, n=n, par=par: e.scalar_tensor_tensor(
                out=rb[par][:, n * 512:(n + 1) * 512], in0=hres[par][:, n * 512:(n + 1) * 512], scalar=ALPHA,
                in1=B.t[bk][:, :], op0=ALU.mult, op1=ALU.add),
                reads=[f"hres{par}", B.key[bk]], writes=[f"b{par}r"])
        layer_norm_block(P, f"b{par}", rb[par], lng, lnb, ob[par], stat)
        P.dma("sp", ho_d[t0:t0 + 128, :], ob[par][:, :], reads=[f"b{par}lnout"])
    P.finalize()
    return nc, P


def projln_inputs(yT_full, h, w, g, b):
    K = yT_full.shape[1]
    common = {"w": _wr(w), "lng": _bc(g), "lnb": _bc(b)}
    maps = []
    for c in range(NCORES):
        bi, s = divmod(c, 4)
        t0 = s * TOK
        m = dict(common)
        m["yT"] = np.ascontiguousarray(yT_full[bi, :, t0:t0 + TOK].reshape(K // 128, 128, TOK).transpose(1, 0, 2))
        m["htm"] = np.ascontiguousarray(h[bi, t0:t0 + TOK])
        maps.append(m)
    return maps


RMS_EPS = 1e-5


def build_mamba(dbg_ntt=None):
    nc = new_nc(); P = Prog(nc)
    NTT = dbg_ntt or (SEQ // 512)
    hT_d = din(nc, "hT", [128, 8, SEQ])
    wx_d = din(nc, "wx", [128, 8, 768]); wz_d = din(nc, "wz", [128, 8, 512]); wdt_d = din(nc, "wdt", [128, 8, 8])
    cwm_d = din(nc, "cwm", [128, 24]); cbm_d = din(nc, "cbm", [128, 6])
    dtb_d = din(nc, "dtb", [128, 8]); alog_d = din(nc, "alog", [128, 8]); dsk_d = din(nc, "dsk", [128, 8])
    ng_d = din(nc, "ng", [128, 512])
    u_d = dout(nc, "u", [SEQ, 512])
    B = Banks(P)
    identF = make_ident(P, F32)
    identb = P.sbuf([128, 128], BF16, "identb")
    P.op("pool", lambda e: e.tensor_copy(out=identb[:], in_=identF[:]), reads=["ident"], writes=["identb"])
    maskT = P.sbuf([128, 128], BF16, "maskT"); tmpm = P.sbuf([128, 128], F32, "tmpm")
    P.op("pool", lambda e: e.memset(tmpm[:], 0.0), writes=["tmpm"])
    P.op("pool", lambda e: e.affine_select(out=tmpm[:], in_=tmpm[:], pattern=[[1, 128]], compare_op=ALU.is_ge,
                                           fill=NEG, base=0, channel_multiplier=-1), reads=["tmpm"], writes=["tmpm"])
    P.op("pool", lambda e: e.tensor_copy(out=maskT[:], in_=tmpm[:]), reads=["tmpm"], writes=["maskT"])
    Tm = P.sbuf([128, 128], F32, "Tm")
    P.op("pool", lambda e: e.memset(Tm[:], 1.0), writes=["Tm"])
    P.op("pool", lambda e: e.affine_select(out=Tm[:], in_=Tm[:], pattern=[[1, 128]], compare_op=ALU.is_ge,
                                           fill=0.0, base=0, channel_multiplier=-1), reads=["Tm"], writes=["Tm"])
    sel = P.sbuf([8, 8, 128], F32, "sel")
    P.op("pool", lambda e: e.memset(sel[:], 0.0), writes=["sel"])
    P.op("pool", lambda e: e.affine_select(out=sel[:], in_=sel[:], pattern=[[-1, 8], [0, 128]],
                                           compare_op=ALU.not_equal, fill=1.0, base=0, channel_multiplier=1),
         reads=["sel"], writes=["sel"])
    sel127 = P.sbuf([128, 128], F32, "sel127")
    P.op("pool", lambda e: e.memset(sel127[:], 0.0), writes=["sel127"])
    P.op("pool", lambda e: e.affine_select(out=sel127[:], in_=sel127[:], pattern=[[0, 128]],
                                           compare_op=ALU.not_equal, fill=1.0, base=-127, channel_multiplier=1),
         reads=["sel127"], writes=["sel127"])
    wx = P.sbuf([128, 8, 768], BF16); wz = P.sbuf([128, 8, 512], BF16); wdt = P.sbuf([128, 8, 8], BF16)
    P.dma("pq", wx[:, :, :], wx_d[:, :, :], writes=["w"]); P.dma("pq", wz[:, :, :], wz_d[:, :, :], writes=["w"])
    P.dma("pq", wdt[:, :, :], wdt_d[:, :, :], writes=["w"])
    cwm = P.sbuf([128, 24], F32); cbm = P.sbuf([128, 6], F32)
    dtb = P.sbuf([128, 8], F32); Abc = P.sbuf([128, 8], F32); dsk = P.sbuf([128, 8], F32); ng = P.sbuf([128, 512], F32)
    for (t, d_) in ((cwm, cwm_d), (cbm, cbm_d), (dtb, dtb_d), (Abc, alog_d), (dsk, dsk_d), (ng, ng_d)):
        P.dma("sp", t[:], d_[:, :], writes=["c"])
    P.op("act", lambda e: e.activation(out=Abc[:], in_=Abc[:], func=AF.Exp), reads=["c"], writes=["A"])
    P.op("dve", lambda e: e.tensor_scalar_mul(out=Abc[:], in0=Abc[:], scalar1=-1.0), reads=["A"], writes=["A"])
    hs = [P.sbuf([128, 8, 512], BF16) for _ in range(2)]
    ub = [[P.sbuf([128, 515], F32) for _ in range(2)] for _ in range(6)]
    for ci in range(6):
        P.op("pool", lambda e, ci=ci: e.memset(ub[ci][1][:, :], 0.0), writes=[f"ub{ci}1"])
    cacc = P.sbuf([128, 512], F32)
    xf = [P.sbuf([128, 512], F32) for _ in range(5)]
    BT = P.sbuf([128, 512], BF16); CT = P.sbuf([128, 512], BF16)
    xs_tm = P.sbuf([128, 512], F32); Btm = P.sbuf([128, 128], BF16)
    dtt = P.sbuf([128, 8], F32); dt = P.sbuf([128, 8], F32); av = P.sbuf([128, 8], F32)
    nacs = P.sbuf([128, 8], F32); Ev = P.sbuf([128, 8], F32); cdbs = P.sbuf([128, 8], F32)
    acsFs = P.sbuf([8, 128], F32); cbT = P.sbuf([128, 128], F32)
    LTs = P.sbuf([128, 8, 128], F32); MT = P.sbuf([128, 8, 128], BF16)
    xc = P.sbuf([128, 512], BF16); xcd = P.sbuf([128, 512], BF16)
    ybuf = P.sbuf([128, 512], F32); xsk = P.sbuf([128, 512], F32)
    prev = P.sbuf([128, 512], F32); prevb = P.sbuf([128, 512], BF16)
    P.op("pool", lambda e: e.memset(prev[:], 0.0), writes=["prev"])
    P.op("pool", lambda e: e.memset(prevb[:], 0.0), writes=["prevb"])
    sz = P.sbuf([128, 512], F32); ug = P.sbuf([128, 512], F32); junk = P.sbuf([128, 512], F32)
    ssq = P.sbuf([128, 1], F32)
    uo = [P.sbuf([128, 512], F32) for _ in range(2)]

    def bc8(t):
        return t[:, 0:8].unsqueeze(2).to_broadcast([128, 8, 64])

    def v3(t):
        return t[:, :].rearrange("p (h d) -> p h d", h=8)

    for tt in range(NTT):
        hp = tt % 2
        c0 = tt * 512
        P.dma("pq", hs[hp][:, :, :], hT_d[:, :, c0:c0 + 512], writes=[f"hs{hp}"])
        for ci in range(6):
            bk = ci % 2
            for k in range(8):
                P.op("pe", lambda e, bk=bk, k=k, ci=ci, hp=hp: e.matmul(
                    B.t[bk][:, :], lhsT=wx[:, k, ci * 128:(ci + 1) * 128], rhs=hs[hp][:, k, :],
                    start=(k == 0), stop=(k == 7)), reads=["w", f"hs{hp}"], writes=[B.key[bk]])
            u = ub[ci][hp]; uprev = ub[ci][1 - hp]
            P.op("act", lambda e, u=u, bk=bk: e.copy(out=u[:, 3:515], in_=B.t[bk][:, :]),
                 reads=[B.key[bk]], writes=[f"ub{ci}{hp}"])
            P.op("pool", lambda e, u=u, uprev=uprev: e.tensor_copy(out=u[:, 0:3], in_=uprev[:, 512:515]),
                 reads=[f"ub{ci}{1 - hp}"], writes=[f"ub{ci}{hp}h"])
            P.op("act", lambda e, u=u, ci=ci: e.activation(out=cacc[:, :], in_=u[:, 3:515], func=AF.Identity,
                                                           scale=cwm[:, ci * 4 + 3:ci * 4 + 4], bias=cbm[:, ci:ci + 1]),
                 reads=[f"ub{ci}{hp}", "c"], writes=["cacc"])
            for tap in range(3):
                P.op("dve", lambda e, u=u, ci=ci, tap=tap: e.scalar_tensor_tensor(
                    out=cacc[:, :], in0=u[:, tap:tap + 512], scalar=cwm[:, ci * 4 + tap:ci * 4 + tap + 1],
                    in1=cacc[:, :], op0=ALU.mult, op1=ALU.add),
                    reads=[f"ub{ci}{hp}", f"ub{ci}{hp}h", "c", "cacc"], writes=["cacc"])
            if ci < 5:
                P.op("act", lambda e, ci=ci: e.activation(out=xf[ci][:, :], in_=cacc[:, :], func=AF.Silu),
                     reads=["cacc"], writes=[f"xf{ci}"])
                if ci == 4:
                    P.op("dve", lambda e: e.tensor_copy(out=BT[:, :], in_=xf[4][:, :]), reads=["xf4"], writes=["BT"])
            else:
                P.op("act", lambda e: e.activation(out=CT[:, :], in_=cacc[:, :], func=AF.Silu),
                     reads=["cacc"], writes=["CT"])
        for bi in range(4):
            blk = slice(bi * 128, (bi + 1) * 128)
            t0 = c0 + bi * 128
            for k in range(8):
                P.op("pe", lambda e, k=k, blk=blk, hp=hp: e.matmul(
                    B.t[2][:, :], lhsT=hs[hp][:, k, blk], rhs=wz[:, k, :], start=(k == 0), stop=(k == 7)),
                    reads=["w", f"hs{hp}"], writes=[B.key[2]])
            for k in range(8):
                P.op("pe", lambda e, k=k, blk=blk, hp=hp: e.matmul(
                    B.t[0][:, 0:8], lhsT=hs[hp][:, k, blk], rhs=wdt[:, k, :], start=(k == 0), stop=(k == 7)),
                    reads=["w", f"hs{hp}"], writes=[B.key[0]])
            for ci in range(4):
                P.op("pe", lambda e, ci=ci, blk=blk: e.transpose(
                    out=B.t[3][:, ci * 128:(ci + 1) * 128], in_=xf[ci][:, blk], identity=identF[:]),
                    reads=[f"xf{ci}", "ident"], writes=[B.key[3]])
            P.op("pe", lambda e, blk=blk: e.transpose(out=B.t[1][:, 0:128], in_=xf[4][:, blk], identity=identF[:]),
                 reads=["xf4", "ident"], writes=[B.key[1]])
            P.op("act", lambda e: e.copy(out=xs_tm[:, :], in_=B.t[3][:, :]), reads=[B.key[3]], writes=["xs_tm"])
            P.op("dve", lambda e: e.tensor_copy(out=Btm[:, :], in_=B.t[1][:, 0:128]), reads=[B.key[1]], writes=["Btm"])
            P.op("dve", lambda e: e.tensor_tensor(out=dtt[:, :], in0=B.t[0][:, 0:8], in1=dtb[:, :], op=ALU.add),
                 reads=[B.key[0], "c"], writes=["dtt"])
            P.op("act", lambda e: e.activation(out=dt[:, :], in_=dtt[:, :], func=AF.Softplus),
                 reads=["dtt"], writes=["dt"])
            P.op("dve", lambda e: e.tensor_tensor(out=av[:, :], in0=dt[:, :], in1=Abc[:, :], op=ALU.mult),
                 reads=["dt", "A"], writes=["av"])
            P.op("pe", lambda e: e.matmul(B.t[0][:, 8:16], lhsT=Tm[:, :], rhs=av[:, :], start=True, stop=True),
                 reads=["Tm", "av"], writes=[B.key[0]])
            P.op("pe", lambda e: e.matmul(B.t[0][0:8, 128:256], lhsT=av[:, :], rhs=Tm[:, :], start=True, stop=True),
                 reads=["Tm", "av"], writes=[B.key[0]])
            P.op("dve", lambda e: e.tensor_scalar_mul(out=nacs[:, :], in0=B.t[0][:, 8:16], scalar1=-1.0),
                 reads=[B.key[0]], writes=["nacs"])
            P.op("act", lambda e: e.activation(out=Ev[:, :], in_=B.t[0][:, 8:16], func=AF.Exp),
                 reads=[B.key[0]], writes=["Ev"])
            P.op("act", lambda e: e.copy(out=acsFs[:, :], in_=B.t[0][0:8, 128:256]), reads=[B.key[0]], writes=["acsFs"])
            P.op("pe", lambda e: e.matmul(B.t[0][:, 16:24], lhsT=sel127[:, :], rhs=Ev[:, :], start=True, stop=True),
                 reads=["sel127", "Ev"], writes=[B.key[0]])
            P.op("dve", lambda e: e.tensor_copy(out=cdbs[:, :], in_=B.t[0][:, 16:24]), reads=[B.key[0]], writes=["cdbs"])
            P.op("pe", lambda e, blk=blk: e.matmul(B.t[1][:, 128:256], lhsT=BT[:, blk], rhs=CT[:, blk],
                                                   start=True, stop=True), reads=["BT", "CT"], writes=[B.key[1]])
            P.op("act", lambda e: e.copy(out=cbT[:, :], in_=B.t[1][:, 128:256]), reads=[B.key[1]], writes=["cbT"])
            for h in range(8):
                bk = 4 + h // 4
                cs = slice((h % 4) * 128, (h % 4 + 1) * 128)
                P.op("pe", lambda e, bk=bk, cs=cs, h=h: e.matmul(B.t[bk][:, cs], lhsT=sel[0:8, h, :], rhs=acsFs[:, :],
                                                                 start=True, stop=False),
                     reads=["sel", "acsFs"], writes=[B.key[bk]])
                P.op("pe", lambda e, bk=bk, cs=cs: e.matmul(B.t[bk][:, cs], lhsT=identb[:, :], rhs=maskT[:, :],
                                                            start=False, stop=True),
                     reads=["identb", "maskT"], writes=[B.key[bk]])
            for h in range(8):
                bk = 4 + h // 4
                cs = slice((h % 4) * 128, (h % 4 + 1) * 128)
                P.op("act", lambda e, bk=bk, cs=cs, h=h: e.activation(out=LTs[:, h, :], in_=B.t[bk][:, cs], func=AF.Exp,
                                                                      bias=nacs[:, h:h + 1]),
                     reads=[B.key[bk], "nacs"], writes=["LTs"])
            P.op("dve", lambda e: e.tensor_tensor(out=MT[:, :, :], in0=LTs[:, :, :],
                                                  in1=cbT[:, :].unsqueeze(1).to_broadcast([128, 8, 128]), op=ALU.mult),
                 reads=["LTs", "cbT"], writes=["MT"])
            P.op("dve", lambda e: e.tensor_tensor(out=v3(xc), in0=v3(xs_tm), in1=bc8(dt), op=ALU.mult),
                 reads=["xs_tm", "dt"], writes=["xc"])
            P.op("dve", lambda e: e.tensor_tensor(out=v3(xcd), in0=v3(xc),
                                                  in1=LTs[:, :, 127:128].to_broadcast([128, 8, 64]), op=ALU.mult),
                 reads=["xc", "LTs"], writes=["xcd"])
            for h in range(8):
                P.op("pe", lambda e, h=h: e.matmul(B.t[6][:, h * 64:(h + 1) * 64], lhsT=MT[:, h, :],
                                                   rhs=xc[:, h * 64:(h + 1) * 64], start=True, stop=True),
                     reads=["MT", "xc"], writes=[B.key[6]])
            P.op("pe", lambda e, blk=blk: e.matmul(B.t[7][:, :], lhsT=CT[:, blk], rhs=prevb[:, :], start=True, stop=True),
                 reads=["CT", "prevb"], writes=[B.key[7]])
            P.op("act", lambda e: e.activation(out=sz[:, :], in_=B.t[2][:, :], func=AF.Silu),
                 reads=[B.key[2]], writes=["sz"])
            P.op("pe", lambda e: e.matmul(B.t[2][:, :], lhsT=Btm[:, :], rhs=xcd[:, :], start=True, stop=True),
                 reads=["Btm", "xcd"], writes=[B.key[2]])
            P.op("dve", lambda e: e.tensor_tensor(out=v3(ybuf), in0=B.t[7][:, :].rearrange("p (h d) -> p h d", h=8),
                                                  in1=bc8(Ev), op=ALU.mult), reads=[B.key[7], "Ev"], writes=["ybuf"])
            P.op("dve", lambda e: e.tensor_tensor(out=ybuf[:, :], in0=ybuf[:, :], in1=B.t[6][:, :], op=ALU.add),
                 reads=[B.key[6], "ybuf"], writes=["ybuf"])
            P.op("dve", lambda e: e.tensor_tensor(out=v3(xsk), in0=v3(xs_tm), in1=bc8(dsk), op=ALU.mult),
                 reads=["xs_tm", "c"], writes=["xsk"])
            P.op("pool", lambda e: e.tensor_tensor(out=ybuf[:, :], in0=ybuf[:, :], in1=xsk[:, :], op=ALU.add),
                 reads=["ybuf", "xsk"], writes=["ybuf"])
            P.op("dve", lambda e: e.tensor_tensor(out=v3(prev), in0=v3(prev), in1=bc8(cdbs), op=ALU.mult),
                 reads=["prev", "cdbs"], writes=["prev"])
            P.op("dve", lambda e: e.tensor_tensor(out=prev[:, :], in0=prev[:, :], in1=B.t[2][:, :], op=ALU.add),
                 reads=["prev", B.key[2]], writes=["prev"])
            P.op("act", lambda e: e.copy(out=prevb[:, :], in_=prev[:, :]), reads=["prev"], writes=["prevb"])
            P.op("dve", lambda e: e.tensor_tensor(out=ug[:, :], in0=ybuf[:, :], in1=sz[:, :], op=ALU.mult),
                 reads=["ybuf", "sz"], writes=["ug"])
            P.op("act", lambda e: e.activation(out=junk[:, :], in_=ug[:, :], func=AF.Square, accum_out=ssq[:, 0:1]),
                 reads=["ug"], writes=["ssq", "junk"])
            P.op("dve", lambda e: e.tensor_scalar(out=ssq[:, 0:1], in0=ssq[:, 0:1], scalar1=1.0 / 512, scalar2=RMS_EPS,
                                                  op0=ALU.mult, op1=ALU.add), reads=["ssq"], writes=["ssq"])
            P.op("act", lambda e: e.activation(out=ssq[:, 0:1], in_=ssq[:, 0:1], func=AF.Sqrt),
                 reads=["ssq"], writes=["ssq"])
            P.op("dve", lambda e: e.reciprocal(out=ssq[:, 0:1], in_=ssq[:, 0:1]), reads=["ssq"], writes=["ssq"])
            up = (tt * 4 + bi) % 2
            P.op("dve", lambda e, up=up: e.scalar_tensor_tensor(out=uo[up][:, :], in0=ug[:, :], scalar=ssq[:, 0:1],
                                                                in1=ng[:, :], op0=ALU.mult, op1=ALU.mult),
                 reads=["ug", "ssq", "c"], writes=[f"uo{up}"])
            P.dma("sp", u_d[t0:t0 + 128, :], uo[up][:, :], reads=[f"uo{up}"])
    P.finalize()
    return nc, P


def mamba_inputs(h, w_in, conv_w, conv_b, dt_bias, a_log, d_skip, norm_g):
    maps = []
    for c in range(NCORES):
        bi, g = divmod(c, 4)
        xcols = np.concatenate([np.arange(2048 + g * 512, 2048 + (g + 1) * 512),
                                np.arange(4096 + g * 128, 4096 + (g + 1) * 128),
                                np.arange(4608 + g * 128, 4608 + (g + 1) * 128)])
        ch = xcols - 2048
        cwm = conv_w[:, ch].reshape(4, 6, 128).transpose(2, 1, 0).reshape(128, 24)
        cbm = conv_b[ch].reshape(6, 128).T
        hs = slice(g * 8, (g + 1) * 8)
        m = {"hT": _fm(np.ascontiguousarray(h[bi])),
             "wx": _wr(np.ascontiguousarray(w_in[:, xcols])),
             "wz": _wr(np.ascontiguousarray(w_in[:, g * 512:(g + 1) * 512])),
             "wdt": _wr(np.ascontiguousarray(w_in[:, 5120 + g * 8:5120 + (g + 1) * 8])),
             "cwm": np.ascontiguousarray(cwm), "cbm": np.ascontiguousarray(cbm),
             "dtb": _bc(dt_bias[hs]), "alog": _bc(a_log[hs]), "dsk": _bc(d_skip[hs]),
             "ng": _bc(norm_g[g * 512:(g + 1) * 512])}
        maps.append(m)
    return maps


def _run(nc, maps):
    res = run_bass_kernel_spmd(nc, maps, core_ids=list(range(NCORES)))
    return res.results


def _tok_gather(results, key):
    return np.stack([np.asarray(r[key]) for r in results]).reshape(2, SEQ, -1)


def kernel(**inp):
    f = lambda k: np.ascontiguousarray(np.asarray(inp[k], dtype=np.float32))
    x = f("x"); p = f("p")
    nc, _ = build_attn()
    res = _run(nc, attn_inputs(x, f("even_w_in")[0], f("even_b_f")[0], f("even_conv_w")[0]))
    yT = np.empty((2, 1024, SEQ), np.float32)
    for c in range(NCORES):
        b, hp = divmod(c, 4)
        yT[b, hp * 128:(hp + 1) * 128] = np.asarray(res[c]["yaT"])
        yT[b, 512 + hp * 128:512 + (hp + 1) * 128] = np.asarray(res[c]["ybT"])
    nc, _ = build_projln(8)
    h1 = _tok_gather(_run(nc, projln_inputs(yT, x, f("even_w_out")[0], f("ln_mix_g")[0], f("ln_mix_b")[0])), "ho")
    nc, _ = build_ffn()
    h2 = _tok_gather(_run(nc, ffn_inputs(h1, p[0], f("ffn_w_up")[0], f("ffn_conv_w")[0], f("ffn_conv_b")[0],
                                         f("ffn_w_down")[0], f("ln_ffn_g")[0], f("ln_ffn_b")[0],
                                         f("ple_w_proj")[0], f("ple_w_gate")[0], f("ple_b_gate")[0])), "ho")
    nc, _ = build_mamba()
    res = _run(nc, mamba_inputs(h2, f("odd_w_in")[0], f("odd_conv_w")[0], f("odd_conv_b")[0], f("odd_dt_bias")[0],
                                f("odd_a_log")[0], f("odd_d_skip")[0], f("odd_norm_g")[0]))
    uT = np.empty((2, 2048, SEQ), np.float32)
    for c in range(NCORES):
        b, g = divmod(c, 4)
        uT[b, g * 512:(g + 1) * 512] = np.asarray(res[c]["u"]).T
    nc, _ = build_projln(16)
    h3 = _tok_gather(_run(nc, projln_inputs(uT, h2, f("odd_w_out")[0], f("ln_mix_g")[1], f("ln_mix_b")[1])), "ho")
    nc, _ = build_ffn()
    out = _tok_gather(_run(nc, ffn_inputs(h3, p[1], f("ffn_w_up")[1], f("ffn_conv_w")[1], f("ffn_conv_b")[1],
                                          f("ffn_w_down")[1], f("ln_ffn_g")[1], f("ln_ffn_b")[1],
                                          f("ple_w_proj")[1], f("ple_w_gate")[1], f("ple_b_gate")[1])), "ho")
    return np.ascontiguousarray(out.astype(np.float32))
```

```python
import numpy as np
import concourse.bass as bass
import concourse.mybir as mybir
from concourse.bass_utils import run_bass_kernel_spmd

F32 = mybir.dt.float32
BF16 = mybir.dt.bfloat16
ALU = mybir.AluOpType
AF = mybir.ActivationFunctionType
AX = mybir.AxisListType

NCORES = 8
D = 1024
SEQ = 8192
TOK = 2048
DFF = 2816
NJ = DFF // 128
ALPHA = 4.0 ** 0.25
LN_EPS = 1e-5

COMPUTE = ("pe", "act", "dve", "pool")
DMAQ = ("sp", "pq")
NDSEM = 12


class Op:
    __slots__ = ("eng", "fn", "reads", "writes", "deps", "inc", "cnt", "dsem", "duse", "barrier")

    def __init__(self, eng, fn, reads, writes):
        self.eng = eng; self.fn = fn; self.reads = reads; self.writes = writes
        self.deps = set(); self.inc = False; self.cnt = 0; self.dsem = None; self.duse = 0
        self.barrier = False


class Prog:
    def __init__(self, nc):
        self.nc = nc
        self.ops = []
        self._uid = 0

    def stream(self, e):
        return "pool" if e in ("pq", "cc") else e

    def op(self, eng, fn, reads=(), writes=()):
        o = Op(eng, fn, tuple(reads), tuple(writes))
        self.ops.append(o)
        return o

    def capture(self, fn):
        n0 = len(self.ops)
        fn()
        got = self.ops[n0:]
        del self.ops[n0:]
        return got

    def interleave(self, a, b):
        i = j = 0
        while i < len(a) or j < len(b):
            if j >= len(b) or (i < len(a) and i * len(b) <= j * len(a)):
                self.ops.append(a[i]); i += 1
            else:
                self.ops.append(b[j]); j += 1

    def interleave_n(self, lists):
        pos = [0] * len(lists)
        total = sum(len(l) for l in lists)
        for _ in range(total):
            best = None
            for i, l in enumerate(lists):
                if pos[i] < len(l):
                    frac = pos[i] / len(l)
                    if best is None or frac < best[0]:
                        best = (frac, i)
            i = best[1]
            self.ops.append(lists[i][pos[i]]); pos[i] += 1

    def dma(self, q, out, in_, reads=(), writes=(), **kw):
        return self.op(q, lambda e: e.dma_start(out=out, in_=in_, **kw), reads, writes)

    def barrier(self):
        for e in COMPUTE + ("sp",):
            o = self.op(e, None)
            o.barrier = True

    SBUF_TOP = 229376

    def sbuf(self, shape, dt, name=None):
        self._uid += 1
        nm = f"{name or 'sb'}_{self._uid}"
        if getattr(self, "arena_ptr", None) is None:
            return self.nc.alloc_sbuf_tensor(nm, list(shape), dt)
        esz = 4 if dt == F32 else 2
        n = 1
        for d in shape[1:]:
            n *= d
        nbytes = (n * esz + 63) // 64 * 64
        off = self.arena_ptr
        assert off + nbytes <= self.SBUF_TOP, f"SBUF arena overflow: {off}+{nbytes}"
        self.arena_ptr = off + nbytes
        return self.nc.alloc_sbuf_tensor_at(nm, list(shape), dt, offset=off)

    def phase(self):
        if getattr(self, "arena_base", None) is None:
            self.arena_base = (self.SBUF_TOP - self.nc.sbuf_bytes_remaining + 63) // 64 * 64
        else:
            self.barrier()
        self.arena_ptr = self.arena_base

    def psum(self, shape, dt=F32, name=None):
        self._uid += 1
        return self.nc.alloc_psum_tensor(f"{name or 'ps'}_{self._uid}", list(shape), dt)

    def finalize(self):
        nc = self.nc
        ops = self.ops
        last_w = {}
        readers = {}
        last_on = {}
        all_dma = []
        for i, o in enumerate(ops):
            st = self.stream(o.eng)
            isdma = (o.eng in DMAQ or o.eng == "cc") and not o.barrier
            if o.barrier:
                for s2, j in last_on.items():
                    if s2 != st and s2 != "__cc":
                        o.deps.add(j)
                for j in all_dma:
                    o.deps.add(j)
            else:
                raw = set()
                for k in o.reads:
                    j = last_w.get(k)
                    if j is not None:
                        raw.add(j)
                war = set()
                for k in o.writes:
                    j = last_w.get(k)
                    if j is not None:
                        raw.add(j)
                    for r in readers.get(k, ()):
                        war.add(r)
                for j in raw:
                    p = ops[j]
                    pst = self.stream(p.eng)
                    pdma = p.eng in DMAQ or p.eng == "cc"
                    if pst == st and not pdma and not isdma and st == "pe":
                        continue
                    o.deps.add(j)
                for j in war:
                    p = ops[j]
                    pst = self.stream(p.eng)
                    pdma = p.eng in DMAQ or p.eng == "cc"
                    if pst == st and not pdma and not isdma:
                        continue
                    o.deps.add(j)
                o.deps.discard(i)
                for k in o.reads:
                    readers.setdefault(k, []).append(i)
                for k in o.writes:
                    last_w[k] = i
                    readers[k] = []
            if o.eng == "cc":
                if last_on.get("__cc") is not None:
                    o.deps.add(last_on["__cc"])
                last_on["__cc"] = i
            if isdma:
                all_dma.append(i)
            elif not o.barrier:
                last_on[st] = i
        waited_idx = {}
        for i, o in enumerate(ops):
            st = self.stream(o.eng)
            for j in sorted(o.deps):
                p = ops[j]
                if p.eng in DMAQ or p.eng == "cc":
                    continue
                key = (st, p.eng)
                if waited_idx.get(key, -1) >= j:
                    continue
                waited_idx[key] = j
                p.inc = True
        cnt = {s: 0 for s in COMPUTE + ("cc",)}
        dcount = {q: 0 for q in DMAQ}
        for o in ops:
            if o.barrier:
                continue
            if o.eng in DMAQ:
                n = dcount[o.eng]; dcount[o.eng] += 1
                o.dsem = (o.eng, n % NDSEM); o.duse = n // NDSEM
            elif o.inc or o.eng == "cc":
                o.inc = True
                cnt[o.eng] += 1
                o.cnt = cnt[o.eng]
        sems = {s: nc.alloc_semaphore(f"s_{s}") for s in COMPUTE + ("cc",)}
        dsems = {(q, k): nc.alloc_semaphore(f"d_{q}{k}") for q in DMAQ for k in range(NDSEM)}
        streams = {s: [] for s in COMPUTE + ("sp",)}
        for i, o in enumerate(ops):
            streams[self.stream(o.eng)].append(i)
        self.stats = {s: len(v) for s, v in streams.items()}

        def emit(stream_name, eng):
            waited = {}
            for i in streams[stream_name]:
                o = ops[i]
                need = {}
                for j in o.deps:
                    p = ops[j]
                    if p.eng in DMAQ:
                        key = p.dsem; val = 16 * (p.duse + 1)
                    else:
                        key = p.eng; val = p.cnt
                    if need.get(key, 0) < val:
                        need[key] = val
                if o.eng in DMAQ and not o.barrier and o.duse > 0:
                    key = o.dsem; val = 16 * o.duse
                    if need.get(key, 0) < val:
                        need[key] = val
                for key, val in need.items():
                    if waited.get(key, 0) >= val:
                        continue
                    waited[key] = val
                    sem = dsems[key] if isinstance(key, tuple) else sems[key]
                    eng.wait_ge(sem, val)
                if o.barrier:
                    continue
                ins = o.fn(eng)
                if o.eng in DMAQ:
                    ins.then_inc(dsems[o.dsem], 16)
                elif o.inc:
                    ins.then_inc(sems[o.eng], 1)

        with nc.Block() as block:
            @block.tensor
            def _(e):
                emit("pe", e)

            @block.scalar
            def _(e):
                emit("act", e)

            @block.vector
            def _(e):
                emit("dve", e)

            @block.gpsimd
            def _(e):
                emit("pool", e)

            @block.sync
            def _(e):
                emit("sp", e)
                fin = {}
                for o in ops:
                    if o.eng in DMAQ and not o.barrier:
                        fin[o.dsem] = max(fin.get(o.dsem, 0), 16 * (o.duse + 1))
                for key, val in fin.items():
                    e.wait_ge(dsems[key], val)


def new_nc():
    return bass.Bass("TRN2", target_bir_lowering=False)


def din(nc, name, shape, dt=F32):
    return nc.dram_tensor(name, list(shape), dt, kind="ExternalInput").ap()


def dout(nc, name, shape, dt=F32):
    return nc.dram_tensor(name, list(shape), dt, kind="ExternalOutput").ap()


def make_ident(P, dt=F32):
    ident = P.sbuf([128, 128], dt, "ident")
    P.op("pool", lambda e: e.memset(ident[:], 0.0), writes=["ident"])
    P.op("pool", lambda e: e.affine_select(out=ident[:], in_=ident[:], pattern=[[-1, 128]],
                                           compare_op=ALU.not_equal, fill=1.0, base=0,
                                           channel_multiplier=1), reads=["ident"], writes=["ident"])
    return ident


class Banks:
    def __init__(self, P):
        self.t = [P.psum([128, 512], F32, f"bank{i}") for i in range(8)]
        self.key = [f"bank{i}" for i in range(8)]


def layer_norm_block(P, tag, r, g_bc, b_bc, out, stat, eng2="pool", n=128):
    rk = tag + "r"
    st6 = stat["st6"]; mv = stat["mv"]; rstd = stat["rstd"]
    for c in range(2):
        P.op("dve", lambda e, c=c: e.bn_stats(out=st6[0:n, c * 6:(c + 1) * 6], in_=r[0:n, c * 512:(c + 1) * 512]),
             reads=[rk], writes=[tag + "st6"])
    P.op("dve", lambda e: e.bn_aggr(out=mv[0:n, 0:2], in_=st6[0:n, 0:12]), reads=[tag + "st6"], writes=[tag + "mv"])
    P.op("dve", lambda e: e.tensor_scalar_add(out=rstd[0:n, 0:1], in0=mv[0:n, 1:2], scalar1=LN_EPS),
         reads=[tag + "mv"], writes=[tag + "rstd"])
    P.op("act", lambda e: e.activation(out=rstd[0:n, 0:1], in_=rstd[0:n, 0:1], func=AF.Sqrt),
         reads=[tag + "rstd"], writes=[tag + "rstd"])
    P.op("dve", lambda e: e.reciprocal(out=rstd[0:n, 0:1], in_=rstd[0:n, 0:1]), reads=[tag + "rstd"],
         writes=[tag + "rstd"])
    P.op("dve", lambda e: e.tensor_scalar(out=r[0:n, :], in0=r[0:n, :], scalar1=mv[0:n, 0:1], scalar2=rstd[0:n, 0:1],
                                          op0=ALU.subtract, op1=ALU.mult),
         reads=[rk, tag + "mv", tag + "rstd"], writes=[rk])
    P.op(eng2, lambda e: e.tensor_tensor(out=r[0:n, :], in0=r[0:n, :], in1=g_bc[0:n, :], op=ALU.mult),
         reads=[rk, "lnconst"], writes=[rk])
    P.op(eng2, lambda e: e.tensor_tensor(out=out[0:n, :], in0=r[0:n, :], in1=b_bc[0:n, :], op=ALU.add),
         reads=[rk, "lnconst"], writes=[tag + "lnout"] + ([rk] if out is r else []))


FFN_IN = {"wup": [NJ, 128, 8, 256], "cw": [128, NJ * 6], "cb": [128, NJ * 2], "wdn": [128, NJ, D],
          "lng": [128, D], "lnb": [128, D], "bg": [128, D], "wg": [128, 8, D], "wp": [128, 2, D],
          "pT": [128, 2, TOK]}


def build_ffn(TT=512, dbg_nt=None, dbg_nj=None, dbg_block=True):
    nc = new_nc(); P = Prog(nc)
    io = {k: din(nc, k, v) for k, v in FFN_IN.items()}
    io["hT"] = din(nc, "hT", [128, 8, TOK + 2]); io["htm"] = din(nc, "htm", [TOK, D])
    io["ho"] = dout(nc, "ho", [TOK, D])
    B = Banks(P)
    emit_ffn(P, B, io, TT, dbg_nt, dbg_nj, dbg_block)
    P.finalize()
    return nc, P


def emit_ffn(P, B, io, TT=512, dbg_nt=None, dbg_nj=None, dbg_block=True):
    NT = TOK // TT
    NB = TT // 128
    hT_d = io["hT"]; htm_d = io["htm"]; wup_d = io["wup"]; cw_d = io["cw"]; cb_d = io["cb"]; wdn_d = io["wdn"]
    lng_d = io["lng"]; lnb_d = io["lnb"]; bg_d = io["bg"]; wg_d = io["wg"]; wp_d = io["wp"]; pT_d = io["pT"]
    ho_d = io["ho"]
    ident = make_ident(P, F32)
    cw = P.sbuf([128, NJ * 6], F32); cb = P.sbuf([128, NJ * 2], F32)
    lng = P.sbuf([128, D], F32); lnb = P.sbuf([128, D], F32); bg = P.sbuf([128, D], F32)
    P.dma("sp", cw[:], cw_d[:, :], writes=["cw"]); P.dma("sp", cb[:], cb_d[:, :], writes=["cw"])
    P.dma("sp", lng[:], lng_d[:, :], writes=["lnconst"]); P.dma("sp", lnb[:], lnb_d[:, :], writes=["lnconst"])
    P.dma("sp", bg[:], bg_d[:, :], writes=["lnconst"])
    wdn = P.sbuf([128, NJ, D], BF16); wg = P.sbuf([128, 8, D], BF16); wp = P.sbuf([128, 2, D], BF16)
    hTs = [P.sbuf([128, 8, TT + 2], BF16) for _ in range(2)]
    wbuf = [P.sbuf([128, 8, 256], BF16) for _ in range(3)]
    ubuf = [[P.sbuf([128, TT + 2], F32) for _ in range(2)] for _ in range(2)]
    cbuf = [[P.sbuf([128, TT], F32) for _ in range(2)] for _ in range(2)]
    sg = [P.sbuf([128, TT], F32) for _ in range(2)]
    hdn = P.sbuf([128, NJ, TT], BF16)
    pTs = [P.sbuf([128, 2, 128], BF16) for _ in range(4)]
    hres = [P.sbuf([128, D], F32) for _ in range(2)]
    rb = [P.sbuf([128, D], F32) for _ in range(2)]
    yb = [P.sbuf([128, D], F32) for _ in range(2)]
    yT = [P.sbuf([128, 8, 128], BF16) for _ in range(2)]
    gsb = [P.sbuf([128, D], F32) for _ in range(2)]
    ob = [P.sbuf([128, D], F32) for _ in range(2)]
    stat = {"st6": P.sbuf([128, 12], F32), "mv": P.sbuf([128, 2], F32), "rstd": P.sbuf([128, 1], F32)}

    resident = ([(wdn[:, j, :], wdn_d[:, j, :], f"wdn{j}") for j in range(NJ)]
                + [(wg[:, k, :], wg_d[:, k, :], "wg") for k in range(8)] + [(wp[:, :, :], wp_d[:, :, :], "wp")])

    def load_resident(n):
        for _ in range(n):
            if resident:
                o, i, k = resident.pop(0)
                P.dma("pq", o, i, writes=[k])

    NTr = dbg_nt or NT
    NJr = dbg_nj or NJ
    iters = [(tt, j) for tt in range(NTr) for j in range(NJr)]
    LA = 2
    NWB = len(wbuf)

    def load_w(idx):
        j = iters[idx][1]
        P.dma("pq", wbuf[idx % NWB][:, :, :], wup_d[j, :, :, :], writes=[f"wbuf{idx % NWB}"])

    def load_hT(tt):
        P.dma("pq", hTs[tt % 2][:, :, :], hT_d[:, :, tt * TT: tt * TT + TT + 2], writes=[f"hTs{tt % 2}"])

    load_hT(0)
    for idx in range(min(LA, len(iters))):
        load_w(idx)
    bankrr = 0
    for tt in range(NTr):
        hp = tt % 2
        for j in range(NJr):
            idx = tt * NJr + j
            if idx + LA < len(iters):
                load_w(idx + LA)
            load_resident(2 if j < NJr - 1 else len(resident))
            if j == 8 and tt + 1 < NTr:
                load_hT(tt + 1)
            if j == 4:
                for bi in range(NB):
                    t0 = (tt * NB + bi) * 128
                    P.dma("pq", pTs[bi][:, :, :], pT_d[:, :, t0:t0 + 128], writes=[f"pTs{bi}"])
            wp_ = idx % NWB
            up = j % 2
            for gv in range(2):
                bh = bankrr % 4; bm = (bankrr + 1) % 4; bankrr += 2
                for (bk, c0, n) in ((bh, 0, 2), (bm, 2, TT)):
                    for k in range(8):
                        P.op("pe", lambda e, bk=bk, k=k, c0=c0, n=n, gv=gv, wp_=wp_, hp=hp: e.matmul(
                            B.t[bk][:, 0:n], lhsT=wbuf[wp_][:, k, gv * 128:(gv + 1) * 128],
                            rhs=hTs[hp][:, k, c0:c0 + n], start=(k == 0), stop=(k == 7)),
                            reads=[f"wbuf{wp_}", f"hTs{hp}"], writes=[B.key[bk]])
                    P.op("act", lambda e, bk=bk, c0=c0, n=n, gv=gv, up=up: e.copy(
                        out=ubuf[gv][up][:, c0:c0 + n], in_=B.t[bk][:, 0:n]),
                        reads=[B.key[bk]], writes=[f"ubuf{gv}{up}"])
            for gv in range(2):
                ci = (j * 2 + gv) * 3
                u = ubuf[gv][up]; c = cbuf[gv][up]
                P.op("act", lambda e, u=u, c=c, ci=ci, j=j, gv=gv: e.activation(
                    out=c[:, :], in_=u[:, 2:TT + 2], func=AF.Identity,
                    scale=cw[:, ci + 2:ci + 3], bias=cb[:, j * 2 + gv:j * 2 + gv + 1]),
                    reads=[f"ubuf{gv}{up}", "cw"], writes=[f"cbuf{gv}{up}"])
                for tap in (1, 0):
                    P.op("dve", lambda e, u=u, c=c, ci=ci, tap=tap: e.scalar_tensor_tensor(
                        out=c[:, :], in0=u[:, tap:tap + TT], scalar=cw[:, ci + tap:ci + tap + 1], in1=c[:, :],
                        op0=ALU.mult, op1=ALU.add),
                        reads=[f"ubuf{gv}{up}", "cw", f"cbuf{gv}{up}"], writes=[f"cbuf{gv}{up}"])
            P.op("act", lambda e, up=up: e.activation(out=sg[up][:, :], in_=cbuf[0][up][:, :], func=AF.Silu),
                 reads=[f"cbuf0{up}"], writes=[f"sg{up}"])
            P.op("dve", lambda e, up=up, j=j: e.tensor_tensor(out=hdn[:, j, :], in0=sg[up][:, :],
                                                             in1=cbuf[1][up][:, :], op=ALU.mult),
                 reads=[f"sg{up}", f"cbuf1{up}"], writes=[f"hdn{j}"])
        for bi in range(NB if dbg_block else 0):
            gb = tt * NB + bi
            par = gb % 2
            t0 = gb * 128
            if bi == 0:
                P.dma("sp", hres[par][:, :], htm_d[t0:t0 + 128, :], writes=[f"hres{par}"])
            if bi + 1 < NB:
                P.dma("sp", hres[1 - par][:, :], htm_d[t0 + 128:t0 + 256, :], writes=[f"hres{1 - par}"])
            for n in range(2):
                bk = 4 + n
                for j in range(NJ):
                    P.op("pe", lambda e, bk=bk, j=j, n=n, bi=bi: e.matmul(
                        B.t[bk][:, :], lhsT=hdn[:, j, bi * 128:(bi + 1) * 128], rhs=wdn[:, j, n * 512:(n + 1) * 512],
                        start=(j == 0), stop=(j == NJ - 1)),
                        reads=[f"hdn{j}", f"wdn{j}"], writes=[B.key[bk]])
                P.op("dve", lambda e, bk=bk, n=n, par=par: e.scalar_tensor_tensor(
                    out=rb[par][:, n * 512:(n + 1) * 512], in0=hres[par][:, n * 512:(n + 1) * 512], scalar=ALPHA,
                    in1=B.t[bk][:, :], op0=ALU.mult, op1=ALU.add),
                    reads=[f"hres{par}", B.key[bk]], writes=[f"b{par}r"])
            layer_norm_block(P, f"b{par}", rb[par], lng, lnb, yb[par], stat, eng2="dve")
            yk = f"b{par}lnout"
            for k in range(8):
                bk = 6 + k // 4
                P.op("pe", lambda e, bk=bk, k=k, par=par: e.transpose(
                    out=B.t[bk][:, (k % 4) * 128:(k % 4 + 1) * 128], in_=yb[par][:, k * 128:(k + 1) * 128],
                    identity=ident[:]), reads=[yk, "ident"], writes=[B.key[bk]])
            for hh in range(2):
                P.op("act", lambda e, hh=hh, par=par: e.copy(
                    out=yT[par][:, hh * 4:(hh + 1) * 4, :], in_=B.t[6 + hh][:, :].rearrange("p (k t) -> p k t", k=4)),
                    reads=[B.key[6 + hh]], writes=[f"yT{par}"])
            for n in range(2):
                for k in range(8):
                    P.op("pe", lambda e, n=n, k=k, par=par: e.matmul(
                        B.t[n][:, :], lhsT=yT[par][:, k, :], rhs=wg[:, k, n * 512:(n + 1) * 512],
                        start=(k == 0), stop=(k == 7)), reads=[f"yT{par}", "wg"], writes=[B.key[n]])
                for k in range(2):
                    P.op("pe", lambda e, n=n, k=k, bi=bi: e.matmul(
                        B.t[2 + n][:, :], lhsT=pTs[bi][:, k, :], rhs=wp[:, k, n * 512:(n + 1) * 512],
                        start=(k == 0), stop=(k == 1)), reads=[f"pTs{bi}", "wp"], writes=[B.key[2 + n]])
                P.op("dve", lambda e, n=n, par=par: e.tensor_tensor(
                    out=gsb[par][:, n * 512:(n + 1) * 512], in0=B.t[n][:, :], in1=bg[:, n * 512:(n + 1) * 512],
                    op=ALU.add), reads=[B.key[n], "lnconst"], writes=[f"gsb{par}"])
            P.op("act", lambda e, par=par: e.activation(out=gsb[par][:, :], in_=gsb[par][:, :], func=AF.Sigmoid),
                 reads=[f"gsb{par}"], writes=[f"gsb{par}"])
            for n in range(2):
                P.op("dve", lambda e, n=n, par=par: e.tensor_tensor(
                    out=gsb[par][:, n * 512:(n + 1) * 512], in0=gsb[par][:, n * 512:(n + 1) * 512],
                    in1=B.t[2 + n][:, :], op=ALU.mult), reads=[f"gsb{par}", B.key[2 + n]], writes=[f"gsb{par}"])
            P.op("dve", lambda e, par=par: e.tensor_tensor(out=ob[par][:, :], in0=gsb[par][:, :], in1=yb[par][:, :],
                                                            op=ALU.add), reads=[f"gsb{par}", yk], writes=[f"ob{par}"])
            P.dma("sp", ho_d[t0:t0 + 128, :], ob[par][:, :], reads=[f"ob{par}"])
            if "hsrc" in io:
                for k in range(8):
                    bk = 6 + k // 4
                    P.op("pe", lambda e, bk=bk, k=k, par=par: e.transpose(
                        out=B.t[bk][:, (k % 4) * 128:(k % 4 + 1) * 128], in_=ob[par][:, k * 128:(k + 1) * 128],
                        identity=ident[:]), reads=[f"ob{par}", "ident"], writes=[B.key[bk]])
                for hh in range(2):
                    P.op("act", lambda e, hh=hh, par=par: e.copy(
                        out=yT[par][:, hh * 4:(hh + 1) * 4, :],
                        in_=B.t[6 + hh][:, :].rearrange("p (k t) -> p k t", k=4)),
                        reads=[B.key[6 + hh]], writes=[f"yT{par}"])
                hk = io.setdefault("hkeys", [])
                hk.append(f"hsrc_{len(hk)}")
                P.dma("sp", io["hsrc"](t0), yT[par][:, :, :], reads=[f"yT{par}"], writes=[hk[-1]])
                if gb == TOK // 128 - 1:
                    hk.append(f"hsrc_{len(hk)}")
                    P.dma("sp", io["hh"][:, :], ob[par][126:128, :], reads=[f"ob{par}"], writes=[hk[-1]])
                io["on_block_done"](gb)


def _fm(a, halo=0):
    t, f = a.shape
    return np.ascontiguousarray(a.reshape(t, f // 128, 128).transpose(2, 1, 0))


def _wr(w):
    k, n = w.shape
    return np.ascontiguousarray(w.reshape(k // 128, 128, n).transpose(1, 0, 2))


def _bc(v):
    return np.ascontiguousarray(np.broadcast_to(v[None, :], (128, v.shape[0])))


def ffn_inputs(h, p_i, w_up, conv_w, conv_b, w_down, g, b, w_proj, w_gate, b_gate):
    wup = w_up.reshape(8, 128, 2, NJ, 128).transpose(3, 1, 0, 2, 4).reshape(NJ, 128, 8, 256)
    cw = conv_w.reshape(3, 2, NJ, 128).transpose(3, 2, 1, 0).reshape(128, NJ * 6)
    cb = conv_b.reshape(2, NJ, 128).transpose(2, 1, 0).reshape(128, NJ * 2)
    common = {"wup": np.ascontiguousarray(wup), "cw": np.ascontiguousarray(cw), "cb": np.ascontiguousarray(cb),
              "wdn": _wr(w_down), "lng": _bc(g), "lnb": _bc(b), "bg": _bc(b_gate),
              "wg": _wr(w_gate), "wp": _wr(w_proj)}
    maps = []
    for c in range(NCORES):
        bi, s = divmod(c, 4)
        t0 = s * TOK
        if h is not None:
            hpad = np.zeros((TOK + 2, D), np.float32)
            lo = max(t0 - 2, 0)
            hpad[2 - (t0 - lo):] = h[bi, lo:t0 + TOK]
        m = dict(common)
        if h is not None:
            m["hT"] = _fm(hpad)
            m["htm"] = np.ascontiguousarray(h[bi, t0:t0 + TOK])
        m["pT"] = _fm(np.ascontiguousarray(p_i[bi, t0:t0 + TOK]))
        maps.append(m)
    return maps


NEG = -30000.0


ATTN_IN = {"xT": [128, 8, SEQ], "wq": [128, 8, 128], "wk": [128, 8, 128], "wv": [128, 8, 128],
           "wf": [128, 8, 2], "bf": [2, 1], "wc": [128, 8, 384], "cwa": [128, 3]}


def build_attn(dbg_nqg=None):
    nc = new_nc(); P = Prog(nc)
    io = {k: din(nc, k, v) for k, v in ATTN_IN.items()}
    io["ybT"] = dout(nc, "ybT", [128, SEQ]); io["yaT"] = dout(nc, "yaT", [128, SEQ])
    B = Banks(P)
    emit_attn(P, B, io, dbg_nqg)
    P.finalize()
    return nc, P


def emit_attn(P, B, io, dbg_nqg=None):
    NTT = SEQ // 512
    xT_d = io["xT"]; wq_d = io["wq"]; wk_d = io["wk"]; wv_d = io["wv"]; wf_d = io["wf"]; bf_d = io["bf"]
    wc_d = io["wc"]; cwa_d = io["cwa"]
    if "ya_ap" in io:
        ya_ap = io["ya_ap"]; yb_ap = io["yb_ap"]; YDT = BF16
    else:
        yb_d = io["ybT"]; ya_d = io["yaT"]; YDT = ya_d.dtype
        ya_ap = lambda c0: ya_d[:, c0:c0 + 512]
        yb_ap = lambda c0: yb_d[:, c0:c0 + 512]
    ykeys = io.setdefault("ykeys", [])

    def ykey():
        ykeys.append(f"ysrc_{len(ykeys)}")
        return ykeys[-1]
    wc = P.sbuf([128, 8, 384], BF16, "wc_s"); cwa = P.sbuf([128, 3], F32, "cwa_s")
    P.dma("pq", wc[:, :, :], wc_d[:, :, :], writes=["w"]); P.dma("sp", cwa[:, :], cwa_d[:, :], writes=["w"])
    mbuf = [P.sbuf([128, 514], F32) for _ in range(2)]
    gcs = P.sbuf([128, 512], F32, "gcs"); cacc = P.sbuf([128, 512], F32, "cacc")
    yas = [P.sbuf([128, 512], YDT) for _ in range(2)]
    P.op("pool", lambda e: e.memset(mbuf[1][:, :], 0.0), writes=["mbuf1"])
    identb = P.sbuf([128, 128], BF16, "identb")
    maskT = P.sbuf([128, 128], BF16, "maskT")
    tmpi = P.sbuf([128, 128], F32, "tmpi")
    P.op("pool", lambda e: e.memset(tmpi[:], 0.0), writes=["tmpi"])
    P.op("pool", lambda e: e.affine_select(out=tmpi[:], in_=tmpi[:], pattern=[[-1, 128]], compare_op=ALU.not_equal,
                                           fill=1.0, base=0, channel_multiplier=1), reads=["tmpi"], writes=["tmpi"])
    P.op("pool", lambda e: e.tensor_copy(out=identb[:], in_=tmpi[:]), reads=["tmpi"], writes=["identb"])
    tmpm = P.sbuf([128, 128], F32, "tmpm")
    P.op("pool", lambda e: e.memset(tmpm[:], 0.0), writes=["tmpm"])
    P.op("pool", lambda e: e.affine_select(out=tmpm[:], in_=tmpm[:], pattern=[[1, 128]], compare_op=ALU.is_ge,
                                           fill=NEG, base=0, channel_multiplier=-1), reads=["tmpm"], writes=["tmpm"])
    P.op("pool", lambda e: e.tensor_copy(out=maskT[:], in_=tmpm[:]), reads=["tmpm"], writes=["maskT"])

    wq = P.sbuf([128, 8, 128], BF16); wk = P.sbuf([128, 8, 128], BF16); wv = P.sbuf([128, 8, 128], BF16)
    wf = P.sbuf([128, 8, 2], BF16); bfs = P.sbuf([2, 1], F32)
    P.dma("pq", wq[:, :, :], wq_d[:, :, :], writes=["w"]); P.dma("pq", wk[:, :, :], wk_d[:, :, :], writes=["w"])
    P.dma("pq", wv[:, :, :], wv_d[:, :, :], writes=["w"]); P.dma("pq", wf[:, :, :], wf_d[:, :, :], writes=["w"])
    P.dma("sp", bfs[:, :], bf_d[:, :], writes=["w"])
    Q = [P.sbuf([70, SEQ], BF16, f"Q{h}") for h in range(2)]
    Kt = [P.sbuf([70, SEQ], BF16, f"K{h}") for h in range(2)]
    Va = P.sbuf([128, SEQ // 128, 2, 128], BF16, "Va")
    P.op("pool", lambda e: e.memset(Va[:, :, :, 64:128], 1.0), writes=["Vones"])
    xs = [P.sbuf([128, 8, 512], BF16) for _ in range(2)]
    Fl = P.sbuf([2, SEQ], F32, "Fl")

    for tt in range(NTT):
        xp = tt % 2
        c0 = tt * 512
        if tt == 0:
            P.dma("pq", xs[0][:, :, :], xT_d[:, :, 0:512], writes=["xs0"])
        if tt + 1 < NTT:
            P.dma("pq", xs[1 - xp][:, :, :], xT_d[:, :, c0 + 512:c0 + 1024], writes=[f"xs{1 - xp}"])
        for (w, dst, nm, bk, scale) in ((wq, Q, "Q", 0, 0.125), (wk, Kt, "K", 1, 1.0)):
            for k in range(8):
                P.op("pe", lambda e, w=w, k=k, bk=bk, xp=xp: e.matmul(
                    B.t[bk][:, :], lhsT=w[:, k, :], rhs=xs[xp][:, k, :], start=(k == 0), stop=(k == 7)),
                    reads=["w", f"xs{xp}"], writes=[B.key[bk]])
            for h in range(2):
                eng = "act" if h == 0 else "dve"
                if eng == "act":
                    P.op("act", lambda e, dst=dst, h=h, bk=bk, c0=c0, scale=scale: e.activation(
                        out=dst[h][0:64, c0:c0 + 512], in_=B.t[bk][h * 64:(h + 1) * 64, :], func=AF.Copy, scale=scale),
                        reads=[B.key[bk]], writes=[f"{nm}{h}_{tt}"])
                else:
                    P.op("dve", lambda e, dst=dst, h=h, bk=bk, c0=c0, scale=scale: e.tensor_scalar_mul(
                        out=dst[h][0:64, c0:c0 + 512], in0=B.t[bk][h * 64:(h + 1) * 64, :], scalar1=scale),
                        reads=[B.key[bk]], writes=[f"{nm}{h}_{tt}"])
        for bi in range(4):
            for k in range(8):
                P.op("pe", lambda e, k=k, bi=bi, xp=xp: e.matmul(
                    B.t[2][:, bi * 128:(bi + 1) * 128], lhsT=xs[xp][:, k, bi * 128:(bi + 1) * 128], rhs=wv[:, k, :],
                    start=(k == 0), stop=(k == 7)), reads=["w", f"xs{xp}"], writes=[B.key[2]])
        P.op("dve", lambda e, tt=tt: e.tensor_copy(
            out=Va[:, tt * 4:(tt + 1) * 4, :, 0:64],
            in_=B.t[2][:, :].rearrange("p (b h d) -> p b h d", b=4, h=2)),
            reads=[B.key[2]], writes=[f"V{tt}"])
        for k in range(8):
            P.op("pe", lambda e, k=k, xp=xp: e.matmul(
                B.t[3][0:2, :], lhsT=wf[:, k, :], rhs=xs[xp][:, k, :], start=(k == 0), stop=(k == 7)),
                reads=["w", f"xs{xp}"], writes=[B.key[3]])
        P.op("act", lambda e, c0=c0: e.activation(out=Fl[:, c0:c0 + 512], in_=B.t[3][0:2, :], func=AF.Sigmoid,
                                                  bias=bfs[:, 0:1]), reads=[B.key[3], "w"], writes=["Fl"])
        for ci in range(3):
            for k in range(8):
                P.op("pe", lambda e, ci=ci, k=k, xp=xp: e.matmul(
                    B.t[4 + ci][:, :], lhsT=wc[:, k, ci * 128:(ci + 1) * 128], rhs=xs[xp][:, k, :],
                    start=(k == 0), stop=(k == 7)), reads=["w", f"xs{xp}"], writes=[B.key[4 + ci]])
        mp = tt % 2
        P.op("act", lambda e: e.copy(out=gcs[:, :], in_=B.t[5][:, :]), reads=[B.key[5]], writes=["gcs"])
        P.op("pool", lambda e, mp=mp: e.tensor_copy(out=mbuf[mp][:, 0:2], in_=mbuf[1 - mp][:, 512:514]),
             reads=[f"mbuf{1 - mp}"], writes=[f"mbuf{mp}h"])
        P.op("dve", lambda e, mp=mp: e.tensor_tensor(out=mbuf[mp][:, 2:514], in0=B.t[6][:, :], in1=gcs[:, :],
                                                     op=ALU.mult), reads=[B.key[6], "gcs"], writes=[f"mbuf{mp}"])
        P.op("act", lambda e, mp=mp: e.activation(out=cacc[:, :], in_=mbuf[mp][:, 2:514], func=AF.Copy,
                                                  scale=cwa[:, 2:3]), reads=[f"mbuf{mp}", "w"], writes=["cacc"])
        for tap in (1, 0):
            P.op("dve", lambda e, mp=mp, tap=tap: e.scalar_tensor_tensor(
                out=cacc[:, :], in0=mbuf[mp][:, tap:tap + 512], scalar=cwa[:, tap:tap + 1], in1=cacc[:, :],
                op0=ALU.mult, op1=ALU.add), reads=[f"mbuf{mp}", f"mbuf{mp}h", "w", "cacc"], writes=["cacc"])
        P.op("dve", lambda e, mp=mp: e.tensor_tensor(out=yas[mp][:, :], in0=B.t[4][:, :], in1=cacc[:, :], op=ALU.mult),
             reads=[B.key[4], "cacc"], writes=[f"yas{mp}"])
        P.dma("sp", ya_ap(c0), yas[mp][:, :], reads=[f"yas{mp}"], writes=[ykey()])
    P.op("act", lambda e: e.activation(out=Fl[:, :], in_=Fl[:, :], func=AF.Ln), reads=["Fl"], writes=["Fl"])
    ones2 = P.sbuf([2, 512], F32, "ones2")
    P.op("pool", lambda e: e.memset(ones2[:], 1.0), writes=["ones2"])
    carry = P.sbuf([2, 1], F32, "carry")
    P.op("pool", lambda e: e.memset(carry[:], 0.0), writes=["carry"])
    PC = 512
    augq = P.sbuf([2, 6, PC], BF16, "augq"); augk = P.sbuf([2, 6, PC], BF16, "augk")
    t1 = P.sbuf([2, PC], F32, "t1"); t2 = P.sbuf([2, PC], F32, "t2")
    P.op("pool", lambda e: e.memset(augq[:, 3:6, :], 1.0), writes=["augq"])
    P.op("pool", lambda e: e.memset(augk[:, 0:3, :], 1.0), writes=["augk"])
    for pc in range(SEQ // PC):
        c0 = pc * PC
        P.op("dve", lambda e, c0=c0: e.tensor_tensor_scan(
            out=Fl[:, c0:c0 + PC], data0=ones2[:, :], data1=Fl[:, c0:c0 + PC], initial=carry[:, 0:1],
            op0=ALU.mult, op1=ALU.add), reads=["Fl", "ones2", "carry"], writes=["Fl"])
        P.op("dve", lambda e, c0=c0: e.tensor_copy(out=carry[:, 0:1], in_=Fl[:, c0 + PC - 1:c0 + PC]),
             reads=["Fl"], writes=["carry"])
        src = Fl[:, c0:c0 + PC]
        for term in range(3):
            P.op("dve", lambda e, term=term, src=src: e.tensor_copy(out=augq[:, term, :], in_=src),
                 reads=["Fl", "t1", "t2"], writes=["augq"])
            P.op("dve", lambda e, term=term: e.tensor_scalar_mul(out=augk[:, 3 + term, :], in0=augq[:, term, :],
                                                                 scalar1=-1.0), reads=["augq"], writes=["augk"])
            if term < 2:
                dst = t1 if term == 0 else t2
                P.op("dve", lambda e, term=term, src=src, dst=dst: e.tensor_tensor(
                    out=dst[:, :], in0=src, in1=augq[:, term, :], op=ALU.subtract),
                    reads=["Fl", "t1", "t2", "augq"], writes=["t1", "t2"])
                src = dst[:, :]
        for h in range(2):
            P.dma("sp", Q[h][64:70, c0:c0 + PC], augq[h:h + 1, :, :], reads=["augq"], writes=[f"Qaug{h}"])
            P.dma("sp", Kt[h][64:70, c0:c0 + PC], augk[h:h + 1, :, :], reads=["augk"], writes=[f"Kaug{h}"])
    pT = [P.sbuf([128, 512], BF16) for _ in range(4)]
    rd = [P.sbuf([64, 512], F32, "rd") for _ in range(2)]
    yst = [P.sbuf([128, 512], YDT) for _ in range(2)]
    NQG = dbg_nqg or (SEQ // 512)
    tiles = []
    for qg in range(NQG):
        for h in range(2):
            nkb = 4 * qg + 4
            for kb in range(nkb):
                j = kb - 4 * qg
                qoff = 0 if j <= 0 else j * 128
                tiles.append(dict(qg=qg, h=h, kb=kb, nkb=nkb, qoff=qoff, n=512 - qoff, diag=(j >= 0),
                                  sb=len(tiles) % 4, q0=qg * 512))
    LA = 2

    def emit_qk(t):
        sb, h, kb, q0, qoff, n, diag, qg = t["sb"], t["h"], t["kb"], t["q0"], t["qoff"], t["n"], t["diag"], t["qg"]
        P.op("pe", lambda e: e.matmul(
            B.t[sb][:, 0:n], lhsT=Kt[h][0:70, kb * 128:(kb + 1) * 128], rhs=Q[h][0:70, q0 + qoff:q0 + 512],
            start=True, stop=(not diag)),
            reads=[f"K{h}_{kb // 4}", f"Kaug{h}", f"Q{h}_{qg}", f"Qaug{h}"], writes=[B.key[sb]])
        if diag:
            P.op("pe", lambda e: e.matmul(B.t[sb][:, 0:128], lhsT=identb[:, :], rhs=maskT[:, :],
                                          start=False, stop=True),
                 reads=["identb", "maskT"], writes=[B.key[sb]])

    for t in tiles[:LA]:
        emit_qk(t)
    for i, t in enumerate(tiles):
        sb, h, kb, qoff, n, nkb, qg, q0 = t["sb"], t["h"], t["kb"], t["qoff"], t["n"], t["nkb"], t["qg"], t["q0"]
        ob = 4 + h
        P.op("act", lambda e, sb=sb, n=n: e.activation(out=pT[sb][:, 0:n], in_=B.t[sb][:, 0:n], func=AF.Exp),
             reads=[B.key[sb]], writes=[f"pT{sb}"])
        if i + LA < len(tiles):
            emit_qk(tiles[i + LA])
        P.op("pe", lambda e, sb=sb, h=h, kb=kb, qoff=qoff, n=n, ob=ob, nkb=nkb: e.matmul(
            B.t[ob][:, qoff:512], lhsT=Va[:, kb, h, :], rhs=pT[sb][:, 0:n],
            start=(kb == 0), stop=(kb == nkb - 1)),
            reads=[f"V{kb // 4}", "Vones", f"pT{sb}"], writes=[B.key[ob]])
        if kb == nkb - 1:
            yp = qg % 2
            P.op("dve", lambda e, ob=ob, h=h: e.reciprocal(out=rd[h][0:64, :], in_=B.t[ob][64:128, :]),
                 reads=[B.key[ob]], writes=[f"rd{h}"])
            P.op("dve", lambda e, ob=ob, h=h, yp=yp: e.tensor_tensor(
                out=yst[yp][h * 64:(h + 1) * 64, :], in0=B.t[ob][0:64, :], in1=rd[h][0:64, :], op=ALU.mult),
                reads=[B.key[ob], f"rd{h}"], writes=[f"yst{yp}"])
            if h == 1:
                P.dma("sp", yb_ap(q0), yst[yp][:, :], reads=[f"yst{yp}"], writes=[ykey()])
                if "on_qg_done" in io:
                    io["on_qg_done"](qg)


def attn_inputs(x, w_in, b_f, conv_w):
    maps = []
    for c in range(NCORES):
        bi, hp = divmod(c, 4)
        m = {"xT": _fm(np.ascontiguousarray(x[bi])),
             "wq": _wr(np.ascontiguousarray(w_in[:, 1536 + hp * 128:1536 + (hp + 1) * 128])),
             "wk": _wr(np.ascontiguousarray(w_in[:, 2048 + hp * 128:2048 + (hp + 1) * 128])),
             "wv": _wr(np.ascontiguousarray(w_in[:, 2560 + hp * 128:2560 + (hp + 1) * 128])),
             "wf": _wr(np.ascontiguousarray(w_in[:, 3072 + hp * 2:3072 + hp * 2 + 2])),
             "bf": np.ascontiguousarray(b_f[hp * 2:hp * 2 + 2].reshape(2, 1)),
             "wc": _wr(np.ascontiguousarray(np.concatenate(
                 [w_in[:, i * 512 + hp * 128:i * 512 + (hp + 1) * 128] for i in range(3)], axis=1))),
             "cwa": np.ascontiguousarray(conv_w[:, hp * 128:(hp + 1) * 128].T)}
        maps.append(m)
    return maps


def build_projln(KC):
    nc = new_nc(); P = Prog(nc)
    yT_d = din(nc, "yT", [128, KC, TOK])
    htm_d = din(nc, "htm", [TOK, D])
    w_d = din(nc, "w", [128, KC, D])
    lng_d = din(nc, "lng", [128, D]); lnb_d = din(nc, "lnb", [128, D])
    ho_d = dout(nc, "ho", [TOK, D])
    B = Banks(P)
    lng = P.sbuf([128, D], F32); lnb = P.sbuf([128, D], F32)
    P.dma("sp", lng[:], lng_d[:, :], writes=["lnconst"]); P.dma("sp", lnb[:], lnb_d[:, :], writes=["lnconst"])
    w = P.sbuf([128, KC, D], BF16)
    yTs = P.sbuf([128, KC, TOK], BF16)
    for k in range(KC):
        P.dma("pq", w[:, k, :], w_d[:, k, :], writes=[f"w{k}"])
    for q in range(4):
        P.dma("pq", yTs[:, :, q * 512:(q + 1) * 512], yT_d[:, :, q * 512:(q + 1) * 512], writes=[f"yT{q}"])
    hres = [P.sbuf([128, D], F32) for _ in range(2)]
    rb = [P.sbuf([128, D], F32) for _ in range(2)]
    ob = [P.sbuf([128, D], F32) for _ in range(2)]
    stat = {"st6": P.sbuf([128, 12], F32), "mv": P.sbuf([128, 2], F32), "rstd": P.sbuf([128, 1], F32)}
    for gb in range(TOK // 128):
        par = gb % 2
        t0 = gb * 128
        P.dma("sp", hres[par][:, :], htm_d[t0:t0 + 128, :], writes=[f"hres{par}"])
        for n in range(2):
            bk = (gb % 2) * 2 + n
            for k in range(KC):
                P.op("pe", lambda e, bk=bk, k=k, n=n, t0=t0: e.matmul(
                    B.t[bk][:, :], lhsT=yTs[:, k, t0:t0 + 128], rhs=w[:, k, n * 512:(n + 1) * 512],
                    start=(k == 0), stop=(k == KC - 1)), reads=[f"yT{gb // 4}", f"w{k}"], writes=[B.key[bk]])
            P.op("dve", lambda e, bk=bk, n=n, par=par: e.scalar_tensor_tensor(
                out=rb[par][:, n * 512:(n + 1) * 512], in0=hres[par][:, n * 512:(n + 1) * 512], scalar=ALPHA,
                in1=B.t[bk][:, :], op0=ALU.mult, op1=ALU.add),
                reads=[f"hres{par}", B.key[bk]], writes=[f"b{par}r"])
        layer_norm_block(P, f"b{par}", rb[par], lng, lnb, ob[par], stat)
        P.dma("sp", ho_d[t0:t0 + 128, :], ob[par][:, :], reads=[f"b{par}lnout"])
    P.finalize()
    return nc, P


def projln_inputs(yT_full, h, w, g, b):
    K = yT_full.shape[1]
    common = {"w": _wr(w), "lng": _bc(g), "lnb": _bc(b)}
    maps = []
    for c in range(NCORES):
        bi, s = divmod(c, 4)
        t0 = s * TOK
        m = dict(common)
        m["yT"] = np.ascontiguousarray(yT_full[bi, :, t0:t0 + TOK].reshape(K // 128, 128, TOK).transpose(1, 0, 2))
        m["htm"] = np.ascontiguousarray(h[bi, t0:t0 + TOK])
        maps.append(m)
    return maps


RMS_EPS = 1e-5


MAMBA_IN = {"wx": [128, 8, 768], "wz": [128, 8, 512], "wdt": [128, 8, 8], "cwm": [128, 24], "cbm": [128, 6],
            "dtb": [128, 8], "alog": [128, 8], "dsk": [128, 8], "ng": [128, 512]}


def build_mamba(dbg_ntt=None):
    nc = new_nc(); P = Prog(nc)
    io = {k: din(nc, k, v) for k, v in MAMBA_IN.items()}
    hT_d = din(nc, "hT", [128, 8, SEQ])
    io["hT_tile"] = lambda tt: hT_d[:, :, tt * 512:(tt + 1) * 512]
    io["u"] = dout(nc, "u", [SEQ, 512])
    B = Banks(P)
    emit_mamba(P, B, io, dbg_ntt)
    P.finalize()
    return nc, P


MAMBA_DB = "xs_tm,Btm,dtt,dt,av,nacs,Ev,cdbs,acsFs,cbT,LTs,MT,xc,xcd,ybuf,xsk,sz,ug,junk,ssq"


def emit_mamba(P, B, io, dbg_ntt=None):
    NTT = dbg_ntt or (SEQ // 512)
    DB = set(MAMBA_DB.split(",")) if MAMBA_DB is not None else set()
    wx_d = io["wx"]; wz_d = io["wz"]; wdt_d = io["wdt"]; cwm_d = io["cwm"]; cbm_d = io["cbm"]
    dtb_d = io["dtb"]; alog_d = io["alog"]; dsk_d = io["dsk"]; ng_d = io["ng"]
    identF = make_ident(P, F32)
    identb = P.sbuf([128, 128], BF16, "identb")
    P.op("pool", lambda e: e.tensor_copy(out=identb[:], in_=identF[:]), reads=["ident"], writes=["identb"])
    maskT = P.sbuf([128, 128], BF16, "maskT"); tmpm = P.sbuf([128, 128], F32, "tmpm")
    P.op("pool", lambda e: e.memset(tmpm[:], 0.0), writes=["tmpm"])
    P.op("pool", lambda e: e.affine_select(out=tmpm[:], in_=tmpm[:], pattern=[[1, 128]], compare_op=ALU.is_ge,
                                           fill=NEG, base=0, channel_multiplier=-1), reads=["tmpm"], writes=["tmpm"])
    P.op("pool", lambda e: e.tensor_copy(out=maskT[:], in_=tmpm[:]), reads=["tmpm"], writes=["maskT"])
    Tm = P.sbuf([128, 128], F32, "Tm")
    P.op("pool", lambda e: e.memset(Tm[:], 1.0), writes=["Tm"])
    P.op("pool", lambda e: e.affine_select(out=Tm[:], in_=Tm[:], pattern=[[1, 128]], compare_op=ALU.is_ge,
                                           fill=0.0, base=0, channel_multiplier=-1), reads=["Tm"], writes=["Tm"])
    sel = P.sbuf([8, 8, 128], F32, "sel")
    P.op("pool", lambda e: e.memset(sel[:], 0.0), writes=["sel"])
    P.op("pool", lambda e: e.affine_select(out=sel[:], in_=sel[:], pattern=[[-1, 8], [0, 128]],
                                           compare_op=ALU.not_equal, fill=1.0, base=0, channel_multiplier=1),
         reads=["sel"], writes=["sel"])
    sel127 = P.sbuf([128, 128], F32, "sel127")
    P.op("pool", lambda e: e.memset(sel127[:], 0.0), writes=["sel127"])
    P.op("pool", lambda e: e.affine_select(out=sel127[:], in_=sel127[:], pattern=[[0, 128]],
                                           compare_op=ALU.not_equal, fill=1.0, base=-127, channel_multiplier=1),
         reads=["sel127"], writes=["sel127"])
    wx = P.sbuf([128, 8, 768], BF16); wz = P.sbuf([128, 8, 512], BF16); wdt = P.sbuf([128, 8, 8], BF16)
    P.dma("pq", wx[:, :, :], wx_d[:, :, :], writes=["w"]); P.dma("pq", wz[:, :, :], wz_d[:, :, :], writes=["w"])
    P.dma("pq", wdt[:, :, :], wdt_d[:, :, :], writes=["w"])
    cwm = P.sbuf([128, 24], F32); cbm = P.sbuf([128, 6], F32)
    dtb = P.sbuf([128, 8], F32); Abc = P.sbuf([128, 8], F32); dsk = P.sbuf([128, 8], F32); ng = P.sbuf([128, 512], F32)
    for (t, d_) in ((cwm, cwm_d), (cbm, cbm_d), (dtb, dtb_d), (Abc, alog_d), (dsk, dsk_d), (ng, ng_d)):
        P.dma("sp", t[:], d_[:, :], writes=["c"])
    P.op("act", lambda e: e.activation(out=Abc[:], in_=Abc[:], func=AF.Exp), reads=["c"], writes=["A"])
    P.op("dve", lambda e: e.tensor_scalar_mul(out=Abc[:], in0=Abc[:], scalar1=-1.0), reads=["A"], writes=["A"])
    hs = [P.sbuf([128, 8, 512], BF16) for _ in range(2)]
    ub = [[P.sbuf([128, 515], F32) for _ in range(2)] for _ in range(6)]
    for ci in range(6):
        P.op("pool", lambda e, ci=ci: e.memset(ub[ci][1][:, :], 0.0), writes=[f"ub{ci}1"])
    cacc2 = [P.sbuf([128, 512], F32) for _ in range(2)]
    xf = [P.sbuf([128, 512], F32) for _ in range(5)]
    BT = P.sbuf([128, 512], BF16); CT = [P.sbuf([128, 512], BF16) for _ in range(2)]
    xs_tm = [P.sbuf([128, 512], F32) for _ in range(2)]; Btm = [P.sbuf([128, 128], BF16) for _ in range(2)]
    dtt = [P.sbuf([128, 8], F32) for _ in range(2)]; dt = [P.sbuf([128, 8], F32) for _ in range(2)]; av = [P.sbuf([128, 8], F32) for _ in range(2)]
    nacs = [P.sbuf([128, 8], F32) for _ in range(2)]; Ev = [P.sbuf([128, 8], F32) for _ in range(2)]; cdbs = [P.sbuf([128, 8], F32) for _ in range(2)]
    acsFs = [P.sbuf([8, 128], F32) for _ in range(2)]; cbT = [P.sbuf([128, 128], F32) for _ in range(2)]
    LTs = [P.sbuf([128, 8, 128], F32) for _ in range(2)]; MT = [P.sbuf([128, 8, 128], BF16) for _ in range(2)]
    xc = [P.sbuf([128, 512], BF16) for _ in range(2)]; xcd = [P.sbuf([128, 512], BF16) for _ in range(2)]
    ybuf = [P.sbuf([128, 512], F32) for _ in range(2)]; xsk = [P.sbuf([128, 512], F32) for _ in range(2)]
    prev = P.sbuf([128, 512], F32); prevb = P.sbuf([128, 512], BF16)
    P.op("pool", lambda e: e.memset(prev[:], 0.0), writes=["prev"])
    P.op("pool", lambda e: e.memset(prevb[:], 0.0), writes=["prevb"])
    sz = [P.sbuf([128, 512], F32) for _ in range(2)]; ug = [P.sbuf([128, 512], F32) for _ in range(2)]; junk = [P.sbuf([128, 512], F32) for _ in range(2)]
    ssq = [P.sbuf([128, 1], F32) for _ in range(2)]
    uo = [P.sbuf([128, 512], F32) for _ in range(2)]
    uTb = [P.sbuf([128, 4, 128], BF16) for _ in range(2)]

    def bc8(t):
        return t[:, 0:8].unsqueeze(2).to_broadcast([128, 8, 64])

    def v3(t):
        return t[:, :].rearrange("p (h d) -> p h d", h=8)

    def early(tt, bi, hp, c0):
        cp = (tt * 4 + bi) % 2
        tp = tt % 2

        def kp(n):
            return cp if n in DB else 0
        blk = slice(bi * 128, (bi + 1) * 128)
        t0 = c0 + bi * 128
        for k in range(8):
            P.op("pe", lambda e, k=k, blk=blk, hp=hp: e.matmul(
                B.t[2][:, :], lhsT=hs[hp][:, k, blk], rhs=wz[:, k, :], start=(k == 0), stop=(k == 7)),
                reads=["w", f"hs{hp}"], writes=[B.key[2]])
        for k in range(8):
            P.op("pe", lambda e, k=k, blk=blk, hp=hp: e.matmul(
                B.t[0][:, 0:8], lhsT=hs[hp][:, k, blk], rhs=wdt[:, k, :], start=(k == 0), stop=(k == 7)),
                reads=["w", f"hs{hp}"], writes=[B.key[0]])
        for ci in range(4):
            P.op("pe", lambda e, ci=ci, blk=blk: e.transpose(
                out=B.t[3][:, ci * 128:(ci + 1) * 128], in_=xf[ci][:, blk], identity=identF[:]),
                reads=[f"xf{ci}", "ident"], writes=[B.key[3]])
        P.op("pe", lambda e, blk=blk: e.transpose(out=B.t[1][:, 0:128], in_=xf[4][:, blk], identity=identF[:]),
             reads=["xf4", "ident"], writes=[B.key[1]])
        P.op("act", lambda e: e.copy(out=xs_tm[kp("xs_tm")][:, :], in_=B.t[3][:, :]), reads=[B.key[3]], writes=[f"xs_tm{kp('xs_tm')}"])
        P.op("dve", lambda e: e.tensor_copy(out=Btm[kp("Btm")][:, :], in_=B.t[1][:, 0:128]), reads=[B.key[1]], writes=[f"Btm{kp('Btm')}"])
        P.op("dve", lambda e: e.tensor_tensor(out=dtt[kp("dtt")][:, :], in0=B.t[0][:, 0:8], in1=dtb[:, :], op=ALU.add),
             reads=[B.key[0], "c"], writes=[f"dtt{kp('dtt')}"])
        P.op("act", lambda e: e.activation(out=dtt[kp("dtt")][:, :], in_=dtt[kp("dtt")][:, :], func=AF.Exp),
             reads=[f"dtt{kp('dtt')}"], writes=[f"dtt{kp('dtt')}"])
        P.op("act", lambda e: e.activation(out=dt[kp("dt")][:, :], in_=dtt[kp("dtt")][:, :], func=AF.Ln, bias=1.0),
             reads=[f"dtt{kp('dtt')}"], writes=[f"dt{kp('dt')}"])
        P.op("dve", lambda e: e.tensor_tensor(out=av[kp("av")][:, :], in0=dt[kp("dt")][:, :], in1=Abc[:, :], op=ALU.mult),
             reads=[f"dt{kp('dt')}", "A"], writes=[f"av{kp('av')}"])
        P.op("pe", lambda e: e.matmul(B.t[0][:, 8:16], lhsT=Tm[:, :], rhs=av[kp("av")][:, :], start=True, stop=True),
             reads=["Tm", f"av{kp('av')}"], writes=[B.key[0]])
        P.op("pe", lambda e: e.matmul(B.t[0][0:8, 128:256], lhsT=av[kp("av")][:, :], rhs=Tm[:, :], start=True, stop=True),
             reads=["Tm", f"av{kp('av')}"], writes=[B.key[0]])
        P.op("dve", lambda e: e.tensor_scalar_mul(out=nacs[kp("nacs")][:, :], in0=B.t[0][:, 8:16], scalar1=-1.0),
             reads=[B.key[0]], writes=[f"nacs{kp('nacs')}"])
        P.op("act", lambda e: e.activation(out=Ev[kp("Ev")][:, :], in_=B.t[0][:, 8:16], func=AF.Exp),
             reads=[B.key[0]], writes=[f"Ev{kp('Ev')}"])
        P.op("act", lambda e: e.copy(out=acsFs[kp("acsFs")][:, :], in_=B.t[0][0:8, 128:256]), reads=[B.key[0]], writes=[f"acsFs{kp('acsFs')}"])
        P.op("pe", lambda e: e.matmul(B.t[0][:, 16:24], lhsT=sel127[:, :], rhs=Ev[kp("Ev")][:, :], start=True, stop=True),
             reads=["sel127", f"Ev{kp('Ev')}"], writes=[B.key[0]])
        P.op("dve", lambda e: e.tensor_copy(out=cdbs[kp("cdbs")][:, :], in_=B.t[0][:, 16:24]), reads=[B.key[0]], writes=[f"cdbs{kp('cdbs')}"])
        P.op("pe", lambda e, blk=blk: e.matmul(B.t[1][:, 128:256], lhsT=BT[:, blk], rhs=CT[tp][:, blk],
                                               start=True, stop=True), reads=["BT", f"CT{tp}"], writes=[B.key[1]])
        P.op("act", lambda e: e.copy(out=cbT[kp("cbT")][:, :], in_=B.t[1][:, 128:256]), reads=[B.key[1]], writes=[f"cbT{kp('cbT')}"])
        for h in range(8):
            bk = 4 + h // 4
            cs = slice((h % 4) * 128, (h % 4 + 1) * 128)
            P.op("pe", lambda e, bk=bk, cs=cs, h=h: e.matmul(B.t[bk][:, cs], lhsT=sel[0:8, h, :], rhs=acsFs[kp("acsFs")][:, :],
                                                             start=True, stop=False),
                 reads=["sel", f"acsFs{kp('acsFs')}"], writes=[B.key[bk]])
            P.op("pe", lambda e, bk=bk, cs=cs: e.matmul(B.t[bk][:, cs], lhsT=identb[:, :], rhs=maskT[:, :],
                                                        start=False, stop=True),
                 reads=["identb", "maskT"], writes=[B.key[bk]])
        for h in range(8):
            bk = 4 + h // 4
            cs = slice((h % 4) * 128, (h % 4 + 1) * 128)
            P.op("act", lambda e, bk=bk, cs=cs, h=h: e.activation(out=LTs[kp("LTs")][:, h, :], in_=B.t[bk][:, cs], func=AF.Exp,
                                                                  bias=nacs[kp("nacs")][:, h:h + 1]),
                 reads=[B.key[bk], f"nacs{kp('nacs')}"], writes=[f"LTs{kp('LTs')}"])
        P.op("dve", lambda e: e.tensor_tensor(out=MT[kp("MT")][:, :, :], in0=LTs[kp("LTs")][:, :, :],
                                              in1=cbT[kp("cbT")][:, :].unsqueeze(1).to_broadcast([128, 8, 128]), op=ALU.mult),
             reads=[f"LTs{kp('LTs')}", f"cbT{kp('cbT')}"], writes=[f"MT{kp('MT')}"])
        P.op("dve", lambda e: e.tensor_tensor(out=v3(xc[kp("xc")]), in0=v3(xs_tm[kp("xs_tm")]), in1=bc8(dt[kp("dt")]), op=ALU.mult),
             reads=[f"xs_tm{kp('xs_tm')}", f"dt{kp('dt')}"], writes=[f"xc{kp('xc')}"])
        P.op("dve", lambda e: e.tensor_tensor(out=v3(xcd[kp("xcd")]), in0=v3(xc[kp("xc")]),
                                              in1=LTs[kp("LTs")][:, :, 127:128].to_broadcast([128, 8, 64]), op=ALU.mult),
             reads=[f"xc{kp('xc')}", f"LTs{kp('LTs')}"], writes=[f"xcd{kp('xcd')}"])
        P.op("act", lambda e: e.activation(out=sz[kp("sz")][:, :], in_=B.t[2][:, :], func=AF.Silu),
             reads=[B.key[2]], writes=[f"sz{kp('sz')}"])

    def late(tt, bi, hp, c0):
        cp = (tt * 4 + bi) % 2
        tp = tt % 2

        def kp(n):
            return cp if n in DB else 0
        blk = slice(bi * 128, (bi + 1) * 128)
        t0 = c0 + bi * 128
        for h in range(8):
            P.op("pe", lambda e, h=h: e.matmul(B.t[6][:, h * 64:(h + 1) * 64], lhsT=MT[kp("MT")][:, h, :],
                                               rhs=xc[kp("xc")][:, h * 64:(h + 1) * 64], start=True, stop=True),
                 reads=[f"MT{kp('MT')}", f"xc{kp('xc')}"], writes=[B.key[6]])
        P.op("pe", lambda e, blk=blk: e.matmul(B.t[7][:, :], lhsT=CT[tp][:, blk], rhs=prevb[:, :], start=True, stop=True),
             reads=[f"CT{tp}", "prevb"], writes=[B.key[7]])
        P.op("dve", lambda e: e.tensor_tensor(out=v3(ybuf[kp("ybuf")]), in0=B.t[7][:, :].rearrange("p (h d) -> p h d", h=8),
                                              in1=bc8(Ev[kp("Ev")]), op=ALU.mult), reads=[B.key[7], f"Ev{kp('Ev')}"], writes=[f"ybuf{kp('ybuf')}"])
        P.op("pe", lambda e: e.matmul(B.t[7][:, :], lhsT=Btm[kp("Btm")][:, :], rhs=xcd[kp("xcd")][:, :], start=True, stop=True),
             reads=[f"Btm{kp('Btm')}", f"xcd{kp('xcd')}"], writes=[B.key[7]])
        P.op("dve", lambda e: e.tensor_tensor(out=ybuf[kp("ybuf")][:, :], in0=ybuf[kp("ybuf")][:, :], in1=B.t[6][:, :], op=ALU.add),
             reads=[B.key[6], f"ybuf{kp('ybuf')}"], writes=[f"ybuf{kp('ybuf')}"])
        P.op("dve", lambda e: e.tensor_tensor(out=v3(xsk[kp("xsk")]), in0=v3(xs_tm[kp("xs_tm")]), in1=bc8(dsk), op=ALU.mult),
             reads=[f"xs_tm{kp('xs_tm')}", "c"], writes=[f"xsk{kp('xsk')}"])
        P.op("dve", lambda e: e.tensor_tensor(out=ybuf[kp("ybuf")][:, :], in0=ybuf[kp("ybuf")][:, :], in1=xsk[kp("xsk")][:, :], op=ALU.add),
             reads=[f"ybuf{kp('ybuf')}", f"xsk{kp('xsk')}"], writes=[f"ybuf{kp('ybuf')}"])
        P.op("dve", lambda e: e.tensor_tensor(out=v3(prev), in0=v3(prev), in1=bc8(cdbs[kp("cdbs")]), op=ALU.mult),
             reads=["prev", f"cdbs{kp('cdbs')}"], writes=["prev"])
        P.op("dve", lambda e: e.tensor_tensor(out=prevb[:, :], in0=prev[:, :], in1=B.t[7][:, :], op=ALU.add),
             reads=["prev", B.key[7]], writes=["prevb"])
        P.op("dve", lambda e: e.tensor_tensor(out=prev[:, :], in0=prev[:, :], in1=B.t[7][:, :], op=ALU.add),
             reads=["prev", B.key[7]], writes=["prev"])
        P.op("dve", lambda e: e.tensor_tensor(out=ug[kp("ug")][:, :], in0=ybuf[kp("ybuf")][:, :], in1=sz[kp("sz")][:, :], op=ALU.mult),
             reads=[f"ybuf{kp('ybuf')}", f"sz{kp('sz')}"], writes=[f"ug{kp('ug')}"])
        P.op("act", lambda e: e.activation(out=junk[kp("junk")][:, :], in_=ug[kp("ug")][:, :], func=AF.Square, accum_out=ssq[kp("ssq")][:, 0:1]),
             reads=[f"ug{kp('ug')}"], writes=[f"ssq{kp('ssq')}", f"junk{kp('junk')}"])
        P.op("dve", lambda e: e.tensor_scalar(out=ssq[kp("ssq")][:, 0:1], in0=ssq[kp("ssq")][:, 0:1], scalar1=1.0 / 512, scalar2=RMS_EPS,
                                              op0=ALU.mult, op1=ALU.add), reads=[f"ssq{kp('ssq')}"], writes=[f"ssq{kp('ssq')}"])
        P.op("act", lambda e: e.activation(out=ssq[kp("ssq")][:, 0:1], in_=ssq[kp("ssq")][:, 0:1], func=AF.Ln),
             reads=[f"ssq{kp('ssq')}"], writes=[f"ssq{kp('ssq')}"])
        P.op("act", lambda e: e.activation(out=ssq[kp("ssq")][:, 0:1], in_=ssq[kp("ssq")][:, 0:1], func=AF.Exp, scale=-0.5),
             reads=[f"ssq{kp('ssq')}"], writes=[f"ssq{kp('ssq')}"])
        up = (tt * 4 + bi) % 2
        P.op("dve", lambda e, up=up: e.scalar_tensor_tensor(out=uo[up][:, :], in0=ug[kp("ug")][:, :], scalar=ssq[kp("ssq")][:, 0:1],
                                                            in1=ng[:, :], op0=ALU.mult, op1=ALU.mult),
             reads=[f"ug{kp('ug')}", f"ssq{kp('ssq')}", "c"], writes=[f"uo{up}"])
        if "u" in io:
            P.dma("sp", io["u"][t0:t0 + 128, :], uo[up][:, :], reads=[f"uo{up}"])
        else:
            for ci in range(4):
                P.op("pe", lambda e, ci=ci, up=up: e.transpose(
                    out=B.t[6][:, ci * 128:(ci + 1) * 128], in_=uo[up][:, ci * 128:(ci + 1) * 128],
                    identity=identF[:]), reads=[f"uo{up}", "ident"], writes=[B.key[6]])
            P.op("act", lambda e, up=up: e.copy(out=uTb[up][:, :, :],
                                                in_=B.t[6][:, :].rearrange("p (c t) -> p c t", c=4)),
                 reads=[B.key[6]], writes=[f"uTb{up}"])
            uk = io.setdefault("ukeys", [])
            for cpi in range(2):
                uk.append(f"usrc_{len(uk)}")
                P.dma("sp", io["usrc"](cpi, t0), uTb[up][:, 2 * cpi:2 * cpi + 2, :], reads=[f"uTb{up}"],
                      writes=[uk[-1]])
            io["on_block_done"](tt * 4 + bi)

    def feature(tt):
        hp = tt % 2
        c0 = tt * 512
        if tt == 0:
            P.dma("pq", hs[0][:, :, :], io["hT_tile"](0), writes=["hs0"])
        if tt + 1 < NTT:
            P.dma("pq", hs[1 - hp][:, :, :], io["hT_tile"](tt + 1), writes=[f"hs{1 - hp}"])
        for ci in range(6):
            bk = ci % 2
            for k in range(8):
                P.op("pe", lambda e, bk=bk, k=k, ci=ci, hp=hp: e.matmul(
                    B.t[bk][:, :], lhsT=wx[:, k, ci * 128:(ci + 1) * 128], rhs=hs[hp][:, k, :],
                    start=(k == 0), stop=(k == 7)), reads=["w", f"hs{hp}"], writes=[B.key[bk]])
            u = ub[ci][hp]; uprev = ub[ci][1 - hp]
            P.op("act", lambda e, u=u, bk=bk: e.copy(out=u[:, 3:515], in_=B.t[bk][:, :]),
                 reads=[B.key[bk]], writes=[f"ub{ci}{hp}"])
            P.op("pool", lambda e, u=u, uprev=uprev: e.tensor_copy(out=u[:, 0:3], in_=uprev[:, 512:515]),
                 reads=[f"ub{ci}{1 - hp}"], writes=[f"ub{ci}{hp}h"])
            P.op("act", lambda e, u=u, ci=ci, cacc=cacc2[ci % 2]: e.activation(out=cacc[:, :], in_=u[:, 3:515], func=AF.Identity,
                                                           scale=cwm[:, ci * 4 + 3:ci * 4 + 4], bias=cbm[:, ci:ci + 1]),
                 reads=[f"ub{ci}{hp}", "c"], writes=[f"cacc{ci % 2}"])
            for tap in range(3):
                P.op("dve", lambda e, u=u, ci=ci, tap=tap, cacc=cacc2[ci % 2]: e.scalar_tensor_tensor(
                    out=cacc[:, :], in0=u[:, tap:tap + 512], scalar=cwm[:, ci * 4 + tap:ci * 4 + tap + 1],
                    in1=cacc[:, :], op0=ALU.mult, op1=ALU.add),
                    reads=[f"ub{ci}{hp}", f"ub{ci}{hp}h", "c", f"cacc{ci % 2}"], writes=[f"cacc{ci % 2}"])
            if ci < 5:
                P.op("act", lambda e, ci=ci, cacc=cacc2[ci % 2]: e.activation(out=xf[ci][:, :], in_=cacc[:, :], func=AF.Silu),
                     reads=[f"cacc{ci % 2}"], writes=[f"xf{ci}"])
                if ci == 4:
                    P.op("dve", lambda e: e.tensor_copy(out=BT[:, :], in_=xf[4][:, :]), reads=["xf4"], writes=["BT"])
            else:
                P.op("act", lambda e, hp=hp, cacc=cacc2[ci % 2]: e.activation(out=CT[hp][:, :], in_=cacc[:, :], func=AF.Silu),
                     reads=[f"cacc{ci % 2}"], writes=[f"CT{hp}"])

    chunks = [(tt, bi) for tt in range(NTT) for bi in range(4)]
    for idx in range(len(chunks) + 1):
        def do_early(idx=idx):
            tt, bi = chunks[idx]
            if bi == 0:
                feature(tt)
            early(tt, bi, tt % 2, tt * 512)

        def do_late(idx=idx):
            tt, bi = chunks[idx - 1]
            late(tt, bi, tt % 2, tt * 512)
        ea = P.capture(do_early) if idx < len(chunks) else []
        la = P.capture(do_late) if idx >= 1 else []
        P.interleave(ea, la)


def mamba_inputs(h, w_in, conv_w, conv_b, dt_bias, a_log, d_skip, norm_g):
    maps = []
    for c in range(NCORES):
        bi, g = divmod(c, 4)
        xcols = np.concatenate([np.arange(2048 + g * 512, 2048 + (g + 1) * 512),
                                np.arange(4096 + g * 128, 4096 + (g + 1) * 128),
                                np.arange(4608 + g * 128, 4608 + (g + 1) * 128)])
        ch = xcols - 2048
        cwm = conv_w[:, ch].reshape(4, 6, 128).transpose(2, 1, 0).reshape(128, 24)
        cbm = conv_b[ch].reshape(6, 128).T
        hs = slice(g * 8, (g + 1) * 8)
        m = {"wx": _wr(np.ascontiguousarray(w_in[:, xcols])),
             "wz": _wr(np.ascontiguousarray(w_in[:, g * 512:(g + 1) * 512])),
             "wdt": _wr(np.ascontiguousarray(w_in[:, 5120 + g * 8:5120 + (g + 1) * 8])),
             "cwm": np.ascontiguousarray(cwm), "cbm": np.ascontiguousarray(cbm),
             "dtb": _bc(dt_bias[hs]), "alog": _bc(a_log[hs]), "dsk": _bc(d_skip[hs]),
             "ng": _bc(norm_g[g * 512:(g + 1) * 512])}
        if h is not None:
            m["hT"] = _fm(np.ascontiguousarray(h[bi]))
        maps.append(m)
    return maps


PADC = 128
GROUPS = [[0, 1, 2, 3], [4, 5, 6, 7]]


def emit_projln_f(P, B, io, KC, load_yT, halo_res, main_res):
    ident = make_ident(P, F32)
    lng = P.sbuf([128, D], F32); lnb = P.sbuf([128, D], F32); flag = P.sbuf([128, 1], F32)
    P.dma("sp", lng[:], io["lng"][:, :], writes=["lnconst"]); P.dma("sp", lnb[:], io["lnb"][:, :], writes=["lnconst"])
    P.dma("sp", flag[:], io["flag"][:, :], writes=["lnconst"])
    w = P.sbuf([128, KC, D], BF16)
    for k in range(KC):
        P.dma("pq", w[:, k, :], io["w"][:, k, :], writes=["w"])
    yTs = P.sbuf([128, KC, TOK + 2], BF16)
    load_yT(P, yTs)
    NW = 4
    hres = [P.sbuf([128, D], F32) for _ in range(NW)]
    rb = [P.sbuf([128, D], F32) for _ in range(NW)]
    ob = rb
    hTb = [P.sbuf([128, 8, 128], BF16) for _ in range(NW)]
    stat = [{"st6": P.sbuf([128, 12], F32), "mv": P.sbuf([128, 2], F32), "rstd": P.sbuf([128, 1], F32)}
            for _ in range(NW)]
    blocks = [(0, 2)] + [(2 + i * 128, 128) for i in range(TOK // 128)]
    def do_block(bi_):
        c0, nt = blocks[bi_]
        par = bi_ % NW
        tb = 2 * par
        if bi_ == 0:
            P.op("sp", lambda e, par=par: e.dma_start(out=hres[par][0:2, :], in_=halo_res(e)), writes=[f"hres{par}"])
        else:
            P.dma("sp", hres[par][:, :], main_res[c0 - 2:c0 - 2 + 128, :], writes=[f"hres{par}"])
        for n in range(2):
            bk = 2 * par + n
            for k in range(KC):
                P.op("pe", lambda e, bk=bk, k=k, n=n, c0=c0, nt=nt: e.matmul(
                    B.t[bk][0:nt, :], lhsT=yTs[:, k, c0:c0 + nt], rhs=w[:, k, n * 512:(n + 1) * 512],
                    start=(k == 0), stop=(k == KC - 1)), reads=["yTs", "w"], writes=[B.key[bk]])
            P.op("dve", lambda e, bk=bk, n=n, par=par, nt=nt: e.scalar_tensor_tensor(
                out=rb[par][0:nt, n * 512:(n + 1) * 512], in0=hres[par][0:nt, n * 512:(n + 1) * 512], scalar=ALPHA,
                in1=B.t[bk][0:nt, :], op0=ALU.mult, op1=ALU.add),
                reads=[f"hres{par}", B.key[bk]], writes=[f"b{par}r"])
        layer_norm_block(P, f"b{par}", rb[par], lng, lnb, ob[par], stat[par], eng2="dve", n=nt)
        yk = f"b{par}r"
        if bi_ > 0:
            P.dma("sp", io["h_d"][c0 - 2:c0 - 2 + 128, :], ob[par][:, :], reads=[yk])
        for k in range(8):
            bk = tb + k // 4
            P.op("pe", lambda e, bk=bk, k=k, par=par, nt=nt: e.transpose(
                out=B.t[bk][:, (k % 4) * 128:(k % 4) * 128 + nt], in_=ob[par][0:nt, k * 128:(k + 1) * 128],
                identity=ident[0:nt, 0:nt]), reads=[yk, "ident"], writes=[B.key[bk]])
        for hh in range(2):
            P.op("act", lambda e, hh=hh, par=par, nt=nt, tb=tb: e.copy(
                out=hTb[par][:, hh * 4:(hh + 1) * 4, 0:nt],
                in_=B.t[tb + hh][:, :].rearrange("p (k t) -> p k t", k=4)[:, :, 0:nt]),
                reads=[B.key[tb + hh]], writes=[f"hTb{par}"])
        if bi_ == 0:
            P.op("dve", lambda e, par=par: e.tensor_scalar_mul(out=hTb[par][:, :, 0:2], in0=hTb[par][:, :, 0:2],
                                                               scalar1=flag[:, 0:1]),
                 reads=[f"hTb{par}", "lnconst"], writes=[f"hTb{par}"])
        P.dma("sp", io["hT_d"][:, :, c0:c0 + nt], hTb[par][:, :, 0:nt], reads=[f"hTb{par}"])

    for b0 in range(0, len(blocks), NW):
        P.interleave_n([P.capture(lambda i=i: do_block(i)) for i in range(b0, min(b0 + NW, len(blocks)))])


def build_fused(dbg=False):
    nc = new_nc(); P = Prog(nc); B = Banks(P)
    A = {k: din(nc, "a_" + k, v) for k, v in ATTN_IN.items()}
    xtm = din(nc, "xtm", [TOK + 2, D]); flag = din(nc, "flag", [128, 1])
    wo0 = din(nc, "wo0", [128, 8, D]); wo1 = din(nc, "wo1", [128, 16, D])
    lnm = [(din(nc, f"lnmg{l}", [128, D]), din(nc, f"lnmb{l}", [128, D])) for l in range(2)]
    F = [{k: din(nc, f"f{l}_" + k, v) for k, v in FFN_IN.items()} for l in range(2)]
    M = {k: din(nc, "m_" + k, v) for k, v in MAMBA_IN.items()}
    out_d = dout(nc, "out", [TOK, D])

    def scratch(name, shape, dt):
        return nc.dram_tensor(name, list(shape), dt).ap()
    ysrc = scratch("ysrc", [4 * 256, 2048], BF16)
    ygath = scratch("ygath", [4 * 1024, 2048], BF16)
    hsrc = scratch("hsrc", [4 * 128, 8 * 512], BF16)
    hgath = scratch("hgath", [4 * 512, 8 * 512], BF16)
    hh_src = scratch("hh_src", [2, D], F32); hh_g = scratch("hh_g", [8, D], F32)
    usrc = scratch("usrc", [8 * 256, 2048], BF16)
    ugath = scratch("ugath", [8 * 1024, 2048], BF16)
    h1_d = scratch("h1_d", [TOK, D], F32); hT1_d = scratch("hT1_d", [128, 8, TOK + 2], BF16)
    h2_d = scratch("h2_d", [TOK, D], F32)
    h3_d = scratch("h3_d", [TOK, D], F32); hT3_d = scratch("hT3_d", [128, 8, TOK + 2], BF16)

    def allgather(src, dst, rkeys, wk):
        P.op("cc", lambda e: e.collective_compute("AllGather", ALU.bypass, replica_groups=GROUPS,
                                                   ins=[src], outs=[dst]), reads=list(rkeys), writes=[wk])

    _dyn = {}

    def dyn(e, tag):
        if tag not in _dyn:
            rank = e.partition_id() % 4
            prev = (rank + 3) % 4
            _dyn[tag] = {"r1024": e.snap(rank * 1024), "p1024": e.snap(prev * 1024), "p2": e.snap(prev * 2)}
        return _dyn[tag]

    P.phase()
    ioA = dict(A)
    ioA["ya_ap"] = lambda c0: ysrc[(c0 // 2048) * 256:(c0 // 2048) * 256 + 128, c0 % 2048:c0 % 2048 + 512]
    ioA["yb_ap"] = lambda c0: ysrc[(c0 // 2048) * 256 + 128:(c0 // 2048) * 256 + 256, c0 % 2048:c0 % 2048 + 512]

    def a_hook(qg):
        if qg % 4 == 3:
            q = qg // 4
            allgather(ysrc[q * 256:(q + 1) * 256, :], ygath[q * 1024:(q + 1) * 1024, :], ioA["ykeys"], f"ygath{q}")
    ioA["on_qg_done"] = a_hook
    emit_attn(P, B, ioA)
    P.phase()

    def load_y(P_, yTs):
        dst = yTs[:, :, :].rearrange("p (h r) t -> p h r t", h=2)
        for half in range(2):
            P_.op("sp", lambda e, half=half: e.dma_start(
                out=dst[:, half, :, 2:TOK + 2],
                in_=ygath[bass.ds(dyn(e, "sp")["r1024"], 1024), :].rearrange(
                    "(r h p) t -> p h r t", r=4, h=2, p=128)[:, half, :, :]), writes=["yTs"])
            P_.op("sp", lambda e, half=half: e.dma_start(
                out=dst[:, half, :, 0:2],
                in_=ygath[bass.ds(dyn(e, "sp")["p1024"], 1024), :].rearrange(
                    "(r h p) t -> p h r t", r=4, h=2, p=128)[:, half, :, 2046:2048]), writes=["yTs"])
    emit_projln_f(P, B, {"w": wo0, "lng": lnm[0][0], "lnb": lnm[0][1], "flag": flag, "h_d": h1_d, "hT_d": hT1_d},
                  8, load_y, lambda e: xtm[0:2, :], xtm[2:TOK + 2, :])
    P.phase()
    ioC = dict(F[0]); ioC.update({"hT": hT1_d, "htm": h1_d, "ho": h2_d, "hh": hh_src})
    ioC["hsrc"] = lambda t0: hsrc[(t0 // 512) * 128:(t0 // 512 + 1) * 128, :].rearrange(
        "p (k t) -> p k t", k=8)[:, :, t0 % 512:t0 % 512 + 128]

    def c_hook(gb):
        if gb % 4 == 3:
            q = gb // 4
            allgather(hsrc[q * 128:(q + 1) * 128, :], hgath[q * 512:(q + 1) * 512, :], ioC["hkeys"], f"hgath{q}")
        if gb == TOK // 128 - 1:
            allgather(hh_src[:, :], hh_g[:, :], ioC["hkeys"], "hhg")
    ioC["on_block_done"] = c_hook
    emit_ffn(P, B, ioC)
    P.phase()
    ioD = dict(M)
    ioD["hT_tile"] = lambda tt: hgath[(tt % 4) * 512 + (tt // 4) * 128:(tt % 4) * 512 + (tt // 4 + 1) * 128, :].rearrange(
        "p (k t) -> p k t", k=8)
    ioD["usrc"] = lambda cp, t0: usrc[(cp * 4 + t0 // 2048) * 256:(cp * 4 + t0 // 2048 + 1) * 256, :].rearrange(
        "(c p) t -> p c t", p=128)[:, :, t0 % 2048:t0 % 2048 + 128]

    def d_hook(blk):
        if blk % 16 == 15:
            q = blk // 16
            for cp in range(2):
                i = cp * 4 + q
                allgather(usrc[i * 256:(i + 1) * 256, :], ugath[i * 1024:(i + 1) * 1024, :], ioD["ukeys"], f"ugath{i}")
    ioD["on_block_done"] = d_hook
    emit_mamba(P, B, ioD)
    P.phase()

    def load_u(P_, yTs):
        for cp in range(2):
            P_.op("pq", lambda e, cp=cp: e.dma_start(
                out=yTs[:, cp * 8:(cp + 1) * 8, 2:TOK + 2],
                in_=ugath[bass.ds(dyn(e, "pq")["r1024"] + cp * 4096, 1024), :].rearrange(
                    "(k p) t -> p k t", p=128)), writes=["yTs"])
            P_.op("pq", lambda e, cp=cp: e.dma_start(
                out=yTs[:, cp * 8:(cp + 1) * 8, 0:2],
                in_=ugath[bass.ds(dyn(e, "pq")["p1024"] + cp * 4096, 1024), :].rearrange(
                    "(k p) t -> p k t", p=128)[:, :, 2046:2048]), writes=["yTs"])
    emit_projln_f(P, B, {"w": wo1, "lng": lnm[1][0], "lnb": lnm[1][1], "flag": flag, "h_d": h3_d, "hT_d": hT3_d},
                  16, load_u, lambda e: hh_g[bass.ds(dyn(e, "sp")["p2"], 2), :], h2_d)
    P.phase()
    ioF = dict(F[1]); ioF.update({"hT": hT3_d, "htm": h3_d, "ho": out_d})
    emit_ffn(P, B, ioF)
    if dbg:
        P.barrier()
        for nm, src in (("dbg_h1", h1_d), ("dbg_h2", h2_d), ("dbg_h3", h3_d)):
            o = dout(nc, nm, [TOK, D])
            for r0 in range(0, TOK, 256):
                P.dma("sp", o[r0:r0 + 256, :], src[r0:r0 + 256, :])
    P.finalize()
    return nc, P


def fused_inputs(inp):
    f = lambda k: np.ascontiguousarray(np.asarray(inp[k], dtype=np.float32))
    x = f("x"); p = f("p")
    am = attn_inputs(x, f("even_w_in")[0], f("even_b_f")[0], f("even_conv_w")[0])
    mm = mamba_inputs(None, f("odd_w_in")[0], f("odd_conv_w")[0], f("odd_conv_b")[0], f("odd_dt_bias")[0],
                      f("odd_a_log")[0], f("odd_d_skip")[0], f("odd_norm_g")[0])
    fm = [ffn_inputs(None, p[l], f("ffn_w_up")[l], f("ffn_conv_w")[l], f("ffn_conv_b")[l], f("ffn_w_down")[l],
                     f("ln_ffn_g")[l], f("ln_ffn_b")[l], f("ple_w_proj")[l], f("ple_w_gate")[l], f("ple_b_gate")[l])
          for l in range(2)]
    wo1 = _wr(f("odd_w_out")[0])
    perm = [r * 4 + cp * 2 + c for cp in range(2) for r in range(4) for c in range(2)]
    common = {"wo0": _wr(f("even_w_out")[0]), "wo1": np.ascontiguousarray(wo1[:, perm, :])}
    for l in range(2):
        common[f"lnmg{l}"] = _bc(f("ln_mix_g")[l]); common[f"lnmb{l}"] = _bc(f("ln_mix_b")[l])
    maps = []
    for c in range(NCORES):
        b, s = divmod(c, 4)
        t0 = s * TOK
        m = dict(common)
        for k in ATTN_IN:
            m["a_" + k] = am[c][k]
        for k in MAMBA_IN:
            m["m_" + k] = mm[c][k]
        for l in range(2):
            for k in FFN_IN:
                m[f"f{l}_" + k] = fm[l][c][k]
        xpad = np.zeros((TOK + 2, D), np.float32)
        lo = max(t0 - 2, 0)
        xpad[2 - (t0 - lo):] = x[b, lo:t0 + TOK]
        m["xtm"] = xpad
        m["flag"] = np.full((128, 1), 0.0 if s == 0 else 1.0, np.float32)
        maps.append(m)
    return maps


def _run(nc, maps):
    res = run_bass_kernel_spmd(nc, maps, core_ids=list(range(NCORES)))
    return res.results


def _tok_gather(results, key):
    return np.stack([np.asarray(r[key]) for r in results]).reshape(2, SEQ, -1)


def kernel(**inp):
    nc, _ = build_fused()
    res = _run(nc, fused_inputs(inp))
    out = _tok_gather(res, "out")
    return np.ascontiguousarray(out.astype(np.float32))
```

```python
import numpy as np
import concourse.bass as bass
import concourse.mybir as mybir
from concourse.bass_utils import run_bass_kernel_spmd

F32 = mybir.dt.float32
BF16 = mybir.dt.bfloat16
ALU = mybir.AluOpType
AF = mybir.ActivationFunctionType
AX = mybir.AxisListType

NCORES = 8
D = 1024
SEQ = 8192
TOK = 2048
DFF = 2816
NJ = DFF // 128
ALPHA = 4.0 ** 0.25
LN_EPS = 1e-5

COMPUTE = ("pe", "act", "dve", "pool")
DMAQ = ("sp", "pq")
NDSEM = 12


class Op:
    __slots__ = ("eng", "fn", "reads", "writes", "deps", "inc", "cnt", "dsem", "duse", "barrier")

    def __init__(self, eng, fn, reads, writes):
        self.eng = eng; self.fn = fn; self.reads = reads; self.writes = writes
        self.deps = set(); self.inc = False; self.cnt = 0; self.dsem = None; self.duse = 0
        self.barrier = False


class Prog:
    def __init__(self, nc):
        self.nc = nc
        self.ops = []
        self._uid = 0

    def stream(self, e):
        return "pool" if e in ("pq", "cc") else e

    def op(self, eng, fn, reads=(), writes=()):
        o = Op(eng, fn, tuple(reads), tuple(writes))
        self.ops.append(o)
        return o

    def capture(self, fn):
        n0 = len(self.ops)
        fn()
        got = self.ops[n0:]
        del self.ops[n0:]
        return got

    def interleave(self, a, b):
        i = j = 0
        while i < len(a) or j < len(b):
            if j >= len(b) or (i < len(a) and i * len(b) <= j * len(a)):
                self.ops.append(a[i]); i += 1
            else:
                self.ops.append(b[j]); j += 1

    def interleave_n(self, lists):
        pos = [0] * len(lists)
        total = sum(len(l) for l in lists)
        for _ in range(total):
            best = None
            for i, l in enumerate(lists):
                if pos[i] < len(l):
                    frac = pos[i] / len(l)
                    if best is None or frac < best[0]:
                        best = (frac, i)
            i = best[1]
            self.ops.append(lists[i][pos[i]]); pos[i] += 1

    def dma(self, q, out, in_, reads=(), writes=(), **kw):
        return self.op(q, lambda e: e.dma_start(out=out, in_=in_, **kw), reads, writes)

    def barrier(self):
        for e in COMPUTE + ("sp",):
            o = self.op(e, None)
            o.barrier = True

    SBUF_TOP = 229376

    def sbuf(self, shape, dt, name=None):
        self._uid += 1
        nm = f"{name or 'sb'}_{self._uid}"
        if getattr(self, "arena_ptr", None) is None:
            return self.nc.alloc_sbuf_tensor(nm, list(shape), dt)
        esz = 4 if dt == F32 else 2
        n = 1
        for d in shape[1:]:
            n *= d
        nbytes = (n * esz + 63) // 64 * 64
        off = self.arena_ptr
        assert off + nbytes <= self.SBUF_TOP, f"SBUF arena overflow: {off}+{nbytes}"
        self.arena_ptr = off + nbytes
        return self.nc.alloc_sbuf_tensor_at(nm, list(shape), dt, offset=off)

    def phase(self):
        if getattr(self, "arena_base", None) is None:
            self.arena_base = (self.SBUF_TOP - self.nc.sbuf_bytes_remaining + 63) // 64 * 64
        else:
            self.barrier()
        self.arena_ptr = self.arena_base

    def psum(self, shape, dt=F32, name=None):
        self._uid += 1
        return self.nc.alloc_psum_tensor(f"{name or 'ps'}_{self._uid}", list(shape), dt)

    def finalize(self):
        nc = self.nc
        ops = self.ops
        last_w = {}
        readers = {}
        last_on = {}
        all_dma = []
        for i, o in enumerate(ops):
            st = self.stream(o.eng)
            isdma = (o.eng in DMAQ or o.eng == "cc") and not o.barrier
            if o.barrier:
                for s2, j in last_on.items():
                    if s2 != st and s2 != "__cc":
                        o.deps.add(j)
                for j in all_dma:
                    o.deps.add(j)
            else:
                raw = set()
                for k in o.reads:
                    j = last_w.get(k)
                    if j is not None:
                        raw.add(j)
                war = set()
                for k in o.writes:
                    j = last_w.get(k)
                    if j is not None:
                        raw.add(j)
                    for r in readers.get(k, ()):
                        war.add(r)
                for j in raw:
                    p = ops[j]
                    pst = self.stream(p.eng)
                    pdma = p.eng in DMAQ or p.eng == "cc"
                    if pst == st and not pdma and not isdma and st == "pe":
                        continue
                    o.deps.add(j)
                for j in war:
                    p = ops[j]
                    pst = self.stream(p.eng)
                    pdma = p.eng in DMAQ or p.eng == "cc"
                    if pst == st and not pdma and not isdma:
                        continue
                    o.deps.add(j)
                o.deps.discard(i)
                for k in o.reads:
                    readers.setdefault(k, []).append(i)
                for k in o.writes:
                    last_w[k] = i
                    readers[k] = []
            if o.eng == "cc":
                if last_on.get("__cc") is not None:
                    o.deps.add(last_on["__cc"])
                last_on["__cc"] = i
            if isdma:
                all_dma.append(i)
            elif not o.barrier:
                last_on[st] = i
        waited_idx = {}
        for i, o in enumerate(ops):
            st = self.stream(o.eng)
            for j in sorted(o.deps):
                p = ops[j]
                if p.eng in DMAQ or p.eng == "cc":
                    continue
                key = (st, p.eng)
                if waited_idx.get(key, -1) >= j:
                    continue
                waited_idx[key] = j
                p.inc = True
        cnt = {s: 0 for s in COMPUTE + ("cc",)}
        dcount = {q: 0 for q in DMAQ}
        for o in ops:
            if o.barrier:
                continue
            if o.eng in DMAQ:
                n = dcount[o.eng]; dcount[o.eng] += 1
                o.dsem = (o.eng, n % NDSEM); o.duse = n // NDSEM
            elif o.inc or o.eng == "cc":
                o.inc = True
                cnt[o.eng] += 1
                o.cnt = cnt[o.eng]
        sems = {s: nc.alloc_semaphore(f"s_{s}") for s in COMPUTE + ("cc",)}
        dsems = {(q, k): nc.alloc_semaphore(f"d_{q}{k}") for q in DMAQ for k in range(NDSEM)}
        streams = {s: [] for s in COMPUTE + ("sp",)}
        for i, o in enumerate(ops):
            streams[self.stream(o.eng)].append(i)
        self.stats = {s: len(v) for s, v in streams.items()}

        def emit(stream_name, eng):
            waited = {}
            for i in streams[stream_name]:
                o = ops[i]
                need = {}
                for j in o.deps:
                    p = ops[j]
                    if p.eng in DMAQ:
                        key = p.dsem; val = 16 * (p.duse + 1)
                    else:
                        key = p.eng; val = p.cnt
                    if need.get(key, 0) < val:
                        need[key] = val
                if o.eng in DMAQ and not o.barrier and o.duse > 0:
                    key = o.dsem; val = 16 * o.duse
                    if need.get(key, 0) < val:
                        need[key] = val
                for key, val in need.items():
                    if waited.get(key, 0) >= val:
                        continue
                    waited[key] = val
                    sem = dsems[key] if isinstance(key, tuple) else sems[key]
                    eng.wait_ge(sem, val)
                if o.barrier:
                    continue
                ins = o.fn(eng)
                if o.eng in DMAQ:
                    ins.then_inc(dsems[o.dsem], 16)
                elif o.inc:
                    ins.then_inc(sems[o.eng], 1)

        with nc.Block() as block:
            @block.tensor
            def _(e):
                emit("pe", e)

            @block.scalar
            def _(e):
                emit("act", e)

            @block.vector
            def _(e):
                emit("dve", e)

            @block.gpsimd
            def _(e):
                emit("pool", e)

            @block.sync
            def _(e):
                emit("sp", e)
                fin = {}
                for o in ops:
                    if o.eng in DMAQ and not o.barrier:
                        fin[o.dsem] = max(fin.get(o.dsem, 0), 16 * (o.duse + 1))
                for key, val in fin.items():
                    e.wait_ge(dsems[key], val)


def new_nc():
    return bass.Bass("TRN2", target_bir_lowering=False)


def din(nc, name, shape, dt=F32):
    return nc.dram_tensor(name, list(shape), dt, kind="ExternalInput").ap()


def dout(nc, name, shape, dt=F32):
    return nc.dram_tensor(name, list(shape), dt, kind="ExternalOutput").ap()


def make_ident(P, dt=F32):
    ident = P.sbuf([128, 128], dt, "ident")
    P.op("pool", lambda e: e.memset(ident[:], 0.0), writes=["ident"])
    P.op("pool", lambda e: e.affine_select(out=ident[:], in_=ident[:], pattern=[[-1, 128]],
                                           compare_op=ALU.not_equal, fill=1.0, base=0,
                                           channel_multiplier=1), reads=["ident"], writes=["ident"])
    return ident


class Banks:
    def __init__(self, P):
        self.t = [P.psum([128, 512], F32, f"bank{i}") for i in range(8)]
        self.key = [f"bank{i}" for i in range(8)]


def layer_norm_block(P, tag, r, g_bc, b_bc, out, stat, eng2="pool", n=128):
    rk = tag + "r"
    st6 = stat["st6"]; mv = stat["mv"]; rstd = stat["rstd"]
    for c in range(2):
        P.op("dve", lambda e, c=c: e.bn_stats(out=st6[0:n, c * 6:(c + 1) * 6], in_=r[0:n, c * 512:(c + 1) * 512]),
             reads=[rk], writes=[tag + "st6"])
    P.op("dve", lambda e: e.bn_aggr(out=mv[0:n, 0:2], in_=st6[0:n, 0:12]), reads=[tag + "st6"], writes=[tag + "mv"])
    P.op("dve", lambda e: e.tensor_scalar_add(out=rstd[0:n, 0:1], in0=mv[0:n, 1:2], scalar1=LN_EPS),
         reads=[tag + "mv"], writes=[tag + "rstd"])
    P.op("act", lambda e: e.activation(out=rstd[0:n, 0:1], in_=rstd[0:n, 0:1], func=AF.Sqrt),
         reads=[tag + "rstd"], writes=[tag + "rstd"])
    P.op("dve", lambda e: e.reciprocal(out=rstd[0:n, 0:1], in_=rstd[0:n, 0:1]), reads=[tag + "rstd"],
         writes=[tag + "rstd"])
    P.op("dve", lambda e: e.tensor_scalar(out=r[0:n, :], in0=r[0:n, :], scalar1=mv[0:n, 0:1], scalar2=rstd[0:n, 0:1],
                                          op0=ALU.subtract, op1=ALU.mult),
         reads=[rk, tag + "mv", tag + "rstd"], writes=[rk])
    P.op(eng2, lambda e: e.tensor_tensor(out=r[0:n, :], in0=r[0:n, :], in1=g_bc[0:n, :], op=ALU.mult),
         reads=[rk, "lnconst"], writes=[rk])
    P.op(eng2, lambda e: e.tensor_tensor(out=out[0:n, :], in0=r[0:n, :], in1=b_bc[0:n, :], op=ALU.add),
         reads=[rk, "lnconst"], writes=[tag + "lnout"] + ([rk] if out is r else []))


FFN_IN = {"wup": [NJ, 128, 8, 256], "cw": [128, NJ * 6], "cb": [128, NJ * 2], "wdn": [128, NJ, D],
          "lng": [128, D], "lnb": [128, D], "bg": [128, D], "wg": [128, 8, D], "wp": [128, 2, D],
          "pT": [128, 2, TOK]}


def build_ffn(TT=512, dbg_nt=None, dbg_nj=None, dbg_block=True):
    nc = new_nc(); P = Prog(nc)
    io = {k: din(nc, k, v) for k, v in FFN_IN.items()}
    io["hT"] = din(nc, "hT", [128, 8, TOK + 2]); io["htm"] = din(nc, "htm", [TOK, D])
    io["ho"] = dout(nc, "ho", [TOK, D])
    B = Banks(P)
    emit_ffn(P, B, io, TT, dbg_nt, dbg_nj, dbg_block)
    P.finalize()
    return nc, P


def emit_ffn(P, B, io, TT=512, dbg_nt=None, dbg_nj=None, dbg_block=True):
    NT = TOK // TT
    NB = TT // 128
    hT_d = io["hT"]; htm_d = io["htm"]; wup_d = io["wup"]; cw_d = io["cw"]; cb_d = io["cb"]; wdn_d = io["wdn"]
    lng_d = io["lng"]; lnb_d = io["lnb"]; bg_d = io["bg"]; wg_d = io["wg"]; wp_d = io["wp"]; pT_d = io["pT"]
    ho_d = io["ho"]
    ident = make_ident(P, F32)
    cw = P.sbuf([128, NJ * 6], F32); cb = P.sbuf([128, NJ * 2], F32)
    lng = P.sbuf([128, D], F32); lnb = P.sbuf([128, D], F32); bg = P.sbuf([128, D], F32)
    P.dma("sp", cw[:], cw_d[:, :], writes=["cw"]); P.dma("sp", cb[:], cb_d[:, :], writes=["cw"])
    P.dma("sp", lng[:], lng_d[:, :], writes=["lnconst"]); P.dma("sp", lnb[:], lnb_d[:, :], writes=["lnconst"])
    P.dma("sp", bg[:], bg_d[:, :], writes=["lnconst"])
    wdn = P.sbuf([128, NJ, D], BF16); wg = P.sbuf([128, 8, D], BF16); wp = P.sbuf([128, 2, D], BF16)
    hTs = [P.sbuf([128, 8, TT + 2], BF16) for _ in range(2)]
    wbuf = [P.sbuf([128, 8, 256], BF16) for _ in range(3)]
    ubuf = [[P.sbuf([128, TT + 2], F32) for _ in range(2)] for _ in range(2)]
    cbuf = [[P.sbuf([128, TT], F32) for _ in range(2)] for _ in range(2)]
    sg = [P.sbuf([128, TT], F32) for _ in range(2)]
    hdn = P.sbuf([128, NJ, TT], BF16)
    pTs = [P.sbuf([128, 2, 128], BF16) for _ in range(4)]
    hres = [P.sbuf([128, D], F32) for _ in range(2)]
    rb = [P.sbuf([128, D], F32) for _ in range(2)]
    yb = [P.sbuf([128, D], F32) for _ in range(2)]
    yT = [P.sbuf([128, 8, 128], BF16) for _ in range(2)]
    gsb = [P.sbuf([128, D], F32) for _ in range(2)]
    ob = [P.sbuf([128, D], F32) for _ in range(2)]
    stat = {"st6": P.sbuf([128, 12], F32), "mv": P.sbuf([128, 2], F32), "rstd": P.sbuf([128, 1], F32)}

    resident = ([(wdn[:, j, :], wdn_d[:, j, :], f"wdn{j}") for j in range(NJ)]
                + [(wg[:, k, :], wg_d[:, k, :], "wg") for k in range(8)] + [(wp[:, :, :], wp_d[:, :, :], "wp")])

    def load_resident(n):
        for _ in range(n):
            if resident:
                o, i, k = resident.pop(0)
                P.dma("pq", o, i, writes=[k])

    NTr = dbg_nt or NT
    NJr = dbg_nj or NJ
    iters = [(tt, j) for tt in range(NTr) for j in range(NJr)]
    LA = 2
    NWB = len(wbuf)

    def load_w(idx):
        j = iters[idx][1]
        P.dma("pq", wbuf[idx % NWB][:, :, :], wup_d[j, :, :, :], writes=[f"wbuf{idx % NWB}"])

    def load_hT(tt):
        P.dma("pq", hTs[tt % 2][:, :, :], hT_d[:, :, tt * TT: tt * TT + TT + 2], writes=[f"hTs{tt % 2}"])

    load_hT(0)
    for idx in range(min(LA, len(iters))):
        load_w(idx)
    bankrr = 0
    for tt in range(NTr):
        hp = tt % 2
        for j in range(NJr):
            idx = tt * NJr + j
            if idx + LA < len(iters):
                load_w(idx + LA)
            load_resident(2 if j < NJr - 1 else len(resident))
            if j == 8 and tt + 1 < NTr:
                load_hT(tt + 1)
            if j == 4:
                for bi in range(NB):
                    t0 = (tt * NB + bi) * 128
                    P.dma("pq", pTs[bi][:, :, :], pT_d[:, :, t0:t0 + 128], writes=[f"pTs{bi}"])
            wp_ = idx % NWB
            up = j % 2
            for gv in range(2):
                bh = bankrr % 4; bm = (bankrr + 1) % 4; bankrr += 2
                for (bk, c0, n) in ((bh, 0, 2), (bm, 2, TT)):
                    for k in range(8):
                        P.op("pe", lambda e, bk=bk, k=k, c0=c0, n=n, gv=gv, wp_=wp_, hp=hp: e.matmul(
                            B.t[bk][:, 0:n], lhsT=wbuf[wp_][:, k, gv * 128:(gv + 1) * 128],
                            rhs=hTs[hp][:, k, c0:c0 + n], start=(k == 0), stop=(k == 7)),
                            reads=[f"wbuf{wp_}", f"hTs{hp}"], writes=[B.key[bk]])
                    P.op("act", lambda e, bk=bk, c0=c0, n=n, gv=gv, up=up: e.copy(
                        out=ubuf[gv][up][:, c0:c0 + n], in_=B.t[bk][:, 0:n]),
                        reads=[B.key[bk]], writes=[f"ubuf{gv}{up}"])
            for gv in range(2):
                ci = (j * 2 + gv) * 3
                u = ubuf[gv][up]; c = cbuf[gv][up]
                P.op("act", lambda e, u=u, c=c, ci=ci, j=j, gv=gv: e.activation(
                    out=c[:, :], in_=u[:, 2:TT + 2], func=AF.Identity,
                    scale=cw[:, ci + 2:ci + 3], bias=cb[:, j * 2 + gv:j * 2 + gv + 1]),
                    reads=[f"ubuf{gv}{up}", "cw"], writes=[f"cbuf{gv}{up}"])
                for tap in (1, 0):
                    P.op("dve", lambda e, u=u, c=c, ci=ci, tap=tap: e.scalar_tensor_tensor(
                        out=c[:, :], in0=u[:, tap:tap + TT], scalar=cw[:, ci + tap:ci + tap + 1], in1=c[:, :],
                        op0=ALU.mult, op1=ALU.add),
                        reads=[f"ubuf{gv}{up}", "cw", f"cbuf{gv}{up}"], writes=[f"cbuf{gv}{up}"])
            P.op("act", lambda e, up=up: e.activation(out=sg[up][:, :], in_=cbuf[0][up][:, :], func=AF.Silu),
                 reads=[f"cbuf0{up}"], writes=[f"sg{up}"])
            P.op("dve", lambda e, up=up, j=j: e.tensor_tensor(out=hdn[:, j, :], in0=sg[up][:, :],
                                                             in1=cbuf[1][up][:, :], op=ALU.mult),
                 reads=[f"sg{up}", f"cbuf1{up}"], writes=[f"hdn{j}"])
        for bi in range(NB if dbg_block else 0):
            gb = tt * NB + bi
            par = gb % 2
            t0 = gb * 128
            if bi == 0:
                P.dma("sp", hres[par][:, :], htm_d[t0:t0 + 128, :], writes=[f"hres{par}"])
            if bi + 1 < NB:
                P.dma("sp", hres[1 - par][:, :], htm_d[t0 + 128:t0 + 256, :], writes=[f"hres{1 - par}"])
            for n in range(2):
                bk = 4 + n
                for j in range(NJ):
                    P.op("pe", lambda e, bk=bk, j=j, n=n, bi=bi: e.matmul(
                        B.t[bk][:, :], lhsT=hdn[:, j, bi * 128:(bi + 1) * 128], rhs=wdn[:, j, n * 512:(n + 1) * 512],
                        start=(j == 0), stop=(j == NJ - 1)),
                        reads=[f"hdn{j}", f"wdn{j}"], writes=[B.key[bk]])
                P.op("dve", lambda e, bk=bk, n=n, par=par: e.scalar_tensor_tensor(
                    out=rb[par][:, n * 512:(n + 1) * 512], in0=hres[par][:, n * 512:(n + 1) * 512], scalar=ALPHA,
                    in1=B.t[bk][:, :], op0=ALU.mult, op1=ALU.add),
                    reads=[f"hres{par}", B.key[bk]], writes=[f"b{par}r"])
            layer_norm_block(P, f"b{par}", rb[par], lng, lnb, yb[par], stat, eng2="dve")
            yk = f"b{par}lnout"
            for k in range(8):
                bk = 6 + k // 4
                P.op("pe", lambda e, bk=bk, k=k, par=par: e.transpose(
                    out=B.t[bk][:, (k % 4) * 128:(k % 4 + 1) * 128], in_=yb[par][:, k * 128:(k + 1) * 128],
                    identity=ident[:]), reads=[yk, "ident"], writes=[B.key[bk]])
            for hh in range(2):
                P.op("act", lambda e, hh=hh, par=par: e.copy(
                    out=yT[par][:, hh * 4:(hh + 1) * 4, :], in_=B.t[6 + hh][:, :].rearrange("p (k t) -> p k t", k=4)),
                    reads=[B.key[6 + hh]], writes=[f"yT{par}"])
            for n in range(2):
                for k in range(8):
                    P.op("pe", lambda e, n=n, k=k, par=par: e.matmul(
                        B.t[n][:, :], lhsT=yT[par][:, k, :], rhs=wg[:, k, n * 512:(n + 1) * 512],
                        start=(k == 0), stop=(k == 7)), reads=[f"yT{par}", "wg"], writes=[B.key[n]])
                for k in range(2):
                    P.op("pe", lambda e, n=n, k=k, bi=bi: e.matmul(
                        B.t[2 + n][:, :], lhsT=pTs[bi][:, k, :], rhs=wp[:, k, n * 512:(n + 1) * 512],
                        start=(k == 0), stop=(k == 1)), reads=[f"pTs{bi}", "wp"], writes=[B.key[2 + n]])
                P.op("dve", lambda e, n=n, par=par: e.tensor_tensor(
                    out=gsb[par][:, n * 512:(n + 1) * 512], in0=B.t[n][:, :], in1=bg[:, n * 512:(n + 1) * 512],
                    op=ALU.add), reads=[B.key[n], "lnconst"], writes=[f"gsb{par}"])
            P.op("act", lambda e, par=par: e.activation(out=gsb[par][:, :], in_=gsb[par][:, :], func=AF.Sigmoid),
                 reads=[f"gsb{par}"], writes=[f"gsb{par}"])
            for n in range(2):
                P.op("dve", lambda e, n=n, par=par: e.tensor_tensor(
                    out=gsb[par][:, n * 512:(n + 1) * 512], in0=gsb[par][:, n * 512:(n + 1) * 512],
                    in1=B.t[2 + n][:, :], op=ALU.mult), reads=[f"gsb{par}", B.key[2 + n]], writes=[f"gsb{par}"])
            P.op("dve", lambda e, par=par: e.tensor_tensor(out=ob[par][:, :], in0=gsb[par][:, :], in1=yb[par][:, :],
                                                            op=ALU.add), reads=[f"gsb{par}", yk], writes=[f"ob{par}"])
            P.dma("sp", ho_d[t0:t0 + 128, :], ob[par][:, :], reads=[f"ob{par}"])
            if "hsrc" in io:
                for k in range(8):
                    bk = 6 + k // 4
                    P.op("pe", lambda e, bk=bk, k=k, par=par: e.transpose(
                        out=B.t[bk][:, (k % 4) * 128:(k % 4 + 1) * 128], in_=ob[par][:, k * 128:(k + 1) * 128],
                        identity=ident[:]), reads=[f"ob{par}", "ident"], writes=[B.key[bk]])
                for hh in range(2):
                    P.op("act", lambda e, hh=hh, par=par: e.copy(
                        out=yT[par][:, hh * 4:(hh + 1) * 4, :],
                        in_=B.t[6 + hh][:, :].rearrange("p (k t) -> p k t", k=4)),
                        reads=[B.key[6 + hh]], writes=[f"yT{par}"])
                hk = io.setdefault("hkeys", [])
                hk.append(f"hsrc_{len(hk)}")
                P.dma("sp", io["hsrc"](t0), yT[par][:, :, :], reads=[f"yT{par}"], writes=[hk[-1]])
                if gb == TOK // 128 - 1:
                    hk.append(f"hsrc_{len(hk)}")
                    P.dma("sp", io["hh"][:, :], ob[par][126:128, :], reads=[f"ob{par}"], writes=[hk[-1]])
                io["on_block_done"](gb)


def _fm(a, halo=0):
    t, f = a.shape
    return np.ascontiguousarray(a.reshape(t, f // 128, 128).transpose(2, 1, 0))


def _wr(w):
    k, n = w.shape
    return np.ascontiguousarray(w.reshape(k // 128, 128, n).transpose(1, 0, 2))


def _bc(v):
    return np.ascontiguousarray(np.broadcast_to(v[None, :], (128, v.shape[0])))


def ffn_inputs(h, p_i, w_up, conv_w, conv_b, w_down, g, b, w_proj, w_gate, b_gate):
    wup = w_up.reshape(8, 128, 2, NJ, 128).transpose(3, 1, 0, 2, 4).reshape(NJ, 128, 8, 256)
    cw = conv_w.reshape(3, 2, NJ, 128).transpose(3, 2, 1, 0).reshape(128, NJ * 6)
    cb = conv_b.reshape(2, NJ, 128).transpose(2, 1, 0).reshape(128, NJ * 2)
    common = {"wup": np.ascontiguousarray(wup), "cw": np.ascontiguousarray(cw), "cb": np.ascontiguousarray(cb),
              "wdn": _wr(w_down), "lng": _bc(g), "lnb": _bc(b), "bg": _bc(b_gate),
              "wg": _wr(w_gate), "wp": _wr(w_proj)}
    maps = []
    for c in range(NCORES):
        bi, s = divmod(c, 4)
        t0 = s * TOK
        if h is not None:
            hpad = np.zeros((TOK + 2, D), np.float32)
            lo = max(t0 - 2, 0)
            hpad[2 - (t0 - lo):] = h[bi, lo:t0 + TOK]
        m = dict(common)
        if h is not None:
            m["hT"] = _fm(hpad)
            m["htm"] = np.ascontiguousarray(h[bi, t0:t0 + TOK])
        m["pT"] = _fm(np.ascontiguousarray(p_i[bi, t0:t0 + TOK]))
        maps.append(m)
    return maps


NEG = -30000.0


ATTN_IN = {"xT": [128, 8, SEQ], "wq": [128, 8, 128], "wk": [128, 8, 128], "wv": [128, 8, 128],
           "wf": [128, 8, 2], "bf": [2, 1], "wc": [128, 8, 384], "cwa": [128, 3]}


def build_attn(dbg_nqg=None):
    nc = new_nc(); P = Prog(nc)
    io = {k: din(nc, k, v) for k, v in ATTN_IN.items()}
    io["ybT"] = dout(nc, "ybT", [128, SEQ]); io["yaT"] = dout(nc, "yaT", [128, SEQ])
    B = Banks(P)
    emit_attn(P, B, io, dbg_nqg)
    P.finalize()
    return nc, P


def emit_attn(P, B, io, dbg_nqg=None):
    NTT = SEQ // 512
    xT_d = io["xT"]; wq_d = io["wq"]; wk_d = io["wk"]; wv_d = io["wv"]; wf_d = io["wf"]; bf_d = io["bf"]
    wc_d = io["wc"]; cwa_d = io["cwa"]
    if "ya_ap" in io:
        ya_ap = io["ya_ap"]; yb_ap = io["yb_ap"]; YDT = BF16
    else:
        yb_d = io["ybT"]; ya_d = io["yaT"]; YDT = ya_d.dtype
        ya_ap = lambda c0: ya_d[:, c0:c0 + 512]
        yb_ap = lambda c0: yb_d[:, c0:c0 + 512]
    ykeys = io.setdefault("ykeys", [])

    def ykey():
        ykeys.append(f"ysrc_{len(ykeys)}")
        return ykeys[-1]
    wc = P.sbuf([128, 8, 384], BF16, "wc_s"); cwa = P.sbuf([128, 3], F32, "cwa_s")
    P.dma("pq", wc[:, :, :], wc_d[:, :, :], writes=["w"]); P.dma("sp", cwa[:, :], cwa_d[:, :], writes=["w"])
    mbuf = [P.sbuf([128, 514], F32) for _ in range(2)]
    gcs = P.sbuf([128, 512], F32, "gcs"); cacc = P.sbuf([128, 512], F32, "cacc")
    yas = [P.sbuf([128, 512], YDT) for _ in range(2)]
    P.op("pool", lambda e: e.memset(mbuf[1][:, :], 0.0), writes=["mbuf1"])
    identb = P.sbuf([128, 128], BF16, "identb")
    maskT = P.sbuf([128, 128], BF16, "maskT")
    tmpi = P.sbuf([128, 128], F32, "tmpi")
    P.op("pool", lambda e: e.memset(tmpi[:], 0.0), writes=["tmpi"])
    P.op("pool", lambda e: e.affine_select(out=tmpi[:], in_=tmpi[:], pattern=[[-1, 128]], compare_op=ALU.not_equal,
                                           fill=1.0, base=0, channel_multiplier=1), reads=["tmpi"], writes=["tmpi"])
    P.op("pool", lambda e: e.tensor_copy(out=identb[:], in_=tmpi[:]), reads=["tmpi"], writes=["identb"])
    tmpm = P.sbuf([128, 128], F32, "tmpm")
    P.op("pool", lambda e: e.memset(tmpm[:], 0.0), writes=["tmpm"])
    P.op("pool", lambda e: e.affine_select(out=tmpm[:], in_=tmpm[:], pattern=[[1, 128]], compare_op=ALU.is_ge,
                                           fill=NEG, base=0, channel_multiplier=-1), reads=["tmpm"], writes=["tmpm"])
    P.op("pool", lambda e: e.tensor_copy(out=maskT[:], in_=tmpm[:]), reads=["tmpm"], writes=["maskT"])

    wq = P.sbuf([128, 8, 128], BF16); wk = P.sbuf([128, 8, 128], BF16); wv = P.sbuf([128, 8, 128], BF16)
    wf = P.sbuf([128, 8, 2], BF16); bfs = P.sbuf([2, 1], F32)
    P.dma("pq", wq[:, :, :], wq_d[:, :, :], writes=["w"]); P.dma("pq", wk[:, :, :], wk_d[:, :, :], writes=["w"])
    P.dma("pq", wv[:, :, :], wv_d[:, :, :], writes=["w"]); P.dma("pq", wf[:, :, :], wf_d[:, :, :], writes=["w"])
    P.dma("sp", bfs[:, :], bf_d[:, :], writes=["w"])
    Q = [P.sbuf([70, SEQ], BF16, f"Q{h}") for h in range(2)]
    Kt = [P.sbuf([70, SEQ], BF16, f"K{h}") for h in range(2)]
    Va = P.sbuf([128, SEQ // 128, 2, 128], BF16, "Va")
    P.op("pool", lambda e: e.memset(Va[:, :, :, 64:128], 1.0), writes=["Vones"])
    xs = [P.sbuf([128, 8, 512], BF16) for _ in range(2)]
    Fl = P.sbuf([2, SEQ], F32, "Fl")

    for tt in range(NTT):
        xp = tt % 2
        c0 = tt * 512
        P.dma("pq", xs[xp][:, :, :], xT_d[:, :, c0:c0 + 512], writes=[f"xs{xp}"])
        for (w, dst, nm, bk, scale) in ((wq, Q, "Q", 0, 0.125), (wk, Kt, "K", 1, 1.0)):
            for k in range(8):
                P.op("pe", lambda e, w=w, k=k, bk=bk, xp=xp: e.matmul(
                    B.t[bk][:, :], lhsT=w[:, k, :], rhs=xs[xp][:, k, :], start=(k == 0), stop=(k == 7)),
                    reads=["w", f"xs{xp}"], writes=[B.key[bk]])
            for h in range(2):
                eng = "act" if h == 0 else "dve"
                if eng == "act":
                    P.op("act", lambda e, dst=dst, h=h, bk=bk, c0=c0, scale=scale: e.activation(
                        out=dst[h][0:64, c0:c0 + 512], in_=B.t[bk][h * 64:(h + 1) * 64, :], func=AF.Copy, scale=scale),
                        reads=[B.key[bk]], writes=[f"{nm}{h}_{tt}"])
                else:
                    P.op("dve", lambda e, dst=dst, h=h, bk=bk, c0=c0, scale=scale: e.tensor_scalar_mul(
                        out=dst[h][0:64, c0:c0 + 512], in0=B.t[bk][h * 64:(h + 1) * 64, :], scalar1=scale),
                        reads=[B.key[bk]], writes=[f"{nm}{h}_{tt}"])
        for bi in range(4):
            for k in range(8):
                P.op("pe", lambda e, k=k, bi=bi, xp=xp: e.matmul(
                    B.t[2][:, bi * 128:(bi + 1) * 128], lhsT=xs[xp][:, k, bi * 128:(bi + 1) * 128], rhs=wv[:, k, :],
                    start=(k == 0), stop=(k == 7)), reads=["w", f"xs{xp}"], writes=[B.key[2]])
        P.op("dve", lambda e, tt=tt: e.tensor_copy(
            out=Va[:, tt * 4:(tt + 1) * 4, :, 0:64],
            in_=B.t[2][:, :].rearrange("p (b h d) -> p b h d", b=4, h=2)),
            reads=[B.key[2]], writes=[f"V{tt}"])
        for k in range(8):
            P.op("pe", lambda e, k=k, xp=xp: e.matmul(
                B.t[3][0:2, :], lhsT=wf[:, k, :], rhs=xs[xp][:, k, :], start=(k == 0), stop=(k == 7)),
                reads=["w", f"xs{xp}"], writes=[B.key[3]])
        P.op("act", lambda e, c0=c0: e.activation(out=Fl[:, c0:c0 + 512], in_=B.t[3][0:2, :], func=AF.Sigmoid,
                                                  bias=bfs[:, 0:1]), reads=[B.key[3], "w"], writes=["Fl"])
        for ci in range(3):
            for k in range(8):
                P.op("pe", lambda e, ci=ci, k=k, xp=xp: e.matmul(
                    B.t[4 + ci][:, :], lhsT=wc[:, k, ci * 128:(ci + 1) * 128], rhs=xs[xp][:, k, :],
                    start=(k == 0), stop=(k == 7)), reads=["w", f"xs{xp}"], writes=[B.key[4 + ci]])
        mp = tt % 2
        P.op("act", lambda e: e.copy(out=gcs[:, :], in_=B.t[5][:, :]), reads=[B.key[5]], writes=["gcs"])
        P.op("pool", lambda e, mp=mp: e.tensor_copy(out=mbuf[mp][:, 0:2], in_=mbuf[1 - mp][:, 512:514]),
             reads=[f"mbuf{1 - mp}"], writes=[f"mbuf{mp}h"])
        P.op("dve", lambda e, mp=mp: e.tensor_tensor(out=mbuf[mp][:, 2:514], in0=B.t[6][:, :], in1=gcs[:, :],
                                                     op=ALU.mult), reads=[B.key[6], "gcs"], writes=[f"mbuf{mp}"])
        P.op("act", lambda e, mp=mp: e.activation(out=cacc[:, :], in_=mbuf[mp][:, 2:514], func=AF.Copy,
                                                  scale=cwa[:, 2:3]), reads=[f"mbuf{mp}", "w"], writes=["cacc"])
        for tap in (1, 0):
            P.op("dve", lambda e, mp=mp, tap=tap: e.scalar_tensor_tensor(
                out=cacc[:, :], in0=mbuf[mp][:, tap:tap + 512], scalar=cwa[:, tap:tap + 1], in1=cacc[:, :],
                op0=ALU.mult, op1=ALU.add), reads=[f"mbuf{mp}", f"mbuf{mp}h", "w", "cacc"], writes=["cacc"])
        P.op("dve", lambda e, mp=mp: e.tensor_tensor(out=yas[mp][:, :], in0=B.t[4][:, :], in1=cacc[:, :], op=ALU.mult),
             reads=[B.key[4], "cacc"], writes=[f"yas{mp}"])
        P.dma("sp", ya_ap(c0), yas[mp][:, :], reads=[f"yas{mp}"], writes=[ykey()])
    P.op("act", lambda e: e.activation(out=Fl[:, :], in_=Fl[:, :], func=AF.Ln), reads=["Fl"], writes=["Fl"])
    ones2 = P.sbuf([2, 512], F32, "ones2")
    P.op("pool", lambda e: e.memset(ones2[:], 1.0), writes=["ones2"])
    carry = P.sbuf([2, 1], F32, "carry")
    P.op("pool", lambda e: e.memset(carry[:], 0.0), writes=["carry"])
    PC = 512
    augq = P.sbuf([2, 6, PC], BF16, "augq"); augk = P.sbuf([2, 6, PC], BF16, "augk")
    t1 = P.sbuf([2, PC], F32, "t1"); t2 = P.sbuf([2, PC], F32, "t2")
    P.op("pool", lambda e: e.memset(augq[:, 3:6, :], 1.0), writes=["augq"])
    P.op("pool", lambda e: e.memset(augk[:, 0:3, :], 1.0), writes=["augk"])
    for pc in range(SEQ // PC):
        c0 = pc * PC
        P.op("dve", lambda e, c0=c0: e.tensor_tensor_scan(
            out=Fl[:, c0:c0 + PC], data0=ones2[:, :], data1=Fl[:, c0:c0 + PC], initial=carry[:, 0:1],
            op0=ALU.mult, op1=ALU.add), reads=["Fl", "ones2", "carry"], writes=["Fl"])
        P.op("dve", lambda e, c0=c0: e.tensor_copy(out=carry[:, 0:1], in_=Fl[:, c0 + PC - 1:c0 + PC]),
             reads=["Fl"], writes=["carry"])
        src = Fl[:, c0:c0 + PC]
        for term in range(3):
            P.op("dve", lambda e, term=term, src=src: e.tensor_copy(out=augq[:, term, :], in_=src),
                 reads=["Fl", "t1", "t2"], writes=["augq"])
            P.op("dve", lambda e, term=term: e.tensor_scalar_mul(out=augk[:, 3 + term, :], in0=augq[:, term, :],
                                                                 scalar1=-1.0), reads=["augq"], writes=["augk"])
            if term < 2:
                dst = t1 if term == 0 else t2
                P.op("dve", lambda e, term=term, src=src, dst=dst: e.tensor_tensor(
                    out=dst[:, :], in0=src, in1=augq[:, term, :], op=ALU.subtract),
                    reads=["Fl", "t1", "t2", "augq"], writes=["t1", "t2"])
                src = dst[:, :]
        for h in range(2):
            P.dma("sp", Q[h][64:70, c0:c0 + PC], augq[h:h + 1, :, :], reads=["augq"], writes=[f"Qaug{h}"])
            P.dma("sp", Kt[h][64:70, c0:c0 + PC], augk[h:h + 1, :, :], reads=["augk"], writes=[f"Kaug{h}"])
    pT = [P.sbuf([128, 512], BF16) for _ in range(4)]
    rd = [P.sbuf([64, 512], F32, "rd") for _ in range(2)]
    yst = [P.sbuf([128, 512], YDT) for _ in range(2)]
    NQG = dbg_nqg or (SEQ // 512)
    tiles = []
    for qg in range(NQG):
        for h in range(2):
            nkb = 4 * qg + 4
            for kb in range(nkb):
                j = kb - 4 * qg
                qoff = 0 if j <= 0 else j * 128
                tiles.append(dict(qg=qg, h=h, kb=kb, nkb=nkb, qoff=qoff, n=512 - qoff, diag=(j >= 0),
                                  sb=len(tiles) % 4, q0=qg * 512))
    LA = 2

    def emit_qk(t):
        sb, h, kb, q0, qoff, n, diag, qg = t["sb"], t["h"], t["kb"], t["q0"], t["qoff"], t["n"], t["diag"], t["qg"]
        P.op("pe", lambda e: e.matmul(
            B.t[sb][:, 0:n], lhsT=Kt[h][0:70, kb * 128:(kb + 1) * 128], rhs=Q[h][0:70, q0 + qoff:q0 + 512],
            start=True, stop=(not diag)),
            reads=[f"K{h}_{kb // 4}", f"Kaug{h}", f"Q{h}_{qg}", f"Qaug{h}"], writes=[B.key[sb]])
        if diag:
            P.op("pe", lambda e: e.matmul(B.t[sb][:, 0:128], lhsT=identb[:, :], rhs=maskT[:, :],
                                          start=False, stop=True),
                 reads=["identb", "maskT"], writes=[B.key[sb]])

    for t in tiles[:LA]:
        emit_qk(t)
    for i, t in enumerate(tiles):
        sb, h, kb, qoff, n, nkb, qg, q0 = t["sb"], t["h"], t["kb"], t["qoff"], t["n"], t["nkb"], t["qg"], t["q0"]
        ob = 4 + h
        P.op("act", lambda e, sb=sb, n=n: e.activation(out=pT[sb][:, 0:n], in_=B.t[sb][:, 0:n], func=AF.Exp),
             reads=[B.key[sb]], writes=[f"pT{sb}"])
        if i + LA < len(tiles):
            emit_qk(tiles[i + LA])
        P.op("pe", lambda e, sb=sb, h=h, kb=kb, qoff=qoff, n=n, ob=ob, nkb=nkb: e.matmul(
            B.t[ob][:, qoff:512], lhsT=Va[:, kb, h, :], rhs=pT[sb][:, 0:n],
            start=(kb == 0), stop=(kb == nkb - 1)),
            reads=[f"V{kb // 4}", "Vones", f"pT{sb}"], writes=[B.key[ob]])
        if kb == nkb - 1:
            yp = qg % 2
            P.op("dve", lambda e, ob=ob, h=h: e.reciprocal(out=rd[h][0:64, :], in_=B.t[ob][64:128, :]),
                 reads=[B.key[ob]], writes=[f"rd{h}"])
            P.op("dve", lambda e, ob=ob, h=h, yp=yp: e.tensor_tensor(
                out=yst[yp][h * 64:(h + 1) * 64, :], in0=B.t[ob][0:64, :], in1=rd[h][0:64, :], op=ALU.mult),
                reads=[B.key[ob], f"rd{h}"], writes=[f"yst{yp}"])
            if h == 1:
                P.dma("sp", yb_ap(q0), yst[yp][:, :], reads=[f"yst{yp}"], writes=[ykey()])
                if "on_qg_done" in io:
                    io["on_qg_done"](qg)


def attn_inputs(x, w_in, b_f, conv_w):
    maps = []
    for c in range(NCORES):
        bi, hp = divmod(c, 4)
        m = {"xT": _fm(np.ascontiguousarray(x[bi])),
             "wq": _wr(np.ascontiguousarray(w_in[:, 1536 + hp * 128:1536 + (hp + 1) * 128])),
             "wk": _wr(np.ascontiguousarray(w_in[:, 2048 + hp * 128:2048 + (hp + 1) * 128])),
             "wv": _wr(np.ascontiguousarray(w_in[:, 2560 + hp * 128:2560 + (hp + 1) * 128])),
             "wf": _wr(np.ascontiguousarray(w_in[:, 3072 + hp * 2:3072 + hp * 2 + 2])),
             "bf": np.ascontiguousarray(b_f[hp * 2:hp * 2 + 2].reshape(2, 1)),
             "wc": _wr(np.ascontiguousarray(np.concatenate(
                 [w_in[:, i * 512 + hp * 128:i * 512 + (hp + 1) * 128] for i in range(3)], axis=1))),
             "cwa": np.ascontiguousarray(conv_w[:, hp * 128:(hp + 1) * 128].T)}
        maps.append(m)
    return maps


def build_projln(KC):
    nc = new_nc(); P = Prog(nc)
    yT_d = din(nc, "yT", [128, KC, TOK])
    htm_d = din(nc, "htm", [TOK, D])
    w_d = din(nc, "w", [128, KC, D])
    lng_d = din(nc, "lng", [128, D]); lnb_d = din(nc, "lnb", [128, D])
    ho_d = dout(nc, "ho", [TOK, D])
    B = Banks(P)
    lng = P.sbuf([128, D], F32); lnb = P.sbuf([128, D], F32)
    P.dma("sp", lng[:], lng_d[:, :], writes=["lnconst"]); P.dma("sp", lnb[:], lnb_d[:, :], writes=["lnconst"])
    w = P.sbuf([128, KC, D], BF16)
    yTs = P.sbuf([128, KC, TOK], BF16)
    for k in range(KC):
        P.dma("pq", w[:, k, :], w_d[:, k, :], writes=[f"w{k}"])
    for q in range(4):
        P.dma("pq", yTs[:, :, q * 512:(q + 1) * 512], yT_d[:, :, q * 512:(q + 1) * 512], writes=[f"yT{q}"])
    hres = [P.sbuf([128, D], F32) for _ in range(2)]
    rb = [P.sbuf([128, D], F32) for _ in range(2)]
    ob = [P.sbuf([128, D], F32) for _ in range(2)]
    stat = {"st6": P.sbuf([128, 12], F32), "mv": P.sbuf([128, 2], F32), "rstd": P.sbuf([128, 1], F32)}
    for gb in range(TOK // 128):
        par = gb % 2
        t0 = gb * 128
        P.dma("sp", hres[par][:, :], htm_d[t0:t0 + 128, :], writes=[f"hres{par}"])
        for n in range(2):
            bk = (gb % 2) * 2 + n
            for k in range(KC):
                P.op("pe", lambda e, bk=bk, k=k, n=n, t0=t0: e.matmul(
                    B.t[bk][:, :], lhsT=yTs[:, k, t0:t0 + 128], rhs=w[:, k, n * 512:(n + 1) * 512],
                    start=(k == 0), stop=(k == KC - 1)), reads=[f"yT{gb // 4}", f"w{k}"], writes=[B.key[bk]])
            P.op("dve", lambda e, bk=bk, n=n, par=par: e.scalar_tensor_tensor(
                out=rb[par][:, n * 512:(n + 1) * 512], in0=hres[par][:, n * 512:(n + 1) * 512], scalar=ALPHA,
                in1=B.t[bk][:, :], op0=ALU.mult, op1=ALU.add),
                reads=[f"hres{par}", B.key[bk]], writes=[f"b{par}r"])
        layer_norm_block(P, f"b{par}", rb[par], lng, lnb, ob[par], stat)
        P.dma("sp", ho_d[t0:t0 + 128, :], ob[par][:, :], reads=[f"b{par}lnout"])
    P.finalize()
    return nc, P


def projln_inputs(yT_full, h, w, g, b):
    K = yT_full.shape[1]
    common = {"w": _wr(w), "lng": _bc(g), "lnb": _bc(b)}
    maps = []
    for c in range(NCORES):
        bi, s = divmod(c, 4)
        t0 = s * TOK
        m = dict(common)
        m["yT"] = np.ascontiguousarray(yT_full[bi, :, t0:t0 + TOK].reshape(K // 128, 128, TOK).transpose(1, 0, 2))
        m["htm"] = np.ascontiguousarray(h[bi, t0:t0 + TOK])
        maps.append(m)
    return maps


RMS_EPS = 1e-5


MAMBA_IN = {"wx": [128, 8, 768], "wz": [128, 8, 512], "wdt": [128, 8, 8], "cwm": [128, 24], "cbm": [128, 6],
            "dtb": [128, 8], "alog": [128, 8], "dsk": [128, 8], "ng": [128, 512]}


def build_mamba(dbg_ntt=None):
    nc = new_nc(); P = Prog(nc)
    io = {k: din(nc, k, v) for k, v in MAMBA_IN.items()}
    hT_d = din(nc, "hT", [128, 8, SEQ])
    io["hT_tile"] = lambda tt: hT_d[:, :, tt * 512:(tt + 1) * 512]
    io["u"] = dout(nc, "u", [SEQ, 512])
    B = Banks(P)
    emit_mamba(P, B, io, dbg_ntt)
    P.finalize()
    return nc, P


MAMBA_DB = "xs_tm,Btm,dtt,dt,av,nacs,Ev,cdbs,acsFs,cbT,LTs,MT,xc,xcd,ybuf,xsk,sz,ug,junk,ssq"


def emit_mamba(P, B, io, dbg_ntt=None):
    NTT = dbg_ntt or (SEQ // 512)
    DB = set(MAMBA_DB.split(",")) if MAMBA_DB is not None else set()
    wx_d = io["wx"]; wz_d = io["wz"]; wdt_d = io["wdt"]; cwm_d = io["cwm"]; cbm_d = io["cbm"]
    dtb_d = io["dtb"]; alog_d = io["alog"]; dsk_d = io["dsk"]; ng_d = io["ng"]
    identF = make_ident(P, F32)
    identb = P.sbuf([128, 128], BF16, "identb")
    P.op("pool", lambda e: e.tensor_copy(out=identb[:], in_=identF[:]), reads=["ident"], writes=["identb"])
    maskT = P.sbuf([128, 128], BF16, "maskT"); tmpm = P.sbuf([128, 128], F32, "tmpm")
    P.op("pool", lambda e: e.memset(tmpm[:], 0.0), writes=["tmpm"])
    P.op("pool", lambda e: e.affine_select(out=tmpm[:], in_=tmpm[:], pattern=[[1, 128]], compare_op=ALU.is_ge,
                                           fill=NEG, base=0, channel_multiplier=-1), reads=["tmpm"], writes=["tmpm"])
    P.op("pool", lambda e: e.tensor_copy(out=maskT[:], in_=tmpm[:]), reads=["tmpm"], writes=["maskT"])
    Tm = P.sbuf([128, 128], F32, "Tm")
    P.op("pool", lambda e: e.memset(Tm[:], 1.0), writes=["Tm"])
    P.op("pool", lambda e: e.affine_select(out=Tm[:], in_=Tm[:], pattern=[[1, 128]], compare_op=ALU.is_ge,
                                           fill=0.0, base=0, channel_multiplier=-1), reads=["Tm"], writes=["Tm"])
    sel = P.sbuf([8, 8, 128], F32, "sel")
    P.op("pool", lambda e: e.memset(sel[:], 0.0), writes=["sel"])
    P.op("pool", lambda e: e.affine_select(out=sel[:], in_=sel[:], pattern=[[-1, 8], [0, 128]],
                                           compare_op=ALU.not_equal, fill=1.0, base=0, channel_multiplier=1),
         reads=["sel"], writes=["sel"])
    sel127 = P.sbuf([128, 128], F32, "sel127")
    P.op("pool", lambda e: e.memset(sel127[:], 0.0), writes=["sel127"])
    P.op("pool", lambda e: e.affine_select(out=sel127[:], in_=sel127[:], pattern=[[0, 128]],
                                           compare_op=ALU.not_equal, fill=1.0, base=-127, channel_multiplier=1),
         reads=["sel127"], writes=["sel127"])
    wx = P.sbuf([128, 8, 768], BF16); wz = P.sbuf([128, 8, 512], BF16); wdt = P.sbuf([128, 8, 8], BF16)
    P.dma("pq", wx[:, :, :], wx_d[:, :, :], writes=["w"]); P.dma("pq", wz[:, :, :], wz_d[:, :, :], writes=["w"])
    P.dma("pq", wdt[:, :, :], wdt_d[:, :, :], writes=["w"])
    cwm = P.sbuf([128, 24], F32); cbm = P.sbuf([128, 6], F32)
    dtb = P.sbuf([128, 8], F32); Abc = P.sbuf([128, 8], F32); dsk = P.sbuf([128, 8], F32); ng = P.sbuf([128, 512], F32)
    for (t, d_) in ((cwm, cwm_d), (cbm, cbm_d), (dtb, dtb_d), (Abc, alog_d), (dsk, dsk_d), (ng, ng_d)):
        P.dma("sp", t[:], d_[:, :], writes=["c"])
    P.op("act", lambda e: e.activation(out=Abc[:], in_=Abc[:], func=AF.Exp), reads=["c"], writes=["A"])
    P.op("dve", lambda e: e.tensor_scalar_mul(out=Abc[:], in0=Abc[:], scalar1=-1.0), reads=["A"], writes=["A"])
    hs = [P.sbuf([128, 8, 512], BF16) for _ in range(2)]
    ub = [[P.sbuf([128, 515], F32) for _ in range(2)] for _ in range(6)]
    for ci in range(6):
        P.op("pool", lambda e, ci=ci: e.memset(ub[ci][1][:, :], 0.0), writes=[f"ub{ci}1"])
    cacc2 = [P.sbuf([128, 512], F32) for _ in range(2)]
    xf = [P.sbuf([128, 512], F32) for _ in range(5)]
    BT = P.sbuf([128, 512], BF16); CT = [P.sbuf([128, 512], BF16) for _ in range(2)]
    xs_tm = [P.sbuf([128, 512], F32) for _ in range(2)]; Btm = [P.sbuf([128, 128], BF16) for _ in range(2)]
    dtt = [P.sbuf([128, 8], F32) for _ in range(2)]; dt = [P.sbuf([128, 8], F32) for _ in range(2)]; av = [P.sbuf([128, 8], F32) for _ in range(2)]
    nacs = [P.sbuf([128, 8], F32) for _ in range(2)]; Ev = [P.sbuf([128, 8], F32) for _ in range(2)]; cdbs = [P.sbuf([128, 8], F32) for _ in range(2)]
    acsFs = [P.sbuf([8, 128], F32) for _ in range(2)]; cbT = [P.sbuf([128, 128], F32) for _ in range(2)]
    LTs = [P.sbuf([128, 8, 128], F32) for _ in range(2)]; MT = [P.sbuf([128, 8, 128], BF16) for _ in range(2)]
    xc = [P.sbuf([128, 512], BF16) for _ in range(2)]; xcd = [P.sbuf([128, 512], BF16) for _ in range(2)]
    ybuf = [P.sbuf([128, 512], F32) for _ in range(2)]; xsk = [P.sbuf([128, 512], F32) for _ in range(2)]
    prev = P.sbuf([128, 512], F32); prevb = P.sbuf([128, 512], BF16)
    P.op("pool", lambda e: e.memset(prev[:], 0.0), writes=["prev"])
    P.op("pool", lambda e: e.memset(prevb[:], 0.0), writes=["prevb"])
    sz = [P.sbuf([128, 512], F32) for _ in range(2)]; ug = [P.sbuf([128, 512], F32) for _ in range(2)]; junk = [P.sbuf([128, 512], F32) for _ in range(2)]
    ssq = [P.sbuf([128, 1], F32) for _ in range(2)]
    uo = [P.sbuf([128, 512], F32) for _ in range(2)]
    uTb = [P.sbuf([128, 4, 128], BF16) for _ in range(2)]

    def bc8(t):
        return t[:, 0:8].unsqueeze(2).to_broadcast([128, 8, 64])

    def v3(t):
        return t[:, :].rearrange("p (h d) -> p h d", h=8)

    def early(tt, bi, hp, c0):
        cp = (tt * 4 + bi) % 2
        tp = tt % 2

        def kp(n):
            return cp if n in DB else 0
        blk = slice(bi * 128, (bi + 1) * 128)
        t0 = c0 + bi * 128
        for k in range(8):
            P.op("pe", lambda e, k=k, blk=blk, hp=hp: e.matmul(
                B.t[2][:, :], lhsT=hs[hp][:, k, blk], rhs=wz[:, k, :], start=(k == 0), stop=(k == 7)),
                reads=["w", f"hs{hp}"], writes=[B.key[2]])
        for k in range(8):
            P.op("pe", lambda e, k=k, blk=blk, hp=hp: e.matmul(
                B.t[0][:, 0:8], lhsT=hs[hp][:, k, blk], rhs=wdt[:, k, :], start=(k == 0), stop=(k == 7)),
                reads=["w", f"hs{hp}"], writes=[B.key[0]])
        for ci in range(4):
            P.op("pe", lambda e, ci=ci, blk=blk: e.transpose(
                out=B.t[3][:, ci * 128:(ci + 1) * 128], in_=xf[ci][:, blk], identity=identF[:]),
                reads=[f"xf{ci}", "ident"], writes=[B.key[3]])
        P.op("pe", lambda e, blk=blk: e.transpose(out=B.t[1][:, 0:128], in_=xf[4][:, blk], identity=identF[:]),
             reads=["xf4", "ident"], writes=[B.key[1]])
        P.op("act", lambda e: e.copy(out=xs_tm[kp("xs_tm")][:, :], in_=B.t[3][:, :]), reads=[B.key[3]], writes=[f"xs_tm{kp('xs_tm')}"])
        P.op("dve", lambda e: e.tensor_copy(out=Btm[kp("Btm")][:, :], in_=B.t[1][:, 0:128]), reads=[B.key[1]], writes=[f"Btm{kp('Btm')}"])
        P.op("dve", lambda e: e.tensor_tensor(out=dtt[kp("dtt")][:, :], in0=B.t[0][:, 0:8], in1=dtb[:, :], op=ALU.add),
             reads=[B.key[0], "c"], writes=[f"dtt{kp('dtt')}"])
        P.op("act", lambda e: e.activation(out=dtt[kp("dtt")][:, :], in_=dtt[kp("dtt")][:, :], func=AF.Exp),
             reads=[f"dtt{kp('dtt')}"], writes=[f"dtt{kp('dtt')}"])
        P.op("act", lambda e: e.activation(out=dt[kp("dt")][:, :], in_=dtt[kp("dtt")][:, :], func=AF.Ln, bias=1.0),
             reads=[f"dtt{kp('dtt')}"], writes=[f"dt{kp('dt')}"])
        P.op("dve", lambda e: e.tensor_tensor(out=av[kp("av")][:, :], in0=dt[kp("dt")][:, :], in1=Abc[:, :], op=ALU.mult),
             reads=[f"dt{kp('dt')}", "A"], writes=[f"av{kp('av')}"])
        P.op("pe", lambda e: e.matmul(B.t[0][:, 8:16], lhsT=Tm[:, :], rhs=av[kp("av")][:, :], start=True, stop=True),
             reads=["Tm", f"av{kp('av')}"], writes=[B.key[0]])
        P.op("pe", lambda e: e.matmul(B.t[0][0:8, 128:256], lhsT=av[kp("av")][:, :], rhs=Tm[:, :], start=True, stop=True),
             reads=["Tm", f"av{kp('av')}"], writes=[B.key[0]])
        P.op("dve", lambda e: e.tensor_scalar_mul(out=nacs[kp("nacs")][:, :], in0=B.t[0][:, 8:16], scalar1=-1.0),
             reads=[B.key[0]], writes=[f"nacs{kp('nacs')}"])
        P.op("act", lambda e: e.activation(out=Ev[kp("Ev")][:, :], in_=B.t[0][:, 8:16], func=AF.Exp),
             reads=[B.key[0]], writes=[f"Ev{kp('Ev')}"])
        P.op("act", lambda e: e.copy(out=acsFs[kp("acsFs")][:, :], in_=B.t[0][0:8, 128:256]), reads=[B.key[0]], writes=[f"acsFs{kp('acsFs')}"])
        P.op("pe", lambda e: e.matmul(B.t[0][:, 16:24], lhsT=sel127[:, :], rhs=Ev[kp("Ev")][:, :], start=True, stop=True),
             reads=["sel127", f"Ev{kp('Ev')}"], writes=[B.key[0]])
        P.op("dve", lambda e: e.tensor_copy(out=cdbs[kp("cdbs")][:, :], in_=B.t[0][:, 16:24]), reads=[B.key[0]], writes=[f"cdbs{kp('cdbs')}"])
        P.op("pe", lambda e, blk=blk: e.matmul(B.t[1][:, 128:256], lhsT=BT[:, blk], rhs=CT[tp][:, blk],
                                               start=True, stop=True), reads=["BT", f"CT{tp}"], writes=[B.key[1]])
        P.op("act", lambda e: e.copy(out=cbT[kp("cbT")][:, :], in_=B.t[1][:, 128:256]), reads=[B.key[1]], writes=[f"cbT{kp('cbT')}"])
        for h in range(8):
            bk = 4 + h // 4
            cs = slice((h % 4) * 128, (h % 4 + 1) * 128)
            P.op("pe", lambda e, bk=bk, cs=cs, h=h: e.matmul(B.t[bk][:, cs], lhsT=sel[0:8, h, :], rhs=acsFs[kp("acsFs")][:, :],
                                                             start=True, stop=False),
                 reads=["sel", f"acsFs{kp('acsFs')}"], writes=[B.key[bk]])
            P.op("pe", lambda e, bk=bk, cs=cs: e.matmul(B.t[bk][:, cs], lhsT=identb[:, :], rhs=maskT[:, :],
                                                        start=False, stop=True),
                 reads=["identb", "maskT"], writes=[B.key[bk]])
        for h in range(8):
            bk = 4 + h // 4
            cs = slice((h % 4) * 128, (h % 4 + 1) * 128)
            P.op("act", lambda e, bk=bk, cs=cs, h=h: e.activation(out=LTs[kp("LTs")][:, h, :], in_=B.t[bk][:, cs], func=AF.Exp,
                                                                  bias=nacs[kp("nacs")][:, h:h + 1]),
                 reads=[B.key[bk], f"nacs{kp('nacs')}"], writes=[f"LTs{kp('LTs')}"])
        P.op("dve", lambda e: e.tensor_tensor(out=MT[kp("MT")][:, :, :], in0=LTs[kp("LTs")][:, :, :],
                                              in1=cbT[kp("cbT")][:, :].unsqueeze(1).to_broadcast([128, 8, 128]), op=ALU.mult),
             reads=[f"LTs{kp('LTs')}", f"cbT{kp('cbT')}"], writes=[f"MT{kp('MT')}"])
        P.op("dve", lambda e: e.tensor_tensor(out=v3(xc[kp("xc")]), in0=v3(xs_tm[kp("xs_tm")]), in1=bc8(dt[kp("dt")]), op=ALU.mult),
             reads=[f"xs_tm{kp('xs_tm')}", f"dt{kp('dt')}"], writes=[f"xc{kp('xc')}"])
        P.op("dve", lambda e: e.tensor_tensor(out=v3(xcd[kp("xcd")]), in0=v3(xc[kp("xc")]),
                                              in1=LTs[kp("LTs")][:, :, 127:128].to_broadcast([128, 8, 64]), op=ALU.mult),
             reads=[f"xc{kp('xc')}", f"LTs{kp('LTs')}"], writes=[f"xcd{kp('xcd')}"])
        P.op("act", lambda e: e.activation(out=sz[kp("sz")][:, :], in_=B.t[2][:, :], func=AF.Silu),
             reads=[B.key[2]], writes=[f"sz{kp('sz')}"])
        P.op("dve", lambda e: e.tensor_tensor(out=v3(xsk[kp("xsk")]), in0=v3(xs_tm[kp("xs_tm")]), in1=bc8(dsk), op=ALU.mult),
             reads=[f"xs_tm{kp('xs_tm')}", "c"], writes=[f"xsk{kp('xsk')}"])

    def late(tt, bi, hp, c0):
        cp = (tt * 4 + bi) % 2
        tp = tt % 2

        def kp(n):
            return cp if n in DB else 0
        blk = slice(bi * 128, (bi + 1) * 128)
        t0 = c0 + bi * 128
        for h in range(8):
            P.op("pe", lambda e, h=h: e.matmul(B.t[6][:, h * 64:(h + 1) * 64], lhsT=MT[kp("MT")][:, h, :],
                                               rhs=xc[kp("xc")][:, h * 64:(h + 1) * 64], start=True, stop=True),
                 reads=[f"MT{kp('MT')}", f"xc{kp('xc')}"], writes=[B.key[6]])
        P.op("pe", lambda e, blk=blk: e.matmul(B.t[7][:, :], lhsT=CT[tp][:, blk], rhs=prevb[:, :], start=True, stop=True),
             reads=[f"CT{tp}", "prevb"], writes=[B.key[7]])
        P.op("dve", lambda e: e.tensor_tensor(out=v3(ybuf[kp("ybuf")]), in0=B.t[7][:, :].rearrange("p (h d) -> p h d", h=8),
                                              in1=bc8(Ev[kp("Ev")]), op=ALU.mult), reads=[B.key[7], f"Ev{kp('Ev')}"], writes=[f"ybuf{kp('ybuf')}"])
        P.op("pe", lambda e: e.matmul(B.t[7][:, :], lhsT=Btm[kp("Btm")][:, :], rhs=xcd[kp("xcd")][:, :], start=True, stop=True),
             reads=[f"Btm{kp('Btm')}", f"xcd{kp('xcd')}"], writes=[B.key[7]])
        P.op("dve", lambda e: e.tensor_tensor(out=ybuf[kp("ybuf")][:, :], in0=ybuf[kp("ybuf")][:, :], in1=B.t[6][:, :], op=ALU.add),
             reads=[B.key[6], f"ybuf{kp('ybuf')}"], writes=[f"ybuf{kp('ybuf')}"])
        P.op("dve", lambda e: e.tensor_tensor(out=ybuf[kp("ybuf")][:, :], in0=ybuf[kp("ybuf")][:, :], in1=xsk[kp("xsk")][:, :], op=ALU.add),
             reads=[f"ybuf{kp('ybuf')}", f"xsk{kp('xsk')}"], writes=[f"ybuf{kp('ybuf')}"])
        P.op("dve", lambda e: e.tensor_tensor(out=v3(prev), in0=v3(prev), in1=bc8(cdbs[kp("cdbs")]), op=ALU.mult),
             reads=["prev", f"cdbs{kp('cdbs')}"], writes=["prev"])
        P.op("dve", lambda e: e.tensor_tensor(out=prevb[:, :], in0=prev[:, :], in1=B.t[7][:, :], op=ALU.add),
             reads=["prev", B.key[7]], writes=["prevb"])
        P.op("dve", lambda e: e.tensor_tensor(out=prev[:, :], in0=prev[:, :], in1=B.t[7][:, :], op=ALU.add),
             reads=["prev", B.key[7]], writes=["prev"])
        P.op("dve", lambda e: e.tensor_tensor(out=ug[kp("ug")][:, :], in0=ybuf[kp("ybuf")][:, :], in1=sz[kp("sz")][:, :], op=ALU.mult),
             reads=[f"ybuf{kp('ybuf')}", f"sz{kp('sz')}"], writes=[f"ug{kp('ug')}"])
        P.op("act", lambda e: e.activation(out=junk[kp("junk")][:, :], in_=ug[kp("ug")][:, :], func=AF.Square, accum_out=ssq[kp("ssq")][:, 0:1]),
             reads=[f"ug{kp('ug')}"], writes=[f"ssq{kp('ssq')}", f"junk{kp('junk')}"])
        P.op("dve", lambda e: e.tensor_scalar(out=ssq[kp("ssq")][:, 0:1], in0=ssq[kp("ssq")][:, 0:1], scalar1=1.0 / 512, scalar2=RMS_EPS,
                                              op0=ALU.mult, op1=ALU.add), reads=[f"ssq{kp('ssq')}"], writes=[f"ssq{kp('ssq')}"])
        P.op("act", lambda e: e.activation(out=ssq[kp("ssq")][:, 0:1], in_=ssq[kp("ssq")][:, 0:1], func=AF.Ln),
             reads=[f"ssq{kp('ssq')}"], writes=[f"ssq{kp('ssq')}"])
        P.op("act", lambda e: e.activation(out=ssq[kp("ssq")][:, 0:1], in_=ssq[kp("ssq")][:, 0:1], func=AF.Exp, scale=-0.5),
             reads=[f"ssq{kp('ssq')}"], writes=[f"ssq{kp('ssq')}"])
        up = (tt * 4 + bi) % 2
        P.op("dve", lambda e, up=up: e.scalar_tensor_tensor(out=uo[up][:, :], in0=ug[kp("ug")][:, :], scalar=ssq[kp("ssq")][:, 0:1],
                                                            in1=ng[:, :], op0=ALU.mult, op1=ALU.mult),
             reads=[f"ug{kp('ug')}", f"ssq{kp('ssq')}", "c"], writes=[f"uo{up}"])
        if "u" in io:
            P.dma("sp", io["u"][t0:t0 + 128, :], uo[up][:, :], reads=[f"uo{up}"])
        else:
            for ci in range(4):
                P.op("pe", lambda e, ci=ci, up=up: e.transpose(
                    out=B.t[6][:, ci * 128:(ci + 1) * 128], in_=uo[up][:, ci * 128:(ci + 1) * 128],
                    identity=identF[:]), reads=[f"uo{up}", "ident"], writes=[B.key[6]])
            P.op("act", lambda e, up=up: e.copy(out=uTb[up][:, :, :],
                                                in_=B.t[6][:, :].rearrange("p (c t) -> p c t", c=4)),
                 reads=[B.key[6]], writes=[f"uTb{up}"])
            uk = io.setdefault("ukeys", [])
            for cpi in range(2):
                uk.append(f"usrc_{len(uk)}")
                P.dma("sp", io["usrc"](cpi, t0), uTb[up][:, 2 * cpi:2 * cpi + 2, :], reads=[f"uTb{up}"],
                      writes=[uk[-1]])
            io["on_block_done"](tt * 4 + bi)

    def feature(tt):
        hp = tt % 2
        c0 = tt * 512
        if tt == 0:
            P.dma("pq", hs[0][:, :, :], io["hT_tile"](0), writes=["hs0"])
        if tt + 1 < NTT:
            P.dma("pq", hs[1 - hp][:, :, :], io["hT_tile"](tt + 1), writes=[f"hs{1 - hp}"])
        for ci in range(6):
            bk = ci % 2
            for k in range(8):
                P.op("pe", lambda e, bk=bk, k=k, ci=ci, hp=hp: e.matmul(
                    B.t[bk][:, :], lhsT=wx[:, k, ci * 128:(ci + 1) * 128], rhs=hs[hp][:, k, :],
                    start=(k == 0), stop=(k == 7)), reads=["w", f"hs{hp}"], writes=[B.key[bk]])
            u = ub[ci][hp]; uprev = ub[ci][1 - hp]
            P.op("act", lambda e, u=u, bk=bk: e.copy(out=u[:, 3:515], in_=B.t[bk][:, :]),
                 reads=[B.key[bk]], writes=[f"ub{ci}{hp}"])
            P.op("pool", lambda e, u=u, uprev=uprev: e.tensor_copy(out=u[:, 0:3], in_=uprev[:, 512:515]),
                 reads=[f"ub{ci}{1 - hp}"], writes=[f"ub{ci}{hp}h"])
            P.op("act", lambda e, u=u, ci=ci, cacc=cacc2[ci % 2]: e.activation(out=cacc[:, :], in_=u[:, 3:515], func=AF.Identity,
                                                           scale=cwm[:, ci * 4 + 3:ci * 4 + 4], bias=cbm[:, ci:ci + 1]),
                 reads=[f"ub{ci}{hp}", "c"], writes=[f"cacc{ci % 2}"])
            for tap in range(3):
                P.op("dve", lambda e, u=u, ci=ci, tap=tap, cacc=cacc2[ci % 2]: e.scalar_tensor_tensor(
                    out=cacc[:, :], in0=u[:, tap:tap + 512], scalar=cwm[:, ci * 4 + tap:ci * 4 + tap + 1],
                    in1=cacc[:, :], op0=ALU.mult, op1=ALU.add),
                    reads=[f"ub{ci}{hp}", f"ub{ci}{hp}h", "c", f"cacc{ci % 2}"], writes=[f"cacc{ci % 2}"])
            if ci < 5:
                P.op("act", lambda e, ci=ci, cacc=cacc2[ci % 2]: e.activation(out=xf[ci][:, :], in_=cacc[:, :], func=AF.Silu),
                     reads=[f"cacc{ci % 2}"], writes=[f"xf{ci}"])
                if ci == 4:
                    P.op("dve", lambda e: e.tensor_copy(out=BT[:, :], in_=xf[4][:, :]), reads=["xf4"], writes=["BT"])
            else:
                P.op("act", lambda e, hp=hp, cacc=cacc2[ci % 2]: e.activation(out=CT[hp][:, :], in_=cacc[:, :], func=AF.Silu),
                     reads=[f"cacc{ci % 2}"], writes=[f"CT{hp}"])

    chunks = [(tt, bi) for tt in range(NTT) for bi in range(4)]
    for idx in range(len(chunks) + 1):
        def do_early(idx=idx):
            tt, bi = chunks[idx]
            if bi == 0:
                feature(tt)
            early(tt, bi, tt % 2, tt * 512)

        def do_late(idx=idx):
            tt, bi = chunks[idx - 1]
            late(tt, bi, tt % 2, tt * 512)
        ea = P.capture(do_early) if idx < len(chunks) else []
        la = P.capture(do_late) if idx >= 1 else []
        P.interleave(ea, la)


def mamba_inputs(h, w_in, conv_w, conv_b, dt_bias, a_log, d_skip, norm_g):
    maps = []
    for c in range(NCORES):
        bi, g = divmod(c, 4)
        xcols = np.concatenate([np.arange(2048 + g * 512, 2048 + (g + 1) * 512),
                                np.arange(4096 + g * 128, 4096 + (g + 1) * 128),
                                np.arange(4608 + g * 128, 4608 + (g + 1) * 128)])
        ch = xcols - 2048
        cwm = conv_w[:, ch].reshape(4, 6, 128).transpose(2, 1, 0).reshape(128, 24)
        cbm = conv_b[ch].reshape(6, 128).T
        hs = slice(g * 8, (g + 1) * 8)
        m = {"wx": _wr(np.ascontiguousarray(w_in[:, xcols])),
             "wz": _wr(np.ascontiguousarray(w_in[:, g * 512:(g + 1) * 512])),
             "wdt": _wr(np.ascontiguousarray(w_in[:, 5120 + g * 8:5120 + (g + 1) * 8])),
             "cwm": np.ascontiguousarray(cwm), "cbm": np.ascontiguousarray(cbm),
             "dtb": _bc(dt_bias[hs]), "alog": _bc(a_log[hs]), "dsk": _bc(d_skip[hs]),
             "ng": _bc(norm_g[g * 512:(g + 1) * 512])}
        if h is not None:
            m["hT"] = _fm(np.ascontiguousarray(h[bi]))
        maps.append(m)
    return maps


PADC = 128
GROUPS = [[0, 1, 2, 3], [4, 5, 6, 7]]


def emit_projln_f(P, B, io, KC, load_yT, halo_res, main_res):
    ident = make_ident(P, F32)
    lng = P.sbuf([128, D], F32); lnb = P.sbuf([128, D], F32); flag = P.sbuf([128, 1], F32)
    P.dma("sp", lng[:], io["lng"][:, :], writes=["lnconst"]); P.dma("sp", lnb[:], io["lnb"][:, :], writes=["lnconst"])
    P.dma("sp", flag[:], io["flag"][:, :], writes=["lnconst"])
    w = P.sbuf([128, KC, D], BF16)
    for k in range(KC):
        P.dma("pq", w[:, k, :], io["w"][:, k, :], writes=["w"])
    yTs = P.sbuf([128, KC, TOK + 2], BF16)
    load_yT(P, yTs)
    NW = 4
    hres = [P.sbuf([128, D], F32) for _ in range(NW)]
    rb = [P.sbuf([128, D], F32) for _ in range(NW)]
    ob = rb
    hTb = [P.sbuf([128, 8, 128], BF16) for _ in range(NW)]
    stat = [{"st6": P.sbuf([128, 12], F32), "mv": P.sbuf([128, 2], F32), "rstd": P.sbuf([128, 1], F32)}
            for _ in range(NW)]
    blocks = [(0, 2)] + [(2 + i * 128, 128) for i in range(TOK // 128)]
    def do_block(bi_):
        c0, nt = blocks[bi_]
        par = bi_ % NW
        tb = 2 * par
        if bi_ == 0:
            P.op("sp", lambda e, par=par: e.dma_start(out=hres[par][0:2, :], in_=halo_res(e)), writes=[f"hres{par}"])
        else:
            P.dma("sp", hres[par][:, :], main_res[c0 - 2:c0 - 2 + 128, :], writes=[f"hres{par}"])
        for n in range(2):
            bk = 2 * par + n
            for k in range(KC):
                P.op("pe", lambda e, bk=bk, k=k, n=n, c0=c0, nt=nt: e.matmul(
                    B.t[bk][0:nt, :], lhsT=yTs[:, k, c0:c0 + nt], rhs=w[:, k, n * 512:(n + 1) * 512],
                    start=(k == 0), stop=(k == KC - 1)), reads=["yTs", "w"], writes=[B.key[bk]])
            P.op("dve", lambda e, bk=bk, n=n, par=par, nt=nt: e.scalar_tensor_tensor(
                out=rb[par][0:nt, n * 512:(n + 1) * 512], in0=hres[par][0:nt, n * 512:(n + 1) * 512], scalar=ALPHA,
                in1=B.t[bk][0:nt, :], op0=ALU.mult, op1=ALU.add),
                reads=[f"hres{par}", B.key[bk]], writes=[f"b{par}r"])
        layer_norm_block(P, f"b{par}", rb[par], lng, lnb, ob[par], stat[par], eng2="dve", n=nt)
        yk = f"b{par}r"
        if bi_ > 0:
            P.dma("sp", io["h_d"][c0 - 2:c0 - 2 + 128, :], ob[par][:, :], reads=[yk])
        for k in range(8):
            bk = tb + k // 4
            P.op("pe", lambda e, bk=bk, k=k, par=par, nt=nt: e.transpose(
                out=B.t[bk][:, (k % 4) * 128:(k % 4) * 128 + nt], in_=ob[par][0:nt, k * 128:(k + 1) * 128],
                identity=ident[0:nt, 0:nt]), reads=[yk, "ident"], writes=[B.key[bk]])
        for hh in range(2):
            P.op("act", lambda e, hh=hh, par=par, nt=nt, tb=tb: e.copy(
                out=hTb[par][:, hh * 4:(hh + 1) * 4, 0:nt],
                in_=B.t[tb + hh][:, :].rearrange("p (k t) -> p k t", k=4)[:, :, 0:nt]),
                reads=[B.key[tb + hh]], writes=[f"hTb{par}"])
        if bi_ == 0:
            P.op("dve", lambda e, par=par: e.tensor_scalar_mul(out=hTb[par][:, :, 0:2], in0=hTb[par][:, :, 0:2],
                                                               scalar1=flag[:, 0:1]),
                 reads=[f"hTb{par}", "lnconst"], writes=[f"hTb{par}"])
        P.dma("sp", io["hT_d"][:, :, c0:c0 + nt], hTb[par][:, :, 0:nt], reads=[f"hTb{par}"])

    for b0 in range(0, len(blocks), NW):
        P.interleave_n([P.capture(lambda i=i: do_block(i)) for i in range(b0, min(b0 + NW, len(blocks)))])


def build_fused(dbg=False):
    nc = new_nc(); P = Prog(nc); B = Banks(P)
    A = {k: din(nc, "a_" + k, v) for k, v in ATTN_IN.items()}
    xtm = din(nc, "xtm", [TOK + 2, D]); flag = din(nc, "flag", [128, 1])
    wo0 = din(nc, "wo0", [128, 8, D]); wo1 = din(nc, "wo1", [128, 16, D])
    lnm = [(din(nc, f"lnmg{l}", [128, D]), din(nc, f"lnmb{l}", [128, D])) for l in range(2)]
    F = [{k: din(nc, f"f{l}_" + k, v) for k, v in FFN_IN.items()} for l in range(2)]
    M = {k: din(nc, "m_" + k, v) for k, v in MAMBA_IN.items()}
    out_d = dout(nc, "out", [TOK, D])

    def scratch(name, shape, dt):
        return nc.dram_tensor(name, list(shape), dt).ap()
    ysrc = scratch("ysrc", [4 * 256, 2048], BF16)
    ygath = scratch("ygath", [4 * 1024, 2048], BF16)
    hsrc = scratch("hsrc", [4 * 128, 8 * 512], BF16)
    hgath = scratch("hgath", [4 * 512, 8 * 512], BF16)
    hh_src = scratch("hh_src", [2, D], F32); hh_g = scratch("hh_g", [8, D], F32)
    usrc = scratch("usrc", [8 * 256, 2048], BF16)
    ugath = scratch("ugath", [8 * 1024, 2048], BF16)
    h1_d = scratch("h1_d", [TOK, D], F32); hT1_d = scratch("hT1_d", [128, 8, TOK + 2], BF16)
    h2_d = scratch("h2_d", [TOK, D], F32)
    h3_d = scratch("h3_d", [TOK, D], F32); hT3_d = scratch("hT3_d", [128, 8, TOK + 2], BF16)

    def allgather(src, dst, rkeys, wk):
        P.op("cc", lambda e: e.collective_compute("AllGather", ALU.bypass, replica_groups=GROUPS,
                                                   ins=[src], outs=[dst]), reads=list(rkeys), writes=[wk])

    _dyn = {}

    def dyn(e, tag):
        if tag not in _dyn:
            rank = e.partition_id() % 4
            prev = (rank + 3) % 4
            _dyn[tag] = {"r1024": e.snap(rank * 1024), "p1024": e.snap(prev * 1024), "p2": e.snap(prev * 2)}
        return _dyn[tag]

    P.phase()
    ioA = dict(A)
    ioA["ya_ap"] = lambda c0: ysrc[(c0 // 2048) * 256:(c0 // 2048) * 256 + 128, c0 % 2048:c0 % 2048 + 512]
    ioA["yb_ap"] = lambda c0: ysrc[(c0 // 2048) * 256 + 128:(c0 // 2048) * 256 + 256, c0 % 2048:c0 % 2048 + 512]

    def a_hook(qg):
        if qg % 4 == 3:
            q = qg // 4
            allgather(ysrc[q * 256:(q + 1) * 256, :], ygath[q * 1024:(q + 1) * 1024, :], ioA["ykeys"], f"ygath{q}")
    ioA["on_qg_done"] = a_hook
    emit_attn(P, B, ioA)
    P.phase()

    def load_y(P_, yTs):
        dst = yTs[:, :, :].rearrange("p (h r) t -> p h r t", h=2)
        for half in range(2):
            P_.op("sp", lambda e, half=half: e.dma_start(
                out=dst[:, half, :, 2:TOK + 2],
                in_=ygath[bass.ds(dyn(e, "sp")["r1024"], 1024), :].rearrange(
                    "(r h p) t -> p h r t", r=4, h=2, p=128)[:, half, :, :]), writes=["yTs"])
            P_.op("sp", lambda e, half=half: e.dma_start(
                out=dst[:, half, :, 0:2],
                in_=ygath[bass.ds(dyn(e, "sp")["p1024"], 1024), :].rearrange(
                    "(r h p) t -> p h r t", r=4, h=2, p=128)[:, half, :, 2046:2048]), writes=["yTs"])
    emit_projln_f(P, B, {"w": wo0, "lng": lnm[0][0], "lnb": lnm[0][1], "flag": flag, "h_d": h1_d, "hT_d": hT1_d},
                  8, load_y, lambda e: xtm[0:2, :], xtm[2:TOK + 2, :])
    P.phase()
    ioC = dict(F[0]); ioC.update({"hT": hT1_d, "htm": h1_d, "ho": h2_d, "hh": hh_src})
    ioC["hsrc"] = lambda t0: hsrc[(t0 // 512) * 128:(t0 // 512 + 1) * 128, :].rearrange(
        "p (k t) -> p k t", k=8)[:, :, t0 % 512:t0 % 512 + 128]

    def c_hook(gb):
        if gb % 4 == 3:
            q = gb // 4
            allgather(hsrc[q * 128:(q + 1) * 128, :], hgath[q * 512:(q + 1) * 512, :], ioC["hkeys"], f"hgath{q}")
        if gb == TOK // 128 - 1:
            allgather(hh_src[:, :], hh_g[:, :], ioC["hkeys"], "hhg")
    ioC["on_block_done"] = c_hook
    emit_ffn(P, B, ioC)
    P.phase()
    ioD = dict(M)
    ioD["hT_tile"] = lambda tt: hgath[(tt % 4) * 512 + (tt // 4) * 128:(tt % 4) * 512 + (tt // 4 + 1) * 128, :].rearrange(
        "p (k t) -> p k t", k=8)
    ioD["usrc"] = lambda cp, t0: usrc[(cp * 4 + t0 // 2048) * 256:(cp * 4 + t0 // 2048 + 1) * 256, :].rearrange(
        "(c p) t -> p c t", p=128)[:, :, t0 % 2048:t0 % 2048 + 128]

    def d_hook(blk):
        if blk % 16 == 15:
            q = blk // 16
            for cp in range(2):
                i = cp * 4 + q
                allgather(usrc[i * 256:(i + 1) * 256, :], ugath[i * 1024:(i + 1) * 1024, :], ioD["ukeys"], f"ugath{i}")
    ioD["on_block_done"] = d_hook
    emit_mamba(P, B, ioD)
    P.phase()

    def load_u(P_, yTs):
        for cp in range(2):
            P_.op("pq", lambda e, cp=cp: e.dma_start(
                out=yTs[:, cp * 8:(cp + 1) * 8, 2:TOK + 2],
                in_=ugath[bass.ds(dyn(e, "pq")["r1024"] + cp * 4096, 1024), :].rearrange(
                    "(k p) t -> p k t", p=128)), writes=["yTs"])
            P_.op("pq", lambda e, cp=cp: e.dma_start(
                out=yTs[:, cp * 8:(cp + 1) * 8, 0:2],
                in_=ugath[bass.ds(dyn(e, "pq")["p1024"] + cp * 4096, 1024), :].rearrange(
                    "(k p) t -> p k t", p=128)[:, :, 2046:2048]), writes=["yTs"])
    emit_projln_f(P, B, {"w": wo1, "lng": lnm[1][0], "lnb": lnm[1][1], "flag": flag, "h_d": h3_d, "hT_d": hT3_d},
                  16, load_u, lambda e: hh_g[bass.ds(dyn(e, "sp")["p2"], 2), :], h2_d)
    P.phase()
    ioF = dict(F[1]); ioF.update({"hT": hT3_d, "htm": h3_d, "ho": out_d})
    emit_ffn(P, B, ioF)
    if dbg:
        P.barrier()
        for nm, src in (("dbg_h1", h1_d), ("dbg_h2", h2_d), ("dbg_h3", h3_d)):
            o = dout(nc, nm, [TOK, D])
            for r0 in range(0, TOK, 256):
                P.dma("sp", o[r0:r0 + 256, :], src[r0:r0 + 256, :])
    P.finalize()
    return nc, P


def fused_inputs(inp):
    f = lambda k: np.ascontiguousarray(np.asarray(inp[k], dtype=np.float32))
    x = f("x"); p = f("p")
    am = attn_inputs(x, f("even_w_in")[0], f("even_b_f")[0], f("even_conv_w")[0])
    mm = mamba_inputs(None, f("odd_w_in")[0], f("odd_conv_w")[0], f("odd_conv_b")[0], f("odd_dt_bias")[0],
                      f("odd_a_log")[0], f("odd_d_skip")[0], f("odd_norm_g")[0])
    fm = [ffn_inputs(None, p[l], f("ffn_w_up")[l], f("ffn_conv_w")[l], f("ffn_conv_b")[l], f("ffn_w_down")[l],
                     f("ln_ffn_g")[l], f("ln_ffn_b")[l], f("ple_w_proj")[l], f("ple_w_gate")[l], f("ple_b_gate")[l])
          for l in range(2)]
    wo1 = _wr(f("odd_w_out")[0])
    perm = [r * 4 + cp * 2 + c for cp in range(2) for r in range(4) for c in range(2)]
    common = {"wo0": _wr(f("even_w_out")[0]), "wo1": np.ascontiguousarray(wo1[:, perm, :])}
    for l in range(2):
        common[f"lnmg{l}"] = _bc(f("ln_mix_g")[l]); common[f"lnmb{l}"] = _bc(f("ln_mix_b")[l])
    maps = []
    for c in range(NCORES):
        b, s = divmod(c, 4)
        t0 = s * TOK
        m = dict(common)
        for k in ATTN_IN:
            m["a_" + k] = am[c][k]
        for k in MAMBA_IN:
            m["m_" + k] = mm[c][k]
        for l in range(2):
            for k in FFN_IN:
                m[f"f{l}_" + k] = fm[l][c][k]
        xpad = np.zeros((TOK + 2, D), np.float32)
        lo = max(t0 - 2, 0)
        xpad[2 - (t0 - lo):] = x[b, lo:t0 + TOK]
        m["xtm"] = xpad
        m["flag"] = np.full((128, 1), 0.0 if s == 0 else 1.0, np.float32)
        maps.append(m)
    return maps


def _run(nc, maps):
    res = run_bass_kernel_spmd(nc, maps, core_ids=list(range(NCORES)))
    return res.results


def _tok_gather(results, key):
    return np.stack([np.asarray(r[key]) for r in results]).reshape(2, SEQ, -1)


def kernel(**inp):
    nc, _ = build_fused()
    res = _run(nc, fused_inputs(inp))
    out = _tok_gather(res, "out")
    return np.ascontiguousarray(out.astype(np.float32))
```
